# Optimizing a Trainium2 kernel written in Bass

```python
import jax, jax.numpy as jnp
from jax import lax
import numpy as np

D_MODEL = 1024
BATCH = 1
SEQ = 16384
DEPTH = 1

CONV_WIDTH = D_MODEL
CONV_KERNEL = 31
N_RET_HEADS = 8
RET_QK_DIM = D_MODEL // N_RET_HEADS
RET_V_DIM = 2 * RET_QK_DIM
RET_QK_WIDTH = N_RET_HEADS * RET_QK_DIM
RET_V_WIDTH = N_RET_HEADS * RET_V_DIM
RET_CHUNK = 128
ROPE_BASE = 10000.0
D_FF = 2816
FFN_CONV_KERNEL = 3
LN_EPS = 1e-5
DEEPNORM_ALPHA = (2.0 * DEPTH) ** 0.25
DEEPNORM_BETA = (8.0 * DEPTH) ** -0.25
N_MOD = 6
IN_SIZES = (RET_QK_WIDTH, RET_QK_WIDTH, RET_V_WIDTH, RET_V_WIDTH,
            CONV_WIDTH, CONV_WIDTH, D_MODEL, D_MODEL)
IN_WIDTH = sum(IN_SIZES)
IN_SPLITS = tuple(int(s) for s in np.cumsum(IN_SIZES)[:-1])

kernel_name = "hybrid_conformer_retention_deepnorm_block"


def layer_norm(x, g=None, b=None):
    xf = x.astype(jnp.float32)
    mu = jnp.mean(xf, axis=-1, keepdims=True)
    var = jnp.mean(jnp.square(xf - mu), axis=-1, keepdims=True)
    y = ((xf - mu) * lax.rsqrt(var + LN_EPS)).astype(x.dtype)
    if g is not None:
        y = y * g + b
    return y


def causal_depthwise_conv(x, w, b):
    k_width, ch = w.shape
    y = lax.conv_general_dilated(
        x, w[:, None, :].astype(x.dtype), window_strides=(1,),
        padding=[(k_width - 1, 0)], dimension_numbers=("NWC", "WIO", "NWC"),
        feature_group_count=ch)
    return y + b


def rotary(x, positions):
    half = x.shape[-1] // 2
    inv_freq = ROPE_BASE ** (-jnp.arange(half, dtype=jnp.float32) / half)
    ang = positions.astype(jnp.float32)[..., None] * inv_freq
    cos = jnp.cos(ang)[:, :, None, :]
    sin = jnp.sin(ang)[:, :, None, :]
    x1, x2 = x[..., :half], x[..., half:]
    return jnp.concatenate([x1 * cos - x2 * sin, x2 * cos + x1 * sin], axis=-1)


def chunkwise_retention(q, k, v):
    bsz, seq, heads, dk = q.shape
    dv = v.shape[-1]
    n_chunks = seq // RET_CHUNK
    log_gamma = jnp.log(1.0 - 2.0 ** (-5.0 - jnp.arange(heads, dtype=jnp.float32)))
    q = q.reshape(bsz, n_chunks, RET_CHUNK, heads, dk)
    k = k.reshape(bsz, n_chunks, RET_CHUNK, heads, dk)
    v = v.reshape(bsz, n_chunks, RET_CHUNK, heads, dv)
    idx = jnp.arange(RET_CHUNK, dtype=jnp.float32)
    rel = idx[:, None] - idx[None, :]
    decay = jnp.where(rel[None] >= 0,
                      jnp.exp(log_gamma[:, None, None] * jnp.maximum(rel, 0.0)[None]),
                      0.0)
    scores = jnp.einsum("bnihd,bnjhd->bnhij", q, k) * decay
    inner = jnp.einsum("bnhij,bnjhe->bnihe", scores, v)
    zeta = jnp.exp(log_gamma[:, None] * (RET_CHUNK - 1.0 - idx)[None])
    kv = jnp.einsum("bnjhd,bnjhe->bnhde", k * zeta.T[None, None, :, :, None], v)
    chunk_decay = jnp.exp(log_gamma * RET_CHUNK)[None, :, None, None]

    def step(state, kv_c):
        return state * chunk_decay + kv_c, state

    _, r_prev = lax.scan(step, jnp.zeros((bsz, heads, dk, dv), jnp.float32),
                         jnp.moveaxis(kv, 1, 0))
    r_prev = jnp.moveaxis(r_prev, 0, 1)
    xi = jnp.exp(log_gamma[:, None] * (idx + 1.0)[None])
    cross = jnp.einsum("bnihd,bnhde->bnihe", q * xi.T[None, None, :, :, None], r_prev)
    return (inner + cross).reshape(bsz, seq, heads, dv)


def token_mixer(h, positions, w_in, b_in, conv_dw_w, conv_dw_b, conv_ln_g, conv_ln_b,
                w_conv_out, ret_gn_g, ret_gn_b, w_ret_out, w_out):
    bsz, seq, _ = h.shape
    proj = jnp.einsum("bsd,de->bse", h, w_in) + b_in
    q, k, v, g_ret, c_val, c_gate, gate_a, gate_b = jnp.split(proj, IN_SPLITS, axis=-1)

    a = c_val * jax.nn.sigmoid(c_gate)
    a = causal_depthwise_conv(a, conv_dw_w, conv_dw_b)
    a = jax.nn.silu(layer_norm(a, conv_ln_g, conv_ln_b))
    y_a = jnp.einsum("bsc,cd->bsd", a, w_conv_out)

    qf = rotary(q.reshape(bsz, seq, N_RET_HEADS, RET_QK_DIM).astype(jnp.float32), positions)
    kf = rotary(k.reshape(bsz, seq, N_RET_HEADS, RET_QK_DIM).astype(jnp.float32), positions)
    kf = kf * (RET_QK_DIM ** -0.5)
    vf = v.reshape(bsz, seq, N_RET_HEADS, RET_V_DIM).astype(jnp.float32)
    r = chunkwise_retention(qf, kf, vf)
    mu = jnp.mean(r, axis=-1, keepdims=True)
    var = jnp.mean(jnp.square(r - mu), axis=-1, keepdims=True)
    r = ((r - mu) * lax.rsqrt(var + LN_EPS)).reshape(bsz, seq, RET_V_WIDTH).astype(h.dtype)
    r = (r * ret_gn_g + ret_gn_b) * jax.nn.silu(g_ret)
    y_b = jnp.einsum("bse,ed->bsd", r, w_ret_out)

    m = jax.nn.sigmoid(gate_a) * y_a + jax.nn.sigmoid(gate_b) * y_b
    return jnp.einsum("bsd,de->bse", m, w_out)


def channel_mixer(h, w_up, ffn_dw_w, ffn_dw_b, w_down):
    u = jnp.einsum("bsd,df->bsf", h, w_up)
    u = causal_depthwise_conv(u, ffn_dw_w, ffn_dw_b)
    val, gate = jnp.split(u, 2, axis=-1)
    return jnp.einsum("bsf,fd->bsd", val * jax.nn.silu(gate), w_down)


def setup_inputs(seed: int = 0) -> dict:
    key = jax.random.key(seed)
    ks = jax.random.split(key, 32)
    f32 = jnp.float32

    def nrm(k, shape, scale):
        return jax.random.normal(k, shape, f32) * scale

    L, D = DEPTH, D_MODEL
    x = jax.random.normal(ks[0], (BATCH, SEQ, D), f32)
    c = jax.random.normal(ks[1], (BATCH, D), f32)
    offset = jax.random.randint(ks[2], (BATCH, 1), 0, 1024, dtype=jnp.int32)
    positions = offset + jnp.arange(SEQ, dtype=jnp.int32)[None, :]
    return {
        "x": x,
        "c": c,
        "positions": positions,
        "w_ada": nrm(ks[3], (L, D, N_MOD * D), 0.5 * D ** -0.5),
        "b_ada": nrm(ks[4], (L, N_MOD * D), 0.02),
        "w_in": nrm(ks[5], (L, D, IN_WIDTH), D ** -0.5),
        "b_in": nrm(ks[6], (L, IN_WIDTH), 0.02),
        "conv_dw_w": nrm(ks[7], (L, CONV_KERNEL, CONV_WIDTH), CONV_KERNEL ** -0.5),
        "conv_dw_b": nrm(ks[8], (L, CONV_WIDTH), 0.02),
        "conv_ln_g": 1.0 + nrm(ks[9], (L, CONV_WIDTH), 0.02),
        "conv_ln_b": nrm(ks[10], (L, CONV_WIDTH), 0.02),
        "w_conv_out": nrm(ks[11], (L, CONV_WIDTH, D), CONV_WIDTH ** -0.5),
        "ret_gn_g": 1.0 + nrm(ks[12], (L, RET_V_WIDTH), 0.02),
        "ret_gn_b": nrm(ks[13], (L, RET_V_WIDTH), 0.02),
        "w_ret_out": nrm(ks[14], (L, RET_V_WIDTH, D), RET_V_WIDTH ** -0.5),
        "w_out": nrm(ks[15], (L, D, D), DEEPNORM_BETA * D ** -0.5),
        "ln1_g": 1.0 + nrm(ks[16], (L, D), 0.02),
        "ln1_b": nrm(ks[17], (L, D), 0.02),
        "w_up": nrm(ks[18], (L, D, 2 * D_FF), D ** -0.5),
        "ffn_dw_w": nrm(ks[19], (L, FFN_CONV_KERNEL, 2 * D_FF), FFN_CONV_KERNEL ** -0.5),
        "ffn_dw_b": nrm(ks[20], (L, 2 * D_FF), 0.02),
        "w_down": nrm(ks[21], (L, D_FF, D), DEEPNORM_BETA * D_FF ** -0.5),
        "ln2_g": 1.0 + nrm(ks[22], (L, D), 0.02),
        "ln2_b": nrm(ks[23], (L, D), 0.02),
    }


def reference(x, c, positions, w_ada, b_ada, w_in, b_in, conv_dw_w, conv_dw_b, conv_ln_g,
              conv_ln_b, w_conv_out, ret_gn_g, ret_gn_b, w_ret_out, w_out, ln1_g, ln1_b,
              w_up, ffn_dw_w, ffn_dw_b, w_down, ln2_g, ln2_b):
    for l in range(DEPTH):
        mod = jnp.einsum("bd,de->be", jax.nn.silu(c), w_ada[l]) + b_ada[l]
        shift1, scale1, gate1, shift2, scale2, gate2 = jnp.split(mod[:, None, :], N_MOD, axis=-1)

        h = layer_norm(x) * (1.0 + scale1) + shift1
        t = token_mixer(h, positions, w_in[l], b_in[l], conv_dw_w[l], conv_dw_b[l],
                        conv_ln_g[l], conv_ln_b[l], w_conv_out[l], ret_gn_g[l], ret_gn_b[l],
                        w_ret_out[l], w_out[l])
        x = layer_norm(DEEPNORM_ALPHA * x + gate1 * t, ln1_g[l], ln1_b[l])

        h = layer_norm(x) * (1.0 + scale2) + shift2
        f = channel_mixer(h, w_up[l], ffn_dw_w[l], ffn_dw_b[l], w_down[l])
        x = layer_norm(DEEPNORM_ALPHA * x + gate2 * f, ln2_g[l], ln2_b[l])
    return x
```

```python
import math
from contextlib import ExitStack
import numpy as np
import concourse.bass as bass
import concourse.mybir as mybir
from concourse.bass_utils import run_bass_kernel_spmd

F32 = mybir.dt.float32
BF16 = mybir.dt.bfloat16
I32 = mybir.dt.int32
AF = mybir.ActivationFunctionType
ALU = mybir.AluOpType
AX = mybir.AxisListType

NCORES = 8
D = 1024
SEQ = 16384
TOK = SEQ // NCORES
NCH = TOK // 128 + 1
H = 8
DK = 128
DV = 256
DFF = 2816
NFT = 2 * DFF // 128
CK = 31
EPS = 1e-5
ALPHA = 2.0 ** 0.25
CG = 2
NSLOT = 6
TM_ORDER = [0, 4, 1, 5, 2, 6, 3, 7]
FUSED = False
LG = [math.log(1.0 - 2.0 ** (-5.0 - h)) for h in range(H)]
DEC128 = [math.exp(LG[h] * 128.0) for h in range(H)]
TWO_PI = 2.0 * math.pi
C1 = 6.28125
C2 = TWO_PI - C1


class Sched:
    ENG = ("pe", "act", "dve", "pool", "sp")

    def __init__(self, nc, es):
        self.nc = nc
        self.streams = {e: [] for e in self.ENG}
        self.count = {e: 0 for e in self.ENG}
        self.sem = {e: es.enter_context(nc.semaphore("s_" + e)) for e in self.ENG}
        self.dsem = {}
        self.dcount = {}
        self.es = es
        self.waited = {}
        self.last_w = {}
        self.readers = {}
        self.eobj = {"pe": nc.tensor, "act": nc.scalar, "dve": nc.vector, "pool": nc.gpsimd, "sp": nc.sync}

    def _need(self, eng, tok):
        key, val, src = tok
        if self.waited.get((eng, key), 0) >= val:
            return
        self.waited[(eng, key)] = val
        self.eobj[eng].wait_ge(self.semof(key), val)

    def _deps(self, eng, reads, writes):
        for r in reads:
            t = self.last_w.get(r)
            if t is not None and not (t[2] == "pe" and eng == "pe"):
                self._need(eng, t)
        for w in writes:
            t = self.last_w.get(w)
            if t is not None and not (t[2] == "pe" and eng == "pe"):
                self._need(eng, t)
            for src, t in self.readers.get(w, {}).items():
                if src != eng:
                    self._need(eng, t)

    def _record(self, tok, reads, writes):
        for w in writes:
            self.last_w[w] = tok
            self.readers[w] = {}
        for r in reads:
            self.readers.setdefault(r, {})[tok[2]] = tok

    def op(self, eng, fn, reads=(), writes=()):
        self._deps(eng, reads, writes)
        self.count[eng] += 1
        tok = (eng, self.count[eng], eng)
        fn(self.eobj[eng]).then_inc(self.sem[eng], 1)
        self._record(tok, reads, writes)

    def dma(self, queue, semname, fn, reads=(), writes=()):
        if semname not in self.dsem:
            self.dsem[semname] = self.es.enter_context(self.nc.semaphore("d_" + semname))
            self.dcount[semname] = 0
        self._deps(queue, reads, writes)
        self.dcount[semname] += 16
        tok = (semname, self.dcount[semname], "dma:" + semname)
        fn(self.eobj[queue]).then_inc(self.dsem[semname], 16)
        self._record(tok, reads, writes)

    def barrier(self):
        for e in self.ENG:
            for o in self.ENG:
                if o != e and self.count[o] > 0:
                    self._need(e, (o, self.count[o], o))
            for s, v in self.dcount.items():
                self._need(e, (s, v, "dma:" + s))

    def semof(self, key):
        return self.sem[key] if key in self.sem else self.dsem[key]

    def emit(self, block):
        nc = self.nc
        engs = {"pe": block.tensor, "act": block.scalar, "dve": block.vector,
                "pool": block.gpsimd, "sp": block.sync}
        for name, deco in engs.items():
            stream = self.streams[name]

            def body(e, stream=stream, name=name):
                for it in stream:
                    if it[0] == "wait":
                        e.wait_ge(self.semof(it[1]), it[2])
                    elif it[0] == "op":
                        it[1](e).then_inc(self.sem[name], 1)
                    else:
                        it[1](e).then_inc(self.dsem[it[2]], 16)
            deco(body)


def build(mode="F"):
    nc = bass.Bass("TRN2", target_bir_lowering=False)
    es = ExitStack()
    S = Sched(nc, es)

    def din(name, shape, dt=F32):
        return nc.dram_tensor(name, list(shape), dt, kind="ExternalInput").ap()

    xh = din("xh", [NCH * 128, D])
    pos_i = din("pos_t", [128, NCH], I32)
    meta = din("meta", [128, 2])
    cf_d = din("cf", [128, 8 * 8 + 8])
    tabs_d = din("tabs", [128, 8 * 3 + 16 * 8])
    caus_d = din("caus", [128, 128])
    idn_d = din("idn", [128, 128])
    invf_d = din("invf", [128, 64])
    oneh_d = din("oneh", [12, 8 * 128])
    c_fm = din("c_fm", [128, 8])
    w_ada = din("w_ada", [D, 6 * D])
    bada_fm = din("bada_fm", [128, 48])
    bada_row = din("bada_row", [1, 6 * D])
    w_in = din("w_in", [D, 10240])
    bin_tm = din("bin_tm", [12, 512])
    bin_fm = din("bin_fm", [128, 48])
    cw_fm = din("cw_fm", [128, 8 * CK])
    cvec_fm = din("cvec_fm", [128, 8 * 3])
    w_conv_out = din("w_conv_out", [D, D])
    gn_fm = din("gn_fm", [128, 32])
    w_ret_out = din("w_ret_out", [2 * D, D])
    w_out = din("w_out", [D, D])
    ln_rows = din("ln_rows", [4, D])
    w_up = din("w_up", [D, 2 * DFF])
    fw_fm = din("fw_fm", [128, NFT * 3])
    fb_fm = din("fb_fm", [128, NFT])
    w_down = din("w_down", [DFF, D])
    if mode in ("F", "B"):
        y = nc.dram_tensor("y", [TOK, D], F32, kind="ExternalOutput").ap()
    if mode == "F":
        bounce_t = nc.dram_tensor("bounce", [2 * H * 128, DV], F32)
        gath_t = nc.dram_tensor("gath", [NCORES * 2 * H * 128, DV], F32)
        bounce = bounce_t.ap()
        gath = gath_t.ap()
    elif mode == "A":
        bounce = nc.dram_tensor("ab", [2 * H * 128, DV], F32, kind="ExternalOutput").ap()
    else:
        gath = din("gath", [NCORES * 2 * H * 128, DV])
    dbg_out = {}

    def sb(name, shape, dt=F32):
        return es.enter_context(nc.sbuf_tensor("sb_" + name, list(shape), dt))

    NT = CG * 128
    meta_t = sb("meta_t", [128, 2])
    cf_t = sb("cf_t", [128, 72])
    tabs = sb("tabs", [128, 24 + 128])
    caus = sb("caus", [128, 128])
    identb = sb("identb", [128, 128], BF16)
    onehb = sb("onehb", [12, 8 * 128], BF16)
    bintm = sb("bintm", [12, 512], BF16)
    binfm = sb("binfm", [128, 48])
    cwfm = sb("cwfm", [128, 8 * CK])
    cvec = sb("cvec", [128, 24])
    gnfm = sb("gnfm", [128, 32])
    fwfm = sb("fwfm", [128, NFT * 3])
    fbfm = sb("fbfm", [128, NFT])
    modfm = sb("modfm", [128, 32])
    gate_bc = sb("gate_bc", [128, 2, D])
    ln_bc = sb("ln_bc", [128, 4, D])
    costab = sb("costab", [128, NCH, 64])
    sintab = sb("sintab", [128, NCH, 64])
    ones32 = sb("ones32", [128, 128])
    wbuf = sb("wbuf", [128, NSLOT, 8, 512], BF16)
    xs2 = sb("xs", [128, 2, CG, D])
    cur = {"xb": 0}

    def XS(ci):
        return xs2[:, cur["xb"], ci, :]

    def XK(ci):
        return ("xs", cur["xb"], ci)
    hT = sb("hT", [128, 8, NT], BF16)
    qxT = sb("qxT", [128, 8, NT], BF16)
    knT = sb("knT", [128, 8, NT], BF16)
    kze = sb("kze", [128, CG, D], BF16)
    pbig = sb("pbig", [128, 8192], BF16)
    aT = sb("aT", [128, 8, 30 + NT], BF16)
    yT = sb("yT", [128, 8, NT])
    pt = sb("pt", [128, 2048])
    a2T = sb("a2T", [128, 8, NT], BF16)
    sig = sb("sig", [128, 8, NT], BF16)
    rT = sb("rT", [128, 16, NT], BF16)
    rnb = sb("rnb", [128, 2048], BF16)
    hb = rnb[:, 0:1024]
    rotb = rnb[:, 1024:2048].rearrange("p (a n) -> p a n", a=2)
    rt32 = pt[:, 0:1024].rearrange("p (a n) -> p a n", a=4)
    ro32 = pt[:, 1024:1536]
    sbf = sb("sbf", [128, 8, 128], BF16)
    Rf = sb("Rf", [128, H, DV])
    Rb = sb("Rb", [128, H, DV], BF16)
    uraw = sb("uraw", [128, 4, 2 + NT])
    uval = sb("uval", [128, 4, NT])
    uacc = sb("uacc", [128, 2, NT])
    uhist = sb("uhist", [128, NFT, 2])
    hTh = sb("hTh", [128, 8, 2], BF16)
    sq = sb("sq", [128, 2, NT])
    lnst = sb("lnst", [128, 3, NT])
    small = sb("small", [128, 160])
    uvf = uval[:].rearrange("p a n -> p (a n)")

    psf = [es.enter_context(nc.psum_tensor("psf%d" % i, [128, 512], F32)) for i in range(6)]
    psb = [es.enter_context(nc.psum_tensor("psb%d" % i, [128, 1024], BF16)) for i in range(2)]
    st = {"f": 0, "b": 0, "w": 0, "ev": 0, "dg": 0, "ur": 0, "gb": 0}

    st["fset"] = [0, 1, 2, 3, 4, 5]

    def bankf():
        fs = st["fset"]
        i = fs[st["f"] % len(fs)]
        st["f"] += 1
        return psf[i], ("psf", i)

    def bankb():
        i = st["b"] % 2
        st["b"] += 1
        return psb[i], ("psb", i)

    v_ap = pbig[:, 0:4096].rearrange("p (c e) -> p c e", c=CG)
    sgT = pbig[:, 4096:8192].rearrange("p (t n) -> p t n", t=16)
    gT = pbig[:, 0:22 * NT].rearrange("p (t n) -> p t n", t=22)
    stage32 = pbig[:].bitcast(F32)

    xi_t = tabs[:, 0:8]
    zeta_t = tabs[:, 8:16]
    zneg_t = tabs[:, 16:24]
    wtA = tabs[:, 24:152].rearrange("p (c h) -> p c h", c=16)
    hm = meta_t[:, 1:2]

    cst = []

    def cload(dst, src, q="sp"):
        S.dma(q, "cst", lambda e, d=dst, s=src: e.dma_start(out=d, in_=s), writes=["const"])

    cload(meta_t[:], meta[:, :])
    cload(cf_t[:], cf_d[:, :])
    cload(tabs[:], tabs_d[:, :])
    cload(caus[:], caus_d[:, :])
    cload(binfm[:], bin_fm[:, :])
    cload(cwfm[:], cw_fm[:, :])
    cload(cvec[:], cvec_fm[:, :])
    cload(gnfm[:], gn_fm[:, :])
    cload(fwfm[:], fw_fm[:, :])
    cload(fbfm[:], fb_fm[:, :])
    for r in range(4):
        cload(ln_bc[:, r, :], ln_rows[r:r + 1, :].partition_broadcast(128))
    cload(gate_bc[:, 0, :], bada_row[0:1, 2 * D:3 * D].partition_broadcast(128))
    cload(gate_bc[:, 1, :], bada_row[0:1, 5 * D:6 * D].partition_broadcast(128))
    idf = pt[:, 0:128]
    ohf = pt[0:12, 128:128 + 1024]
    bif = uvf[0:12, 0:512]
    posf = uvf[:, 512:512 + NCH]
    posi = sb("posi", [128, NCH], I32)
    invf = uvf[:, 576:640]
    cfm = uvf[:, 640:648]
    badf = uvf[:, 648:696]
    cload(idf, idn_d[:, :])
    cload(ohf, oneh_d[:, :])
    cload(bif, bin_tm[:, :])
    cload(posi[:], pos_i[:, :])
    cload(invf, invf_d[:, :])
    cload(cfm, c_fm[:, :])
    cload(badf, bada_fm[:, :])
    S.barrier()
    blocks = []

    uids = {}

    def add_block(w, r0, kc, c0, ncols):
        key = (w.tensor.name, r0, c0)
        if key not in uids:
            uids[key] = len(uids)
        blocks.append((w[r0:r0 + kc * 128, c0:c0 + ncols], kc, ncols, uids[key]))

    def add_dg_block(ct):
        blocks.append((None, 0, 0, ("dg", ct)))

    if mode in ("F", "A"):
        for nb in range(2, 8):
            add_block(w_in, 0, 8, nb * 512, 512)
    groups = [[0]] + [[1 + 2 * g, 2 + 2 * g] for g in range(8)]
    for g in (groups if mode in ("F", "B") else []):
        for nb in TM_ORDER + list(range(8, 16)):
            add_block(w_in, 0, 8, nb * 512, 512)
        for ct in range(8):
            add_dg_block(ct)
        for nb in range(16, 18):
            add_block(w_in, 0, 8, nb * 512, 512)
        for nb in range(2):
            add_block(w_conv_out, 0, 8, nb * 512, 512)
        for nb in range(18, 20):
            add_block(w_in, 0, 8, nb * 512, 512)
        for nb in range(2):
            for kb in range(2):
                add_block(w_ret_out, kb * 1024, 8, nb * 512, 512)
        for nb in range(2):
            add_block(w_out, 0, 8, nb * 512, 512)
        for p in (range(6) if g != [0] else []):
            nv = 512 if p < 5 else 256
            add_block(w_up, 0, 8, p * 512, nv)
            add_block(w_up, 0, 8, DFF + p * 512, nv)
        if g != [0]:
            for nb in range(2):
                add_block(w_down, 0, 8, nb * 512, 512)
                add_block(w_down, 1024, 8, nb * 512, 512)
                add_block(w_down, 2048, 6, nb * 512, 512)
    wst_ = {"loaded": 0, "next": 0}

    def wload_next():
        i = wst_["loaded"]
        if i >= len(blocks):
            return
        src, kc, ncols, uid = blocks[i]
        slot = i % NSLOT
        if src is None:
            ct = uid[1]
            S.dma("sp", "w%d" % slot, lambda e, ct=ct, slot=slot: e.dma_start(
                out=wbuf[:, slot].rearrange("p k n -> p (k n)")[:, 0:CK * 128], in_=dgsc[ct * 128:(ct + 1) * 128, 0:CK * 128]),
                reads=["dgsc"], writes=[("w", slot)])
            wst_["loaded"] += 1
            return
        if wst_.get("wsc") is None:
            nuse = {}
            for b in blocks:
                if b[0] is not None:
                    nuse[b[3]] = nuse.get(b[3], 0) + 1
            wst_["nuse"] = nuse
            wst_["cast"] = set()
            wst_["wsc"] = nc.dram_tensor("wsc", [max(1, len(uids)) * 128, 4096], BF16).ap()
        wsc = wst_["wsc"]
        scr = wsc[uid * 128:(uid + 1) * 128, :].rearrange("p (k n) -> p k n", k=8)[:, 0:kc, 0:ncols]
        if uid not in wst_["cast"]:
            wst_["cast"].add(uid)
            S.dma("pool", "w%d" % slot, lambda e, src=src, kc=kc, ncols=ncols, slot=slot: e.dma_start(
                out=wbuf[:, slot, 0:kc, 0:ncols], in_=src.rearrange("(k p) n -> p k n", p=128)),
                writes=[("w", slot)])
            if wst_["nuse"][uid] > 1:
                def wb(scr=scr, kc=kc, ncols=ncols, slot=slot, uid=uid):
                    S.dma("sp", "wb%d" % slot, lambda e: e.dma_start(
                        out=scr, in_=wbuf[:, slot, 0:kc, 0:ncols]), reads=[("w", slot)], writes=[("wsc", uid)])
                if wst_.get("defer") is not None:
                    wst_["defer"].append(wb)
                else:
                    wb()
        else:
            S.dma("sp", "w%d" % slot, lambda e, scr=scr, kc=kc, ncols=ncols, slot=slot: e.dma_start(
                out=wbuf[:, slot, 0:kc, 0:ncols], in_=scr), reads=[("wsc", uid)], writes=[("w", slot)])
        wst_["loaded"] += 1

    def wacquire():
        i = wst_["next"]
        wst_["next"] += 1
        assert i < wst_["loaded"], "weight block not prefetched"
        slot = i % NSLOT
        return wbuf[:, slot], ("w", slot)

    wst_["defer"] = []
    for _ in range(NSLOT):
        wload_next()
    deferred_wb = wst_["defer"]
    wst_["defer"] = None

    CK_ = ["const"]
    S.op("dve", lambda e: e.tensor_copy(out=identb[:], in_=idf), reads=CK_, writes=["identb"])
    S.op("dve", lambda e: e.tensor_copy(out=onehb[:], in_=ohf), reads=CK_, writes=["onehb"])
    S.op("dve", lambda e: e.tensor_copy(out=bintm[:], in_=bif), reads=CK_, writes=["bintm"])
    S.op("dve", lambda e: e.tensor_copy(out=posf, in_=posi[:]), reads=CK_, writes=["posf"])
    S.op("pool", lambda e: e.memset(ones32[:], 1.0), writes=["ones32"])
    dgsc = nc.dram_tensor("dgsc", [8 * 128, 4096], BF16).ap()
    if mode in ("F", "B"):
        dstg = [yT[:].rearrange("p a n -> p (a n)").bitcast(BF16), rT[:].rearrange("p a n -> p (a n)")]
        for ct in range(8):
            stg = dstg[ct % 2][:, 0:CK * 128]
            sk = ("dgst", ct % 2)
            S.op("dve", lambda e, ct=ct, stg=stg: e.tensor_tensor(
                out=stg.rearrange("p (t m) -> p t m", t=CK),
                in0=identb[:].unsqueeze(1).to_broadcast([128, CK, 128]),
                in1=cwfm[:, ct * CK:(ct + 1) * CK].unsqueeze(2).to_broadcast([128, CK, 128]), op=ALU.mult),
                reads=["identb", "const"], writes=[sk])
            S.dma("sp", "dgw%d" % (ct % 2), lambda e, ct=ct, stg=stg: e.dma_start(
                out=dgsc[ct * 128:(ct + 1) * 128, 0:CK * 128], in_=stg), reads=[sk], writes=["dgsc"])
        S.barrier()
    NA = NCH * 64
    ang = stage32[:, 0:NA].rearrange("p (c j) -> p c j", c=NCH)
    kf = stage32[:, NA:2 * NA].rearrange("p (c j) -> p c j", c=NCH)
    ki = pbig[:].bitcast(I32)[:, 2 * NA:3 * NA].rearrange("p (c j) -> p c j", c=NCH)
    msk = yT[:].rearrange("p a n -> p (a n)")[:, 0:NA].rearrange("p (c j) -> p c j", c=NCH)
    S.op("dve", lambda e: e.tensor_tensor(out=ang, in0=posf.unsqueeze(2).to_broadcast([128, NCH, 64]),
                                          in1=invf.unsqueeze(1).to_broadcast([128, NCH, 64]), op=ALU.mult),
         reads=["posf", "const"], writes=["ang"])
    S.op("dve", lambda e: e.tensor_scalar(out=kf, in0=ang, scalar1=1.0 / TWO_PI, scalar2=None, op0=ALU.mult),
         reads=["ang"], writes=["kf"])
    S.op("dve", lambda e: e.tensor_copy(out=ki, in_=kf), reads=["kf"], writes=["ki"])
    S.op("dve", lambda e: e.tensor_copy(out=kf, in_=ki), reads=["ki"], writes=["kf"])
    S.op("dve", lambda e: e.scalar_tensor_tensor(out=ang, in0=kf, scalar=-C1, in1=ang, op0=ALU.mult, op1=ALU.add),
         reads=["kf", "ang"], writes=["ang"])
    S.op("dve", lambda e: e.scalar_tensor_tensor(out=ang, in0=kf, scalar=-C2, in1=ang, op0=ALU.mult, op1=ALU.add),
         reads=["kf", "ang"], writes=["ang"])

    def wrap(t):
        S.op("dve", lambda e: e.tensor_single_scalar(out=msk, in_=t, scalar=math.pi, op=ALU.is_gt),
             reads=["ang"], writes=["msk"])
        S.op("dve", lambda e: e.scalar_tensor_tensor(out=t, in0=msk, scalar=-TWO_PI, in1=t, op0=ALU.mult, op1=ALU.add),
             reads=["msk", "ang"], writes=["ang"])
        S.op("dve", lambda e: e.tensor_single_scalar(out=msk, in_=t, scalar=-math.pi, op=ALU.is_lt),
             reads=["ang"], writes=["msk"])
        S.op("dve", lambda e: e.scalar_tensor_tensor(out=t, in0=msk, scalar=TWO_PI, in1=t, op0=ALU.mult, op1=ALU.add),
             reads=["msk", "ang"], writes=["ang"])
        S.op("dve", lambda e: e.tensor_scalar(out=t, in0=t, scalar1=math.pi, scalar2=-math.pi, op0=ALU.min, op1=ALU.max),
             reads=["ang"], writes=["ang"])

    wrap(ang)
    S.op("act", lambda e: e.activation(out=sintab[:], in_=ang, func=AF.Sin), reads=["ang"], writes=["sintab"])
    S.op("dve", lambda e: e.tensor_scalar(out=ang, in0=ang, scalar1=math.pi / 2, scalar2=None, op0=ALU.add),
         reads=["ang", "sintab"], writes=["ang"])
    wrap(ang)
    S.op("act", lambda e: e.activation(out=costab[:], in_=ang, func=AF.Sin), reads=["ang"], writes=["costab"])

    silc = small[:, 0:8]
    S.op("act", lambda e: e.activation(out=silc, in_=cfm, func=AF.Silu), reads=CK_, writes=["silc"])
    S.barrier()
    lbc = aT[:].rearrange("p a n -> p (a n)").bitcast(F32)[:, 0:1024].rearrange("p (k m) -> p k m", k=8)
    S.op("dve", lambda e: e.tensor_copy(out=lbc, in_=silc.unsqueeze(2).to_broadcast([128, 8, 128])),
         reads=["silc"], writes=["lbc"])
    wsts = [stage32.rearrange("p (k n) -> p k n", k=8),
            xs2[:].rearrange("p a c d -> p (a c d)").rearrange("p (k n) -> p k n", k=8)]
    for nb in (range(4) if mode == "A" else range(12)):
        wst = wsts[nb % 2]
        wkey = ("wst", nb % 2)
        S.dma("sp", "wst%d" % (nb % 2), lambda e, nb=nb, wst=wst: e.dma_start(
            out=wst, in_=w_ada[:, nb * 512:(nb + 1) * 512].rearrange("(k p) n -> p k n", p=128)),
            writes=[wkey])
        sec = nb // 2
        ps, pk = bankf()
        for k in range(8):
            S.op("pe", lambda e, k=k, ps=ps, wst=wst: e.matmul(ps[:], lbc[:, k, :], wst[:, k, :], start=(k == 0), stop=(k == 7)),
                 reads=["lbc", wkey], writes=[pk])
        if sec in (2, 5):
            gi = 0 if sec == 2 else 1
            dst = gate_bc[:, gi, (nb % 2) * 512:(nb % 2 + 1) * 512]
            S.op("dve", lambda e, ps=ps, dst=dst: e.tensor_tensor(out=dst, in0=ps[:], in1=dst, op=ALU.add),
                 reads=[pk, "const"], writes=["gate_bc"])
        else:
            gt = nb * 4
            col = {0: 0, 1: 8, 3: 16, 4: 24}[sec] + (nb % 2) * 4
            plus1 = 1.0 if sec in (1, 4) else 0.0
            for et in range(4):
                tmp = sq[:, et % 2, 0:128]
                S.op("dve", lambda e, ps=ps, et=et, tmp=tmp: e.tensor_tensor(
                    out=tmp, in0=ps[:, et * 128:(et + 1) * 128], in1=identb[:], op=ALU.mult),
                    reads=[pk, "identb"], writes=[("sq", et % 2)])
                S.op("dve", lambda e, et=et, tmp=tmp, col=col: e.reduce_sum(
                    out=modfm[:, col + et:col + et + 1], in_=tmp, axis=AX.X),
                    reads=[("sq", et % 2)], writes=["modfm"])
            S.op("dve", lambda e, col=col, gt=gt, plus1=plus1: e.scalar_tensor_tensor(
                out=modfm[:, col:col + 4], in0=modfm[:, col:col + 4], scalar=plus1, in1=badf[:, gt:gt + 4],
                op0=ALU.add, op1=ALU.add), reads=["modfm", "const"], writes=["modfm"])
    S.barrier()
    for wb_ in deferred_wb:
        wb_()
    S.op("pool", lambda e: e.memset(uhist[:], 0.0), writes=[("uhist", ft) for ft in range(NFT)])
    S.op("pool", lambda e: e.memset(aT[:], 0.0), writes=[("aT", ct) for ct in range(8)])

    def _kl(k):
        return list(k) if isinstance(k, list) else [k]

    PT_ALL = [("ptA", i) for i in range(4)] + ["ro_a", "ro_b", "ptC"]

    def run(g):
        for _ in g:
            pass

    def interleave(*gens):
        gens = list(gens)
        while gens:
            for g in list(gens):
                try:
                    next(g)
                except StopIteration:
                    gens.remove(g)

    def g_ln_stats(src, srck, par, res):
        o = {0: 16, 1: 112, 2: 128, 3: 144}[par]
        p = str(par)
        stt = small[:, o:o + 12]
        S.op("dve", lambda e: e.bn_stats(out=stt[:, 0:6], in_=src[:, 0:512]), reads=_kl(srck), writes=["lnstat" + p])
        S.op("dve", lambda e: e.bn_stats(out=stt[:, 6:12], in_=src[:, 512:1024]), reads=_kl(srck), writes=["lnstat2" + p])
        yield
        mv = small[:, o + 12:o + 14]
        S.op("dve", lambda e: e.bn_aggr(out=mv, in_=stt), reads=["lnstat" + p, "lnstat2" + p], writes=["mv" + p])
        rs = small[:, o + 14:o + 15]
        nmr = small[:, o + 15:o + 16]
        S.op("dve", lambda e: e.tensor_scalar(out=rs, in0=mv[:, 1:2], scalar1=EPS, scalar2=None, op0=ALU.add),
             reads=["mv" + p], writes=["rs" + p])
        yield
        S.op("act", lambda e: e.activation(out=rs, in_=rs, func=AF.Sqrt), reads=["rs" + p], writes=["rs" + p])
        yield
        S.op("dve", lambda e: e.reciprocal(out=rs, in_=rs), reads=["rs" + p], writes=["rs" + p])
        S.op("dve", lambda e: e.scalar_tensor_tensor(out=nmr, in0=mv[:, 0:1], scalar=-1.0, in1=rs,
                                                     op0=ALU.mult, op1=ALU.mult), reads=["mv" + p, "rs" + p], writes=["nmr" + p])
        res["rs"] = rs
        res["nmr"] = nmr
        res["keys"] = ["rs" + p, "nmr" + p]
        yield

    def transposes_to(src_bf, srck, n_tiles, evac):
        t0 = 0
        while t0 < n_tiles:
            nt = min(8, n_tiles - t0)
            pb, pk = bankb()
            for t in range(nt):
                S.op("pe", lambda e, t=t, t0=t0, pb=pb: e.transpose(
                    pb[:, t * 128:(t + 1) * 128], src_bf[:, (t0 + t) * 128:(t0 + t + 1) * 128], identb[:]),
                    reads=_kl(srck) + ["identb"], writes=[pk])
            evac(pb, pk, t0, nt)
            t0 += nt

    def g_ln1_and_hT(ci, col0, mod_off, par=0, spar=None, xb=None):
        if xb is None:
            xb = cur["xb"]
        src = xs2[:, xb, ci, :]
        xk = ("xs", xb, ci)
        res = {}
        yield from g_ln_stats(src, xk, par if spar is None else spar, res)
        hbp = rnb[:, par * 1024:(par + 1) * 1024]
        hk = ["rnbA"] if par == 0 else ["rnbB0", "rnbB1"]
        S.op("act", lambda e: e.activation(out=hbp, in_=src, func=AF.Identity, bias=res["nmr"], scale=res["rs"]),
             reads=[xk] + res["keys"], writes=hk)
        yield
        pb, pk = bankb()
        for t in range(8):
            S.op("pe", lambda e, t=t: e.transpose(pb[:, t * 128:(t + 1) * 128], hbp[:, t * 128:(t + 1) * 128], identb[:]),
                 reads=hk + ["identb"], writes=[pk])
        yield
        for k in range(8):
            S.op("act", lambda e, k=k: e.activation(
                out=hT[:, k, col0:col0 + 128], in_=pb[:, k * 128:(k + 1) * 128], func=AF.Identity,
                bias=modfm[:, mod_off + k:mod_off + k + 1], scale=modfm[:, mod_off + 8 + k:mod_off + 9 + k]),
                reads=[pk, "modfm"], writes=[("hT", ci)])
            if k % 4 == 3:
                yield

    def ln1_and_hT(ci, col0, mod_off):
        run(g_ln1_and_hT(ci, col0, mod_off, 0))

    def g_post_ln(z, zk, ci, par, grow, brow):
        xsrc = XS(ci)
        xk = XK(ci)
        S.op("dve", lambda e: e.scalar_tensor_tensor(out=z, in0=xsrc, scalar=ALPHA, in1=z,
                                                     op0=ALU.mult, op1=ALU.add),
             reads=[xk] + zk, writes=zk)
        yield
        res = {}
        yield from g_ln_stats(z, zk, par, res)
        S.op("act", lambda e: e.activation(out=z, in_=z, func=AF.Identity, bias=res["nmr"], scale=res["rs"]),
             reads=zk + res["keys"], writes=zk)
        yield
        S.op("dve", lambda e: e.tensor_tensor(out=z, in0=z, in1=ln_bc[:, grow, :], op=ALU.mult),
             reads=zk + ["const"], writes=zk)
        yield
        S.op("pool", lambda e: e.tensor_tensor(out=xsrc, in0=z, in1=ln_bc[:, brow, :], op=ALU.add),
             reads=zk + ["const"], writes=[xk])
        yield

    def proj_tm(ci, col0, wslot, wk, nb_bias):
        ps, pk = bankf()
        for k in range(8):
            S.op("pe", lambda e, k=k, ps=ps: e.matmul(ps[:], hT[:, k, col0:col0 + 128], wslot[:, k, :],
                                                      start=(k == 0), stop=False),
                 reads=[("hT", 0), ("hT", 1), wk], writes=[pk])
        S.op("pe", lambda e, ps=ps: e.matmul(ps[:], onehb[:, nb_bias * 128:(nb_bias + 1) * 128], bintm[:, :],
                                             start=False, stop=True),
             reads=["onehb", "bintm"], writes=[pk])
        return ps, pk

    def rotary(ps, pk, chunk, kind, blk, ci):
        pv = ps[:].rearrange("p (h t j) -> p h t j", h=4, t=2)
        x1 = pv[:, :, 0, :]
        x2 = pv[:, :, 1, :]
        cb = costab[:, chunk, :].unsqueeze(1).to_broadcast([128, 4, 64])
        sn = sintab[:, chunk, :].unsqueeze(1).to_broadcast([128, 4, 64])
        t = [rt32[:, i, :].rearrange("p (h j) -> p h j", h=4) for i in range(4)]
        rd = [pk, "costab", "sintab"]
        S.op("dve", lambda e: e.tensor_tensor(out=t[0], in0=x1, in1=cb, op=ALU.mult), reads=rd, writes=[("ptA", 0)])
        S.op("dve", lambda e: e.tensor_tensor(out=t[1], in0=x2, in1=sn, op=ALU.mult), reads=rd, writes=[("ptA", 1)])
        S.op("dve", lambda e: e.tensor_tensor(out=t[2], in0=x2, in1=cb, op=ALU.mult), reads=rd, writes=[("ptA", 2)])
        S.op("dve", lambda e: e.tensor_tensor(out=t[3], in0=x1, in1=sn, op=ALU.mult), reads=rd, writes=[("ptA", 3)])
        ov = ro32.rearrange("p (h t j) -> p h t j", h=4, t=2)
        S.op("pool", lambda e: e.tensor_tensor(out=ov[:, :, 0, :], in0=t[0], in1=t[1], op=ALU.subtract),
             reads=[("ptA", 0), ("ptA", 1)], writes=["ro_a"])
        S.op("pool", lambda e: e.tensor_tensor(out=ov[:, :, 1, :], in0=t[2], in1=t[3], op=ALU.add),
             reads=[("ptA", 2), ("ptA", 3)], writes=["ro_b"])
        o3 = ro32.rearrange("p (h d) -> p h d", h=4)
        hs = slice(blk * 4, blk * 4 + 4)
        if kind == "kA":
            return None
        dst = rotb[:, ci, :].rearrange("p (h d) -> p h d", h=4)
        tab = xi_t if kind == "q" else zneg_t
        S.op("dve", lambda e: e.tensor_tensor(out=dst, in0=o3, in1=tab[:, hs].unsqueeze(2).to_broadcast([128, 4, 128]),
                                              op=ALU.mult), reads=["ro_a", "ro_b", "const"], writes=["rnbB%d" % ci])
        return None

    def kzeta_from_ro(dst, dstk, blk, table, extra_hm):
        o3 = ro32.rearrange("p (h d) -> p h d", h=4)
        hs = slice(blk * 4, blk * 4 + 4)
        d3 = dst.rearrange("p (h d) -> p h d", h=4)
        S.op("pool", lambda e: e.tensor_tensor(out=d3, in0=o3, in1=table[:, hs].unsqueeze(2).to_broadcast([128, 4, 128]),
                                               op=ALU.mult), reads=["ro_a", "ro_b", "const"], writes=[dstk])
        if extra_hm:
            S.op("pool", lambda e: e.tensor_scalar(out=dst, in0=dst, scalar1=hm, scalar2=0.0, op0=ALU.mult, op1=ALU.add),
                 reads=[dstk, "const"], writes=[dstk])

    if mode in ("F", "A"):
        kw = [wacquire() for _ in range(2)]
        vw = [wacquire() for _ in range(4)]
        vz = pbig[:, 0:2048]
        st["f"] = 0
        kvb = [bankf() for _ in range(4)]
        st["f"] = 4

        def bankA():
            i = 4 + (st["f"] % 2)
            st["f"] += 1
            return psf[i], ("psf", i)

        def kv_accum(c_own, first, last):
            for h in range(H):
                ps, pk = kvb[h // 2]
                S.op("pe", lambda e, h=h, ps=ps: e.matmul(
                    ps[:, (h % 2) * 256:(h % 2 + 1) * 256], kze[:, 0, h * 128:(h + 1) * 128], vz[:, h * 256:(h + 1) * 256],
                    start=first, stop=last), reads=["kzeA", "vz"], writes=[pk])

        def g_pre(c_own):
            chunk = c_own + 1
            par = c_own % 2
            S.dma("sp", "xld%d" % par, lambda e: e.dma_start(out=xs2[:, 0, par, :], in_=xh[chunk * 128:(chunk + 1) * 128, :]),
                  writes=[("xs", 0, par)])
            yield
            yield from g_ln1_and_hT(par, par * 128, 0, par)

        def g_main(c_own):
            chunk = c_own + 1
            par = c_own % 2
            col = par * 128
            for blk in range(2):
                ps, pk = psf[4 + (st["f"] % 2)], ("psf", 4 + (st["f"] % 2))
                st["f"] += 1
                for k in range(8):
                    S.op("pe", lambda e, k=k, ps=ps, blk=blk: e.matmul(ps[:], hT[:, k, col:col + 128], kw[blk][0][:, k, :],
                                                                       start=(k == 0), stop=False),
                         reads=[("hT", par), kw[blk][1]], writes=[pk])
                S.op("pe", lambda e, ps=ps, blk=blk: e.matmul(ps[:], onehb[:, (2 + blk) * 128:(3 + blk) * 128], bintm[:, :],
                                                              start=False, stop=True),
                     reads=["onehb", "bintm"], writes=[pk])
                yield
                rotary(ps, pk, chunk, "kA", blk, 0)
                S.op("act", lambda e, blk=blk: e.activation(out=kze[:, 0, blk * 512:(blk + 1) * 512], in_=ro32, func=AF.Copy),
                     reads=["ro_a", "ro_b"], writes=["kzeA"])
                yield
            for blk in range(4):
                ps, pk = psf[4 + (st["f"] % 2)], ("psf", 4 + (st["f"] % 2))
                st["f"] += 1
                for k in range(8):
                    S.op("pe", lambda e, k=k, ps=ps, blk=blk: e.matmul(ps[:], hT[:, k, col:col + 128], vw[blk][0][:, k, :],
                                                                       start=(k == 0), stop=False),
                         reads=[("hT", par), vw[blk][1]], writes=[pk])
                S.op("pe", lambda e, ps=ps, blk=blk: e.matmul(ps[:], onehb[:, (4 + blk) * 128:(5 + blk) * 128], bintm[:, :],
                                                              start=False, stop=True),
                     reads=["onehb", "bintm"], writes=[pk])
                for hh in range(2):
                    h = blk * 2 + hh
                    S.op("act", lambda e, ps=ps, hh=hh, h=h: e.activation(
                        out=vz[:, h * 256:(h + 1) * 256], in_=ps[:, hh * 256:(hh + 1) * 256], func=AF.Copy,
                        scale=wtA[:, c_own, h:h + 1]), reads=[pk, "const"], writes=["vz"])
                yield

        run(g_pre(0))
        for c_own in range(16):
            gl = [g_main(c_own)]
            if c_own + 1 < 16:
                gl.append(g_pre(c_own + 1))
            interleave(*gl)
            if c_own < 15:
                kv_accum(c_own, c_own == 0, c_own == 14)
            else:
                Bst = stage32[:, 2048:4096].rearrange("p (h e) -> p h e", h=H)
                for b in range(4):
                    ps, pk = kvb[b]
                    S.op("dve", lambda e, ps=ps, b=b: e.tensor_copy(
                        out=Bst[:, 2 * b:2 * b + 2, :], in_=ps[:].rearrange("p (h e) -> p h e", h=2)),
                        reads=[pk], writes=["Bst"])
                kv_accum(c_own, True, True)
                Ast = pt[:].rearrange("p (h e) -> p h e", h=H)
                for h in range(H):
                    ps, pk = kvb[h // 2]
                    S.op("dve", lambda e, h=h, ps=ps: e.scalar_tensor_tensor(
                        out=Ast[:, h, :], in0=Bst[:, h, :], scalar=DEC128[h], in1=ps[:, (h % 2) * 256:(h % 2 + 1) * 256],
                        op0=ALU.mult, op1=ALU.add), reads=[pk, "Bst"], writes=["Ast"] + PT_ALL)
                S.dma("sp", "bnc", lambda e: e.dma_start(
                    out=bounce[0:1024, :].rearrange("(h d) e -> d h e", d=128), in_=Ast), reads=["Ast"], writes=["bounce"])
                S.dma("sp", "bnc", lambda e: e.dma_start(
                    out=bounce[1024:2048, :].rearrange("(h d) e -> d h e", d=128), in_=Bst), reads=["Bst"], writes=["bounce"])
        for _ in range(6):
            wload_next()
        st["f"] = 0
    S.barrier()
    if mode == "A":
        es.close()
        return nc
    if mode == "F":
        cc_sem = es.enter_context(nc.semaphore("cc_sem"))
        nc.gpsimd.collective_compute("AllGather", ALU.bypass, replica_groups=[list(range(NCORES))],
                                     ins=[bounce_t.ap().opt()], outs=[gath_t.ap().opt()]).then_inc(cc_sem)
        nc.gpsimd.wait_ge(cc_sem, 1)
    RF_ALL = [("Rf", h_) for h_ in range(H)]
    S.op("pool", lambda e: e.memset(Rf[:], 0.0), writes=RF_ALL)
    S.barrier()
    gbuf = [stage32[:, 0:2048], stage32[:, 2048:4096], yT[:].rearrange("p a n -> p (a n)"),
            rT[:].rearrange("p a n -> p (a n)").bitcast(F32)]
    gbuf = [g_.rearrange("p (h e) -> p h e", h=H) for g_ in gbuf]
    for j in range(NCORES):
        for part in range(2):
            bi = (2 * j + part) % 4
            gk = ("gst", bi)
            r0_ = j * 2048 + part * 1024
            S.dma("sp", "gld%d" % bi, lambda e, r0_=r0_, bi=bi: e.dma_start(
                out=gbuf[bi], in_=gath[r0_:r0_ + 1024, :].rearrange("(h d) e -> d h e", d=128)),
                reads=["gath"], writes=[gk])
            for h in range(H):
                sc = cf_t[:, j * 8 + h:j * 8 + h + 1] if part == 0 else cf_t[:, 64 + j:65 + j]
                S.op("dve", lambda e, bi=bi, h=h, sc=sc: e.scalar_tensor_tensor(
                    out=Rf[:, h, :], in0=gbuf[bi][:, h, :], scalar=sc, in1=Rf[:, h, :],
                    op0=ALU.mult, op1=ALU.add), reads=[gk, ("Rf", h), "const"], writes=[("Rf", h)])
    S.op("act", lambda e: e.activation(out=Rb[:], in_=Rf[:], func=AF.Copy), reads=RF_ALL, writes=["Rb"])
    S.barrier()

    for gi, g in enumerate(groups):
        C = len(g)
        nt = C * 128
        halo = (gi == 0)
        c0 = g[0]
        cur["xb"] = gi % 2

        def xload(gj):
            gg = groups[gj]
            xb = gj % 2
            S.dma("sp", "xld%d" % xb, lambda e: e.dma_start(
                out=xs2[:, xb, 0:len(gg), :],
                in_=xh[gg[0] * 128:(gg[0] + len(gg)) * 128, :].rearrange("(c p) d -> p c d", p=128)),
                writes=[("xs", xb, i) for i in range(CG)])
        if gi == 0:
            xload(0)
            interleave(*[g_ln1_and_hT(ci, ci * 128, 0, ci) for ci in range(C)])
        if gi + 1 < len(groups):
            xload(gi + 1)

        def s0_next():
            if gi + 1 >= len(groups):
                return []
            return [g_ln1_and_hT(ci, ci * 128, 0, ci, spar=2 + ci, xb=(gi + 1) % 2) for ci in range(len(groups[gi + 1]))]
        pend_tr = []
        for nb in TM_ORDER:
            wslot, wk = wacquire()
            for ci in range(C):
                chunk = g[ci]
                ps, pk = proj_tm(ci, ci * 128, wslot, wk, nb)
                if nb < 2:
                    rotary(ps, pk, chunk, "q", nb, ci)

                    def evq(pb, pk2, t0, ntl, nb=nb, ci=ci):
                        S.op("act", lambda e: e.activation(
                            out=qxT[:, nb * 4:nb * 4 + 4, ci * 128:(ci + 1) * 128],
                            in_=pb[:, 0:512].rearrange("p (k t) -> p k t", k=4), func=AF.Copy),
                            reads=[pk2], writes=["qxT"])
                    pend_tr.append(lambda ci=ci, evq=evq: transposes_to(rotb[:, ci, :], "rnbB%d" % ci, 4, evq))
                elif nb < 4:
                    blk = nb - 2
                    rotary(ps, pk, chunk, "k", blk, ci)
                    kzeta_from_ro(kze[:, ci, blk * 512:(blk + 1) * 512], ("kze", ci), blk, zeta_t, halo)

                    def evk(pb, pk2, t0, ntl, blk=blk, ci=ci):
                        S.op("act", lambda e: e.activation(
                            out=knT[:, blk * 4:blk * 4 + 4, ci * 128:(ci + 1) * 128],
                            in_=pb[:, 0:512].rearrange("p (k t) -> p k t", k=4), func=AF.Copy),
                            reads=[pk2], writes=["knT"])
                    pend_tr.append(lambda ci=ci, evk=evk: transposes_to(rotb[:, ci, :], "rnbB%d" % ci, 4, evk))
                else:
                    blk = nb - 4
                    S.op("act", lambda e, ps=ps, blk=blk, ci=ci: e.activation(
                        out=v_ap[:, ci, blk * 512:(blk + 1) * 512], in_=ps[:], func=AF.Copy),
                        reads=[pk], writes=[("v", ci)])
            if nb >= 4:
                while pend_tr:
                    pend_tr.pop(0)()
            wload_next()
        while pend_tr:
            pend_tr.pop(0)()
        def fm_blocks(nbs):
            for nb in nbs:
                wslot, wk = wacquire()
                for et in range(4):
                    tile = (nb - 8) * 4 + et
                    ps, pk = bankf()
                    for k in range(8):
                        S.op("pe", lambda e, k=k, ps=ps, et=et, wslot=wslot: e.matmul(
                            ps[:, 0:nt], wslot[:, k, et * 128:(et + 1) * 128], hT[:, k, 0:nt],
                            start=(k == 0), stop=(k == 7)), reads=[("hT", 0), ("hT", 1), wk], writes=[pk])
                    bcol = binfm[:, tile:tile + 1]
                    if tile < 16:
                        S.op("act", lambda e, ps=ps, tile=tile, bcol=bcol: e.activation(
                            out=sgT[:, tile, 0:nt], in_=ps[:, 0:nt], func=AF.Silu, bias=bcol),
                            reads=[pk, "const"], writes=["sgT"])
                    elif tile < 24:
                        ct = tile - 16
                        S.op("act", lambda e, ps=ps, ct=ct, bcol=bcol: e.activation(
                            out=yT[:, ct, 0:nt], in_=ps[:, 0:nt], func=AF.Identity, bias=bcol),
                            reads=[pk, "const"], writes=[("yT", ct)])
                    elif tile < 32:
                        ct = tile - 24
                        S.op("act", lambda e, ps=ps, ct=ct, bcol=bcol: e.activation(
                            out=sq[:, ct % 2, 0:nt], in_=ps[:, 0:nt], func=AF.Sigmoid, bias=bcol),
                            reads=[pk, "const"], writes=[("sq", ct % 2)])
                        S.op("dve", lambda e, ct=ct: e.tensor_tensor(
                            out=aT[:, ct, 30:30 + nt], in0=yT[:, ct, 0:nt], in1=sq[:, ct % 2, 0:nt], op=ALU.mult),
                            reads=[("yT", ct), ("sq", ct % 2)], writes=[("aT", ct)])
                        if halo:
                            S.op("dve", lambda e, ct=ct: e.tensor_scalar(
                                out=aT[:, ct, 30:30 + nt], in0=aT[:, ct, 30:30 + nt], scalar1=hm, scalar2=None,
                                op0=ALU.mult), reads=[("aT", ct), "const"], writes=[("aT", ct)])
                    else:
                        dt_ = (tile - 32) % 8
                        S.op("act", lambda e, ps=ps, dt_=dt_, bcol=bcol: e.activation(
                            out=sig[:, dt_, 0:nt], in_=ps[:, 0:nt], func=AF.Sigmoid, bias=bcol),
                            reads=[pk, "const"], writes=["sig"])
                wload_next()

        fm_blocks(range(8, 16))
        def g_conv():
            ps_st, pkst = psf[5], ("psf", 5)
            ps_s = ps_st[:, 0:nt]
            ps_q = ps_st[:, 256:256 + nt]
            pks = pkst
            pkq = pkst
            for ct in range(8):
                ps, pk = psf[3 + ct % 2], ("psf", 3 + ct % 2)
                wslot, wk = wacquire()
                dgv = wslot.rearrange("p k n -> p (k n)")[:, 0:CK * 128].rearrange("p (t m) -> p t m", t=CK)
                for k in range(CK):
                    S.op("pe", lambda e, k=k, ct=ct, ps=ps, dgv=dgv: e.matmul(
                        ps[:, 0:nt], dgv[:, k, :], aT[:, ct, k:k + nt], start=(k == 0), stop=(k == CK - 1)),
                        reads=[wk, ("aT", ct)], writes=[pk])
                    if k % 8 == 7:
                        yield
                wload_next()
                acc = yT[:, ct, 0:nt]
                S.op("act", lambda e, ct=ct, acc=acc, ps=ps: e.activation(
                    out=acc, in_=ps[:, 0:nt], func=AF.Identity, bias=cvec[:, ct:ct + 1]),
                    reads=[pk, "const"], writes=[("yT", ct)])
                S.op("act", lambda e, ct=ct, acc=acc: e.activation(out=sq[:, ct % 2, 0:nt], in_=acc, func=AF.Square),
                     reads=[("yT", ct)], writes=[("sq", ct % 2)])
                S.op("pe", lambda e, ct=ct, acc=acc: e.matmul(ps_s, ones32[:], acc, start=(ct == 0), stop=(ct == 7)),
                     reads=[("yT", ct), "ones32"], writes=[pks])
                S.op("pe", lambda e, ct=ct: e.matmul(ps_q, ones32[:], sq[:, ct % 2, 0:nt], start=(ct == 0), stop=(ct == 7)),
                     reads=[("sq", ct % 2), "ones32"], writes=[pkq])
                yield
            S.op("pool", lambda e: e.tensor_copy(out=aT[:, :, 0:30], in_=aT[:, :, nt:nt + 30]),
                 reads=[("aT", ct) for ct in range(8)], writes=[("aT", ct) for ct in range(8)])
            mean = lnst[:, 0, 0:nt]
            var = lnst[:, 1, 0:nt]
            rstd = lnst[:, 2, 0:nt]
            S.op("act", lambda e: e.activation(out=mean, in_=ps_s, func=AF.Copy, scale=1.0 / D),
                 reads=[pks], writes=["cmean"])
            S.op("dve", lambda e: e.tensor_tensor(out=var, in0=mean, in1=mean, op=ALU.mult), reads=["cmean"], writes=["cvar"])
            S.op("dve", lambda e: e.scalar_tensor_tensor(out=var, in0=ps_q, scalar=1.0 / D, in1=var,
                                                         op0=ALU.mult, op1=ALU.subtract), reads=[pkq, "cvar"], writes=["cvar"])
            S.op("dve", lambda e: e.tensor_scalar(out=var, in0=var, scalar1=EPS, scalar2=None, op0=ALU.add),
                 reads=["cvar"], writes=["cvar"])
            yield
            S.op("act", lambda e: e.activation(out=rstd, in_=var, func=AF.Sqrt), reads=["cvar"], writes=["crstd"])
            S.op("dve", lambda e: e.reciprocal(out=rstd, in_=rstd), reads=["crstd"], writes=["crstd"])
            yield
            for ct in range(8):
                acc = yT[:, ct, 0:nt]
                S.op("dve", lambda e, acc=acc: e.tensor_tensor(out=acc, in0=acc, in1=mean, op=ALU.subtract),
                     reads=[("yT", ct), "cmean"], writes=[("yT", ct)])
                S.op("pool", lambda e, acc=acc: e.tensor_tensor(out=acc, in0=acc, in1=rstd, op=ALU.mult),
                     reads=[("yT", ct), "crstd"], writes=[("yT", ct)])
                S.op("act", lambda e, acc=acc, ct=ct: e.activation(
                    out=a2T[:, ct, 0:nt], in_=acc, func=AF.Silu, bias=cvec[:, 16 + ct:17 + ct], scale=cvec[:, 8 + ct:9 + ct]),
                    reads=[("yT", ct), "const"], writes=["a2T"])
                if ct % 2 == 1:
                    yield

        PT_ALL = [("ptA", i) for i in range(4)] + ["ro_a", "ro_b", "ptC"]
        PTZ = [("ptA", i) for i in range(4)]
        RNB_ALL = ["rnbA", "rnbB0", "rnbB1"]
        UV_ALL = [("uval", i) for i in range(4)]
        def g_ret():
            for ci in range(C):
                cs = slice(ci * 128, (ci + 1) * 128)
                for half in range(2):
                    ps, pk = bankf()
                    for hh in range(4):
                        h = half * 4 + hh
                        S.op("pe", lambda e, ps=ps, hh=hh, h=h, cs=cs: e.matmul(
                            ps[:, hh * 128:(hh + 1) * 128], knT[:, h, cs], qxT[:, h, cs], start=True, stop=True),
                            reads=["knT", "qxT"], writes=[pk])
                    S.op("dve", lambda e, ps=ps, half=half: e.tensor_tensor(
                        out=sbf[:, half * 4:half * 4 + 4, :], in0=ps[:].rearrange("p (h i) -> p h i", h=4),
                        in1=caus[:].unsqueeze(1).to_broadcast([128, 4, 128]), op=ALU.mult),
                        reads=[pk, "const"], writes=[("sbf", half)])
                    yield
                r32 = pt[:].rearrange("p (h e) -> p h e", h=H)
                for pr in range(4):
                    ps, pk = bankf()
                    for hh in range(2):
                        h = pr * 2 + hh
                        S.op("pe", lambda e, ps=ps, hh=hh, h=h, ci=ci: e.matmul(
                            ps[:, hh * 256:(hh + 1) * 256], sbf[:, h, :], v_ap[:, ci, h * 256:(h + 1) * 256],
                            start=True, stop=False), reads=[("sbf", h // 4), ("v", ci)], writes=[pk])
                        S.op("pe", lambda e, ps=ps, hh=hh, h=h, cs=cs: e.matmul(
                            ps[:, hh * 256:(hh + 1) * 256], qxT[:, h, cs], Rb[:, h, :],
                            start=False, stop=True), reads=["qxT", "Rb"], writes=[pk])
                    S.op("act", lambda e, ps=ps, pr=pr: e.activation(
                        out=pt[:, pr * 512:(pr + 1) * 512], in_=ps[:], func=AF.Copy), reads=[pk], writes=PT_ALL)
                    yield
                for pr in range(4):
                    ps, pk = bankf()
                    for hh in range(2):
                        h = pr * 2 + hh
                        S.op("pe", lambda e, ps=ps, hh=hh, h=h, ci=ci: e.matmul(
                            ps[:, hh * 256:(hh + 1) * 256], kze[:, ci, h * 128:(h + 1) * 128], v_ap[:, ci, h * 256:(h + 1) * 256],
                            start=True, stop=True), reads=[("kze", ci), ("v", ci)], writes=[pk])
                    for hh in range(2):
                        h = pr * 2 + hh
                        S.op("dve", lambda e, ps=ps, hh=hh, h=h: e.scalar_tensor_tensor(
                            out=Rf[:, h, :], in0=Rf[:, h, :], scalar=DEC128[h], in1=ps[:, hh * 256:(hh + 1) * 256],
                            op0=ALU.mult, op1=ALU.add), reads=[pk, ("Rf", h)], writes=[("Rf", h)])
                    yield
                S.op("act", lambda e: e.activation(out=Rb[:], in_=Rf[:], func=AF.Copy), reads=RF_ALL, writes=["Rb"])
                gst6 = small[:, 32:80].rearrange("p (h s) -> p h s", h=H)
                gmv = small[:, 80:96].rearrange("p (h s) -> p h s", h=H)
                gr = small[:, 96:104]
                gnm = small[:, 104:112]
                for h in range(H):
                    S.op("dve", lambda e, h=h: e.bn_stats(out=gst6[:, h, :], in_=r32[:, h, :]), reads=PT_ALL, writes=[("gst6", h)])
                    S.op("dve", lambda e, h=h: e.bn_aggr(out=gmv[:, h, :], in_=gst6[:, h, :]), reads=[("gst6", h)], writes=["gmv"])
                    if h % 2 == 1:
                        yield
                S.op("dve", lambda e: e.tensor_scalar(out=gr, in0=gmv[:, :, 1], scalar1=EPS, scalar2=None, op0=ALU.add),
                     reads=["gmv"], writes=["gr"])
                S.op("act", lambda e: e.activation(out=gr, in_=gr, func=AF.Sqrt), reads=["gr"], writes=["gr"])
                S.op("dve", lambda e: e.reciprocal(out=gr, in_=gr), reads=["gr"], writes=["gr"])
                S.op("dve", lambda e: e.tensor_tensor(out=r32, in0=r32, in1=gmv[:, :, 0:1].to_broadcast([128, H, DV]),
                                                      op=ALU.subtract), reads=PT_ALL + ["gmv"], writes=PT_ALL)
                S.op("pool", lambda e: e.tensor_tensor(out=rnb[:].rearrange("p (h e) -> p h e", h=H), in0=r32,
                                                       in1=gr.unsqueeze(2).to_broadcast([128, H, DV]), op=ALU.mult),
                     reads=PT_ALL + ["gr"], writes=RNB_ALL)
                yield

                def evr(pb, pk2, t0, ntl, ci=ci):
                    tmp = uvf[:, 0:1024].rearrange("p (k t) -> p k t", k=8)
                    S.op("dve", lambda e: e.tensor_tensor(
                        out=tmp, in0=pb[:].rearrange("p (k t) -> p k t", k=8),
                        in1=gnfm[:, t0:t0 + 8].unsqueeze(2).to_broadcast([128, 8, 128]), op=ALU.mult),
                        reads=[pk2, "const"], writes=UV_ALL)
                    S.op("pool", lambda e: e.tensor_tensor(
                        out=tmp, in0=tmp, in1=gnfm[:, 16 + t0:24 + t0].unsqueeze(2).to_broadcast([128, 8, 128]), op=ALU.add),
                        reads=UV_ALL + ["const"], writes=UV_ALL)
                    S.op("pool", lambda e: e.tensor_tensor(
                        out=rT[:, t0:t0 + 8, ci * 128:(ci + 1) * 128], in0=tmp, in1=sgT[:, t0:t0 + 8, ci * 128:(ci + 1) * 128],
                        op=ALU.mult), reads=UV_ALL + ["sgT"], writes=["rT"])
                transposes_to(rnb, RNB_ALL, 16, evr)
                yield

        st["fset"] = [0, 1, 2]
        interleave(g_conv(), g_ret())
        st["fset"] = [0, 1, 2, 3, 4, 5]
        fm_blocks(range(16, 18))
        for nb in range(2):
            wslot, wk = wacquire()
            for et in range(4):
                dt_ = nb * 4 + et
                ps, pk = bankf()
                for k in range(8):
                    S.op("pe", lambda e, k=k, ps=ps, et=et, wslot=wslot: e.matmul(
                        ps[:, 0:nt], wslot[:, k, et * 128:(et + 1) * 128], a2T[:, k, 0:nt],
                        start=(k == 0), stop=(k == 7)), reads=["a2T", wk], writes=[pk])
                S.op("dve", lambda e, ps=ps, dt_=dt_: e.tensor_tensor(
                    out=yT[:, dt_, 0:nt], in0=ps[:, 0:nt], in1=sig[:, dt_, 0:nt], op=ALU.mult),
                    reads=[pk, "sig"], writes=[("yT", dt_)])
            wload_next()
        fm_blocks(range(18, 20))
        for nb in range(2):
            wa, wka = wacquire()
            wb_, wkb = wacquire()
            for et in range(4):
                dt_ = nb * 4 + et
                ps, pk = bankf()
                for k in range(16):
                    ws = wa if k < 8 else wb_
                    S.op("pe", lambda e, k=k, ps=ps, et=et, ws=ws: e.matmul(
                        ps[:, 0:nt], ws[:, k % 8, et * 128:(et + 1) * 128], rT[:, k, 0:nt],
                        start=(k == 0), stop=(k == 15)), reads=["rT", wka, wkb], writes=[pk])
                tmp = uacc[:, et % 2, 0:nt]
                S.op("dve", lambda e, ps=ps, dt_=dt_, tmp=tmp: e.tensor_tensor(
                    out=tmp, in0=ps[:, 0:nt], in1=sig[:, dt_, 0:nt], op=ALU.mult),
                    reads=[pk, "sig"], writes=[("uacc", et % 2)])
                S.op("pool", lambda e, dt_=dt_, tmp=tmp: e.tensor_tensor(
                    out=a2T[:, dt_, 0:nt], in0=tmp, in1=yT[:, dt_, 0:nt], op=ALU.add),
                    reads=[("uacc", et % 2), ("yT", dt_)], writes=["a2T"])
            wload_next()
            wload_next()
        wo = [wacquire() for _ in range(2)]
        ZK = [PTZ, ["ro_a", "ro_b", "ptC"]]
        for ci in range(C):
            z = pt[:, ci * 1024:(ci + 1) * 1024]
            for nb in range(2):
                ps, pk = bankf()
                for k in range(8):
                    S.op("pe", lambda e, k=k, ps=ps, nb=nb, ci=ci: e.matmul(
                        ps[:], a2T[:, k, ci * 128:(ci + 1) * 128], wo[nb][0][:, k, :], start=(k == 0), stop=(k == 7)),
                        reads=["a2T", wo[nb][1]], writes=[pk])
                S.op("dve", lambda e, ps=ps, nb=nb, z=z: e.tensor_tensor(
                    out=z[:, nb * 512:(nb + 1) * 512], in0=ps[:], in1=gate_bc[:, 0, nb * 512:(nb + 1) * 512], op=ALU.mult),
                    reads=[pk, "gate_bc"], writes=ZK[ci])
        interleave(*[g_post_ln(pt[:, ci * 1024:(ci + 1) * 1024], ZK[ci], ci, ci, 0, 1) for ci in range(C)])
        wload_next()
        wload_next()
        interleave(*[g_ln1_and_hT(ci, ci * 128, 16, ci) for ci in range(C)])
        if halo:
            S.op("act", lambda e: e.activation(out=hTh[:], in_=hT[:, :, 126:128], func=AF.Copy),
                 reads=[("hT", 0)], writes=["hTh"])
            interleave(*s0_next())
            continue
        first_own = (gi == 1)
        GT_ALL = ["gT", ("v", 0), ("v", 1), "sgT"]
        gbufs = [(uacc[:, 0, 0:nt], ("uacc", 0)), (uacc[:, 1, 0:nt], ("uacc", 1)),
                 (sq[:, 0, 0:nt], ("sq", 0)), (sq[:, 1, 0:nt], ("sq", 1))]
        pending = []

        def flush_tail():
            while pending:
                pending.pop(0)()

        for p in range(6):
            ntile = 4 if p < 5 else 2
            for part in range(2):
                wslot, wk = wacquire()
                for et in range(ntile):
                    ft = p * 4 + et + part * 22
                    ps, pk = bankf()
                    for k in range(8):
                        S.op("pe", lambda e, k=k, ps=ps, et=et, wslot=wslot: e.matmul(
                            ps[:, 0:nt], wslot[:, k, et * 128:(et + 1) * 128], hT[:, k, 0:nt],
                            start=(k == 0), stop=(k == 7)), reads=[("hT", 0), ("hT", 1), wk], writes=[pk])
                    ub4 = st["ur"] % 4
                    st["ur"] += 1
                    ur = uraw[:, ub4, :]
                    urb = ("urawb", ub4)
                    urh = ("urawh", ub4)
                    if halo:
                        S.op("act", lambda e, ps=ps, ur=ur: e.activation(out=ur[:, 2:2 + nt], in_=ps[:, 0:nt], func=AF.Copy),
                             reads=[pk], writes=[urb])
                        S.op("pool", lambda e, ur=ur, ft=ft: e.tensor_scalar(
                            out=uhist[:, ft, :], in0=ur[:, nt:nt + 2], scalar1=hm, scalar2=0.0, op0=ALU.mult, op1=ALU.add),
                            reads=[urb, "const"], writes=[("uhist", ft)])
                        continue
                    if first_own:
                        for k in range(8):
                            S.op("pe", lambda e, k=k, ps=ps, et=et, wslot=wslot: e.matmul(
                                ps[:, nt:nt + 2], wslot[:, k, et * 128:(et + 1) * 128], hTh[:, k, :],
                                start=(k == 0), stop=(k == 7)), reads=["hTh", wk], writes=[pk])
                        S.op("act", lambda e, ps=ps, ur=ur: e.activation(out=ur[:, 0:2], in_=ps[:, nt:nt + 2], func=AF.Copy, scale=hm),
                             reads=[pk, "const"], writes=[urh])
                    else:
                        S.op("pool", lambda e, ur=ur, ft=ft: e.tensor_copy(out=ur[:, 0:2], in_=uhist[:, ft, :]),
                             reads=[("uhist", ft)], writes=[urh])
                    S.op("act", lambda e, ps=ps, ur=ur: e.activation(out=ur[:, 2:2 + nt], in_=ps[:, 0:nt], func=AF.Copy),
                         reads=[pk], writes=[urb])
                    if part == 0:
                        acc, acck = uval[:, et, 0:nt], ("uval", et)
                    else:
                        acc, acck = gbufs[st["gb"] % 4]
                        st["gb"] += 1
                    fw = fwfm[:, ft * 3:ft * 3 + 3]
                    S.op("act", lambda e, ps=ps, acc=acc, fw=fw, ft=ft: e.activation(
                        out=acc, in_=ps[:, 0:nt], func=AF.Identity, bias=fbfm[:, ft:ft + 1], scale=fw[:, 2:3]),
                        reads=[pk, "const"], writes=[acck])
                    flush_tail()
                    S.op("dve", lambda e, ur=ur, acc=acc, fw=fw: e.scalar_tensor_tensor(
                        out=acc, in0=ur[:, 1:1 + nt], scalar=fw[:, 1:2], in1=acc, op0=ALU.mult, op1=ALU.add),
                        reads=[urb, urh, acck], writes=[acck])
                    S.op("dve", lambda e, ur=ur, acc=acc, fw=fw: e.scalar_tensor_tensor(
                        out=acc, in0=ur[:, 0:nt], scalar=fw[:, 0:1], in1=acc, op0=ALU.mult, op1=ALU.add),
                        reads=[urb, urh, acck], writes=[acck])
                    S.op("pool", lambda e, ur=ur, ft=ft: e.tensor_copy(out=uhist[:, ft, :], in_=ur[:, nt:nt + 2]),
                         reads=[urb], writes=[("uhist", ft)])
                    if part == 1:
                        def tail(acc=acc, acck=acck, et=et, ft=ft):
                            S.op("act", lambda e: e.activation(out=acc, in_=acc, func=AF.Silu),
                                 reads=[acck], writes=[acck])
                            S.op("pool", lambda e: e.tensor_tensor(
                                out=gT[:, ft - 22, 0:nt], in0=uval[:, et, 0:nt], in1=acc, op=ALU.mult),
                                reads=[acck, ("uval", et)], writes=GT_ALL)
                        pending.append(tail)
                wload_next()
        flush_tail()
        if halo:
            interleave(*s0_next())
            continue
        for nb in range(2):
            wd = [wacquire() for _ in range(3)]
            for ci in range(C):
                z = pt[:, ci * 1024:(ci + 1) * 1024]
                ps, pk = bankf()
                for k in range(22):
                    ws, wkk = wd[k // 8]
                    S.op("pe", lambda e, k=k, ps=ps, ws=ws, ci=ci: e.matmul(
                        ps[:], gT[:, k, ci * 128:(ci + 1) * 128], ws[:, k % 8, :], start=(k == 0), stop=(k == 21)),
                        reads=GT_ALL + [wkk], writes=[pk])
                S.op("dve", lambda e, ps=ps, nb=nb, z=z: e.tensor_tensor(
                    out=z[:, nb * 512:(nb + 1) * 512], in0=ps[:], in1=gate_bc[:, 1, nb * 512:(nb + 1) * 512], op=ALU.mult),
                    reads=[pk, "gate_bc"], writes=ZK[ci])
            for _ in range(3):
                wload_next()
        interleave(*([g_post_ln(pt[:, ci * 1024:(ci + 1) * 1024], ZK[ci], ci, ci, 2, 3) for ci in range(C)] + s0_next()))
        r0 = (c0 - 1) * 128
        S.dma("sp", "yst%d" % (gi % 2), lambda e, r0=r0, C=C, gi=gi: e.dma_start(
            out=y[r0:r0 + C * 128, :].rearrange("(c p) d -> p c d", p=128), in_=xs2[:, gi % 2, 0:C, :]),
            reads=[("xs", gi % 2, i) for i in range(CG)], writes=["yout"])
    S.barrier()

    es.close()
    return nc


def _bf16_safe(a):
    return np.ascontiguousarray(a, dtype=np.float32)


def kernel(x, c, positions, w_ada, b_ada, w_in, b_in, conv_dw_w, conv_dw_b, conv_ln_g, conv_ln_b,
           w_conv_out, ret_gn_g, ret_gn_b, w_ret_out, w_out, ln1_g, ln1_b, w_up, ffn_dw_w, ffn_dw_b,
           w_down, ln2_g, ln2_b):
    f = _bf16_safe
    x2 = f(x)[0]
    pos = np.asarray(positions)[0].astype(np.int32)

    def fm(v, ntile):
        return np.ascontiguousarray(np.asarray(v, np.float32).reshape(ntile, 128).T)

    lg = np.array(LG, np.float64)
    p = np.arange(128, dtype=np.float64)[:, None]
    s = 128.0 ** -0.5
    xi = np.exp(lg[None, :] * (p + 1.0))
    zeta = s * np.exp(lg[None, :] * (127.0 - p))
    zneg = s * np.exp(-lg[None, :] * (p + 1.0))
    wtA = np.zeros((128, 16, 8))
    for cc in range(16):
        if cc < 15:
            wtA[:, cc, :] = s * np.exp(lg[None, :] * (1919.0 - (128.0 * cc + p)))
        else:
            wtA[:, cc, :] = s * np.exp(lg[None, :] * (127.0 - p))
    tabs = np.concatenate([xi, zeta, zneg, wtA.reshape(128, 128)], axis=1).astype(np.float32)
    caus = (np.arange(128)[None, :] >= np.arange(128)[:, None]).astype(np.float32)
    idn = np.eye(128, dtype=np.float32)
    half = 64
    invf1 = (np.float32(10000.0) ** (-np.arange(half, dtype=np.float32) / np.float32(half))).astype(np.float32)
    invf = np.ascontiguousarray(np.broadcast_to(invf1[None, :], (128, 64))).astype(np.float32)
    oneh = np.zeros((12, 8 * 128), np.float32)
    for r in range(8):
        oneh[r, r * 128:(r + 1) * 128] = 1.0

    b_in1 = np.asarray(b_in, np.float32)[0]
    shared = {
        "tabs": tabs, "caus": caus, "idn": idn, "invf": invf, "oneh": oneh,
        "c_fm": fm(np.asarray(c)[0], 8),
        "w_ada": f(w_ada)[0], "bada_fm": fm(np.asarray(b_ada)[0], 48), "bada_row": f(b_ada),
        "w_in": f(w_in)[0],
        "bin_tm": np.ascontiguousarray(b_in1[0:6144].reshape(12, 512)),
        "bin_fm": fm(np.concatenate([b_in1[4096:6144], b_in1[6144:10240]]), 48),
        "cw_fm": np.ascontiguousarray(np.asarray(conv_dw_w, np.float32)[0].T.reshape(8, 128, CK).transpose(1, 0, 2).reshape(128, 8 * CK)),
        "cvec_fm": np.concatenate([fm(np.asarray(conv_dw_b)[0], 8), fm(np.asarray(conv_ln_g)[0], 8),
                                   fm(np.asarray(conv_ln_b)[0], 8)], axis=1),
        "w_conv_out": f(w_conv_out)[0],
        "gn_fm": np.concatenate([fm(np.asarray(ret_gn_g)[0], 16), fm(np.asarray(ret_gn_b)[0], 16)], axis=1),
        "w_ret_out": f(w_ret_out)[0], "w_out": f(w_out)[0],
        "ln_rows": np.stack([np.asarray(a, np.float32)[0] for a in (ln1_g, ln1_b, ln2_g, ln2_b)]),
        "w_up": f(w_up)[0],
        "fw_fm": np.ascontiguousarray(np.asarray(ffn_dw_w, np.float32)[0].T.reshape(NFT, 128, 3).transpose(1, 0, 2).reshape(128, NFT * 3)),
        "fb_fm": fm(np.asarray(ffn_dw_b)[0], NFT),
        "w_down": f(w_down)[0],
    }
    shared = {k: np.ascontiguousarray(v, dtype=np.float32) for k, v in shared.items()}
    in_maps = []
    for i in range(NCORES):
        start = i * TOK
        xh = np.zeros((NCH * 128, D), np.float32)
        ph = np.zeros((NCH * 128,), np.int32)
        xh[128:] = x2[start:start + TOK]
        ph[128:] = pos[start:start + TOK]
        if i > 0:
            xh[:128] = x2[start - 128:start]
            ph[:128] = pos[start - 128:start]
        meta = np.zeros((128, 2), np.float32)
        meta[:, 0] = i
        meta[:, 1] = 1.0 if i > 0 else 0.0
        cf = np.zeros((8, 8))
        cb = np.zeros((8,))
        for j in range(NCORES):
            if j <= i - 2:
                cf[j, :] = np.exp(lg * (1920.0 + 2048.0 * (i - 2 - j)))
            if j == i - 1:
                cb[j] = 1.0
        cft = np.concatenate([cf.reshape(-1), cb]).astype(np.float32)
        m = dict(shared)
        m["xh"] = xh
        m["pos_t"] = np.ascontiguousarray(ph.reshape(NCH, 128).T)
        m["meta"] = meta
        m["cf"] = np.ascontiguousarray(np.broadcast_to(cft[None, :], (128, 72))).astype(np.float32)
        in_maps.append(m)
    if FUSED:
        nc = build("F")
        res = run_bass_kernel_spmd(nc, in_maps, core_ids=list(range(NCORES)))
    else:
        nca = build("A")
        resa = run_bass_kernel_spmd(nca, in_maps, core_ids=list(range(NCORES)))
        gath = np.ascontiguousarray(np.concatenate([r["ab"] for r in resa.results], axis=0), dtype=np.float32)
        for m in in_maps:
            m["gath"] = gath
        nc = build("B")
        res = run_bass_kernel_spmd(nc, in_maps, core_ids=list(range(NCORES)))
    out = np.concatenate([r["y"] for r in res.results], axis=0)
    return out.reshape(1, SEQ, D).astype(np.float32)
```

```python
import math
from contextlib import ExitStack
import numpy as np
import concourse.bass as bass
import concourse.mybir as mybir
from concourse.bass_utils import run_bass_kernel_spmd

F32 = mybir.dt.float32
BF16 = mybir.dt.bfloat16
I32 = mybir.dt.int32
AF = mybir.ActivationFunctionType
ALU = mybir.AluOpType
AX = mybir.AxisListType

NCORES = 8
D = 1024
SEQ = 16384
TOK = SEQ // NCORES
NCH = TOK // 128 + 1
H = 8
DK = 128
DV = 256
DFF = 2816
NFT = 2 * DFF // 128
CK = 31
EPS = 1e-5
ALPHA = 2.0 ** 0.25
CG = 2
NSLOT = 6
TM_ORDER = [0, 4, 1, 5, 2, 6, 3, 7]
FUSED = False
LG = [math.log(1.0 - 2.0 ** (-5.0 - h)) for h in range(H)]
DEC128 = [math.exp(LG[h] * 128.0) for h in range(H)]
TWO_PI = 2.0 * math.pi
C1 = 6.28125
C2 = TWO_PI - C1


class Sched:
    ENG = ("pe", "act", "dve", "pool", "sp")

    def __init__(self, nc, es):
        self.nc = nc
        self.streams = {e: [] for e in self.ENG}
        self.count = {e: 0 for e in self.ENG}
        self.sem = {e: es.enter_context(nc.semaphore("s_" + e)) for e in self.ENG}
        self.dsem = {}
        self.dcount = {}
        self.es = es
        self.waited = {}
        self.last_w = {}
        self.readers = {}
        self.eobj = {"pe": nc.tensor, "act": nc.scalar, "dve": nc.vector, "pool": nc.gpsimd, "sp": nc.sync}

    def _need(self, eng, tok):
        key, val, src = tok
        if self.waited.get((eng, key), 0) >= val:
            return
        self.waited[(eng, key)] = val
        self.eobj[eng].wait_ge(self.semof(key), val)

    def _deps(self, eng, reads, writes):
        for r in reads:
            t = self.last_w.get(r)
            if t is not None and not (t[2] == "pe" and eng == "pe"):
                self._need(eng, t)
        for w in writes:
            t = self.last_w.get(w)
            if t is not None and not (t[2] == "pe" and eng == "pe"):
                self._need(eng, t)
            for src, t in self.readers.get(w, {}).items():
                if src != eng:
                    self._need(eng, t)

    def _record(self, tok, reads, writes):
        for w in writes:
            self.last_w[w] = tok
            self.readers[w] = {}
        for r in reads:
            self.readers.setdefault(r, {})[tok[2]] = tok

    def op(self, eng, fn, reads=(), writes=()):
        self._deps(eng, reads, writes)
        self.count[eng] += 1
        tok = (eng, self.count[eng], eng)
        fn(self.eobj[eng]).then_inc(self.sem[eng], 1)
        self._record(tok, reads, writes)

    def dma(self, queue, semname, fn, reads=(), writes=()):
        if semname not in self.dsem:
            self.dsem[semname] = self.es.enter_context(self.nc.semaphore("d_" + semname))
            self.dcount[semname] = 0
        self._deps(queue, reads, writes)
        self.dcount[semname] += 16
        tok = (semname, self.dcount[semname], "dma:" + semname)
        fn(self.eobj[queue]).then_inc(self.dsem[semname], 16)
        self._record(tok, reads, writes)

    def barrier(self):
        for e in self.ENG:
            for o in self.ENG:
                if o != e and self.count[o] > 0:
                    self._need(e, (o, self.count[o], o))
            for s, v in self.dcount.items():
                self._need(e, (s, v, "dma:" + s))

    def semof(self, key):
        return self.sem[key] if key in self.sem else self.dsem[key]

    def emit(self, block):
        nc = self.nc
        engs = {"pe": block.tensor, "act": block.scalar, "dve": block.vector,
                "pool": block.gpsimd, "sp": block.sync}
        for name, deco in engs.items():
            stream = self.streams[name]

            def body(e, stream=stream, name=name):
                for it in stream:
                    if it[0] == "wait":
                        e.wait_ge(self.semof(it[1]), it[2])
                    elif it[0] == "op":
                        it[1](e).then_inc(self.sem[name], 1)
                    else:
                        it[1](e).then_inc(self.dsem[it[2]], 16)
            deco(body)


def build(mode="F"):
    nc = bass.Bass("TRN2", target_bir_lowering=False)
    es = ExitStack()
    S = Sched(nc, es)

    def din(name, shape, dt=F32):
        return nc.dram_tensor(name, list(shape), dt, kind="ExternalInput").ap()

    xh = din("xh", [NCH * 128, D])
    pos_i = din("pos_t", [128, NCH], I32)
    meta = din("meta", [128, 2])
    cf_d = din("cf", [128, 8 * 8 + 8])
    tabs_d = din("tabs", [128, 8 * 3 + 16 * 8])
    caus_d = din("caus", [128, 128])
    idn_d = din("idn", [128, 128])
    invf_d = din("invf", [128, 64])
    oneh_d = din("oneh", [12, 8 * 128])
    c_fm = din("c_fm", [128, 8])
    w_ada = din("w_ada", [D, 6 * D])
    bada_fm = din("bada_fm", [128, 48])
    bada_row = din("bada_row", [1, 6 * D])
    w_in = din("w_in", [D, 10240])
    bin_tm = din("bin_tm", [12, 512])
    bin_fm = din("bin_fm", [128, 48])
    cw_fm = din("cw_fm", [128, 8 * CK])
    cvec_fm = din("cvec_fm", [128, 8 * 3])
    w_conv_out = din("w_conv_out", [D, D])
    gn_fm = din("gn_fm", [128, 32])
    w_ret_out = din("w_ret_out", [2 * D, D])
    w_out = din("w_out", [D, D])
    ln_rows = din("ln_rows", [4, D])
    w_up = din("w_up", [D, 2 * DFF])
    fw_fm = din("fw_fm", [128, NFT * 3])
    fb_fm = din("fb_fm", [128, NFT])
    w_down = din("w_down", [DFF, D])
    if mode in ("F", "B"):
        y = nc.dram_tensor("y", [TOK, D], F32, kind="ExternalOutput").ap()
    if mode == "F":
        bounce_t = nc.dram_tensor("bounce", [2 * H * 128, DV], F32)
        gath_t = nc.dram_tensor("gath", [NCORES * 2 * H * 128, DV], F32)
        bounce = bounce_t.ap()
        gath = gath_t.ap()
    elif mode == "A":
        bounce = nc.dram_tensor("ab", [2 * H * 128, DV], F32, kind="ExternalOutput").ap()
    else:
        gath = din("gath", [NCORES * 2 * H * 128, DV])
    dbg_out = {}

    def sb(name, shape, dt=F32):
        return es.enter_context(nc.sbuf_tensor("sb_" + name, list(shape), dt))

    NT = CG * 128
    meta_t = sb("meta_t", [128, 2])
    cf_t = sb("cf_t", [128, 72])
    tabs = sb("tabs", [128, 24 + 128])
    caus = sb("caus", [128, 128])
    identb = sb("identb", [128, 128], BF16)
    onehb = sb("onehb", [12, 8 * 128], BF16)
    bintm = sb("bintm", [12, 512], BF16)
    binfm = sb("binfm", [128, 48])
    cwfm = sb("cwfm", [128, 8 * CK])
    cvec = sb("cvec", [128, 24])
    gnfm = sb("gnfm", [128, 32])
    fwfm = sb("fwfm", [128, NFT * 3])
    fbfm = sb("fbfm", [128, NFT])
    modfm = sb("modfm", [128, 32])
    gate_bc = sb("gate_bc", [128, 2, D])
    ln_bc = sb("ln_bc", [128, 4, D])
    costab = sb("costab", [128, NCH, 64])
    sintab = sb("sintab", [128, NCH, 64])
    ones32 = sb("ones32", [128, 128])
    wbuf = sb("wbuf", [128, NSLOT, 8, 512], BF16)
    xs2 = sb("xs", [128, 2, CG, D])
    cur = {"xb": 0}

    def XS(ci):
        return xs2[:, cur["xb"], ci, :]

    def XK(ci):
        return ("xs", cur["xb"], ci)
    hT = sb("hT", [128, 8, NT], BF16)
    qxT = sb("qxT", [128, 8, NT], BF16)
    knT = sb("knT", [128, 8, NT], BF16)
    kze = sb("kze", [128, CG, D], BF16)
    pbig = sb("pbig", [128, 8192], BF16)
    aT = sb("aT", [128, 8, 30 + NT], BF16)
    yT = sb("yT", [128, 8, NT])
    pt = sb("pt", [128, 2048])
    a2T = sb("a2T", [128, 8, NT], BF16)
    sig = sb("sig", [128, 8, NT], BF16)
    rT = sb("rT", [128, 16, NT], BF16)
    rnb = sb("rnb", [128, 2048], BF16)
    hb = rnb[:, 0:1024]
    rotb = rnb[:, 1024:2048].rearrange("p (a n) -> p a n", a=2)
    rt32 = pt[:, 0:1024].rearrange("p (a n) -> p a n", a=4)
    ro32 = pt[:, 1024:1536]
    sbf = sb("sbf", [128, 8, 128], BF16)
    Rf = sb("Rf", [128, H, DV])
    Rb = sb("Rb", [128, H, DV], BF16)
    uraw = sb("uraw", [128, 4, 2 + NT])
    uval = sb("uval", [128, 4, NT])
    uacc = sb("uacc", [128, 2, NT])
    uhist = sb("uhist", [128, NFT, 2])
    hTh = sb("hTh", [128, 8, 2], BF16)
    sq = sb("sq", [128, 2, NT])
    lnst = sb("lnst", [128, 3, NT])
    small = sb("small", [128, 160])
    uvf = uval[:].rearrange("p a n -> p (a n)")

    psf = [es.enter_context(nc.psum_tensor("psf%d" % i, [128, 512], F32)) for i in range(6)]
    psb = [es.enter_context(nc.psum_tensor("psb%d" % i, [128, 1024], BF16)) for i in range(2)]
    st = {"f": 0, "b": 0, "w": 0, "ev": 0, "dg": 0, "ur": 0, "gb": 0}

    st["fset"] = [0, 1, 2, 3, 4, 5]

    def bankf():
        fs = st["fset"]
        i = fs[st["f"] % len(fs)]
        st["f"] += 1
        return psf[i], ("psf", i)

    def bankb():
        i = st["b"] % 2
        st["b"] += 1
        return psb[i], ("psb", i)

    v_ap = pbig[:, 0:4096].rearrange("p (c e) -> p c e", c=CG)
    sgT = pbig[:, 4096:8192].rearrange("p (t n) -> p t n", t=16)
    gT = pbig[:, 0:22 * NT].rearrange("p (t n) -> p t n", t=22)
    stage32 = pbig[:].bitcast(F32)

    xi_t = tabs[:, 0:8]
    zeta_t = tabs[:, 8:16]
    zneg_t = tabs[:, 16:24]
    wtA = tabs[:, 24:152].rearrange("p (c h) -> p c h", c=16)
    hm = meta_t[:, 1:2]

    cst = []

    def cload(dst, src, q="sp"):
        S.dma(q, "cst", lambda e, d=dst, s=src: e.dma_start(out=d, in_=s), writes=["const"])

    cload(meta_t[:], meta[:, :])
    cload(cf_t[:], cf_d[:, :])
    cload(tabs[:], tabs_d[:, :])
    cload(caus[:], caus_d[:, :])
    cload(binfm[:], bin_fm[:, :])
    cload(cwfm[:], cw_fm[:, :])
    cload(cvec[:], cvec_fm[:, :])
    cload(gnfm[:], gn_fm[:, :])
    cload(fwfm[:], fw_fm[:, :])
    cload(fbfm[:], fb_fm[:, :])
    for r in range(4):
        cload(ln_bc[:, r, :], ln_rows[r:r + 1, :].partition_broadcast(128))
    cload(gate_bc[:, 0, :], bada_row[0:1, 2 * D:3 * D].partition_broadcast(128))
    cload(gate_bc[:, 1, :], bada_row[0:1, 5 * D:6 * D].partition_broadcast(128))
    idf = pt[:, 0:128]
    ohf = pt[0:12, 128:128 + 1024]
    bif = uvf[0:12, 0:512]
    posf = uvf[:, 512:512 + NCH]
    posi = sb("posi", [128, NCH], I32)
    invf = uvf[:, 576:640]
    cfm = uvf[:, 640:648]
    badf = uvf[:, 648:696]
    cload(idf, idn_d[:, :])
    cload(ohf, oneh_d[:, :])
    cload(bif, bin_tm[:, :])
    cload(posi[:], pos_i[:, :])
    cload(invf, invf_d[:, :])
    cload(cfm, c_fm[:, :])
    cload(badf, bada_fm[:, :])
    S.barrier()
    blocks = []

    uids = {}

    def add_block(w, r0, kc, c0, ncols):
        key = (w.tensor.name, r0, c0)
        if key not in uids:
            uids[key] = len(uids)
        blocks.append((w[r0:r0 + kc * 128, c0:c0 + ncols], kc, ncols, uids[key]))

    def add_dg_block(ct):
        blocks.append((None, 0, 0, ("dg", ct)))

    if mode in ("F", "A"):
        for nb in range(2, 8):
            add_block(w_in, 0, 8, nb * 512, 512)
    groups = [[0]] + [[1 + 2 * g, 2 + 2 * g] for g in range(8)]
    for g in (groups if mode in ("F", "B") else []):
        for nb in TM_ORDER + list(range(8, 16)):
            add_block(w_in, 0, 8, nb * 512, 512)
        for ct in range(8):
            add_dg_block(ct)
        for nb in range(16, 18):
            add_block(w_in, 0, 8, nb * 512, 512)
        for nb in range(2):
            add_block(w_conv_out, 0, 8, nb * 512, 512)
        for nb in range(18, 20):
            add_block(w_in, 0, 8, nb * 512, 512)
        for nb in range(2):
            for kb in range(2):
                add_block(w_ret_out, kb * 1024, 8, nb * 512, 512)
        for nb in range(2):
            add_block(w_out, 0, 8, nb * 512, 512)
        for p in (range(6) if g != [0] else []):
            nv = 512 if p < 5 else 256
            add_block(w_up, 0, 8, p * 512, nv)
            add_block(w_up, 0, 8, DFF + p * 512, nv)
        if g != [0]:
            for nb in range(2):
                add_block(w_down, 0, 8, nb * 512, 512)
                add_block(w_down, 1024, 8, nb * 512, 512)
                add_block(w_down, 2048, 6, nb * 512, 512)
    wst_ = {"loaded": 0, "next": 0}

    def wload_next():
        i = wst_["loaded"]
        if i >= len(blocks):
            return
        src, kc, ncols, uid = blocks[i]
        slot = i % NSLOT
        if src is None:
            ct = uid[1]
            S.dma("sp", "w%d" % slot, lambda e, ct=ct, slot=slot: e.dma_start(
                out=wbuf[:, slot].rearrange("p k n -> p (k n)")[:, 0:CK * 128], in_=dgsc[ct * 128:(ct + 1) * 128, 0:CK * 128]),
                reads=["dgsc"], writes=[("w", slot)])
            wst_["loaded"] += 1
            return
        if wst_.get("wsc") is None:
            nuse = {}
            for b in blocks:
                if b[0] is not None:
                    nuse[b[3]] = nuse.get(b[3], 0) + 1
            wst_["nuse"] = nuse
            wst_["cast"] = set()
            wst_["wsc"] = nc.dram_tensor("wsc", [max(1, len(uids)) * 128, 4096], BF16).ap()
        wsc = wst_["wsc"]
        scr = wsc[uid * 128:(uid + 1) * 128, :].rearrange("p (k n) -> p k n", k=8)[:, 0:kc, 0:ncols]
        if uid not in wst_["cast"]:
            wst_["cast"].add(uid)
            S.dma("pool", "w%d" % slot, lambda e, src=src, kc=kc, ncols=ncols, slot=slot: e.dma_start(
                out=wbuf[:, slot, 0:kc, 0:ncols], in_=src.rearrange("(k p) n -> p k n", p=128)),
                writes=[("w", slot)])
            if wst_["nuse"][uid] > 1:
                def wb(scr=scr, kc=kc, ncols=ncols, slot=slot, uid=uid):
                    S.dma("sp", "wb%d" % slot, lambda e: e.dma_start(
                        out=scr, in_=wbuf[:, slot, 0:kc, 0:ncols]), reads=[("w", slot)], writes=[("wsc", uid)])
                if wst_.get("defer") is not None:
                    wst_["defer"].append(wb)
                else:
                    wb()
        else:
            S.dma("sp", "w%d" % slot, lambda e, scr=scr, kc=kc, ncols=ncols, slot=slot: e.dma_start(
                out=wbuf[:, slot, 0:kc, 0:ncols], in_=scr), reads=[("wsc", uid)], writes=[("w", slot)])
        wst_["loaded"] += 1

    def wacquire():
        i = wst_["next"]
        wst_["next"] += 1
        assert i < wst_["loaded"], "weight block not prefetched"
        slot = i % NSLOT
        return wbuf[:, slot], ("w", slot)

    wst_["defer"] = []
    for _ in range(NSLOT):
        wload_next()
    deferred_wb = wst_["defer"]
    wst_["defer"] = None

    CK_ = ["const"]
    S.op("dve", lambda e: e.tensor_copy(out=identb[:], in_=idf), reads=CK_, writes=["identb"])
    S.op("dve", lambda e: e.tensor_copy(out=onehb[:], in_=ohf), reads=CK_, writes=["onehb"])
    S.op("dve", lambda e: e.tensor_copy(out=bintm[:], in_=bif), reads=CK_, writes=["bintm"])
    S.op("dve", lambda e: e.tensor_copy(out=posf, in_=posi[:]), reads=CK_, writes=["posf"])
    S.op("pool", lambda e: e.memset(ones32[:], 1.0), writes=["ones32"])
    dgsc = nc.dram_tensor("dgsc", [8 * 128, 4096], BF16).ap()
    if mode in ("F", "B"):
        dstg = [yT[:].rearrange("p a n -> p (a n)").bitcast(BF16), rT[:].rearrange("p a n -> p (a n)")]
        for ct in range(8):
            stg = dstg[ct % 2][:, 0:CK * 128]
            sk = ("dgst", ct % 2)
            S.op("dve", lambda e, ct=ct, stg=stg: e.tensor_tensor(
                out=stg.rearrange("p (t m) -> p t m", t=CK),
                in0=identb[:].unsqueeze(1).to_broadcast([128, CK, 128]),
                in1=cwfm[:, ct * CK:(ct + 1) * CK].unsqueeze(2).to_broadcast([128, CK, 128]), op=ALU.mult),
                reads=["identb", "const"], writes=[sk])
            S.dma("sp", "dgw%d" % (ct % 2), lambda e, ct=ct, stg=stg: e.dma_start(
                out=dgsc[ct * 128:(ct + 1) * 128, 0:CK * 128], in_=stg), reads=[sk], writes=["dgsc"])
        S.barrier()
    NA = NCH * 64
    ang = stage32[:, 0:NA].rearrange("p (c j) -> p c j", c=NCH)
    kf = stage32[:, NA:2 * NA].rearrange("p (c j) -> p c j", c=NCH)
    ki = pbig[:].bitcast(I32)[:, 2 * NA:3 * NA].rearrange("p (c j) -> p c j", c=NCH)
    msk = yT[:].rearrange("p a n -> p (a n)")[:, 0:NA].rearrange("p (c j) -> p c j", c=NCH)
    S.op("dve", lambda e: e.tensor_tensor(out=ang, in0=posf.unsqueeze(2).to_broadcast([128, NCH, 64]),
                                          in1=invf.unsqueeze(1).to_broadcast([128, NCH, 64]), op=ALU.mult),
         reads=["posf", "const"], writes=["ang"])
    S.op("dve", lambda e: e.tensor_scalar(out=kf, in0=ang, scalar1=1.0 / TWO_PI, scalar2=None, op0=ALU.mult),
         reads=["ang"], writes=["kf"])
    S.op("dve", lambda e: e.tensor_copy(out=ki, in_=kf), reads=["kf"], writes=["ki"])
    S.op("dve", lambda e: e.tensor_copy(out=kf, in_=ki), reads=["ki"], writes=["kf"])
    S.op("dve", lambda e: e.scalar_tensor_tensor(out=ang, in0=kf, scalar=-C1, in1=ang, op0=ALU.mult, op1=ALU.add),
         reads=["kf", "ang"], writes=["ang"])
    S.op("dve", lambda e: e.scalar_tensor_tensor(out=ang, in0=kf, scalar=-C2, in1=ang, op0=ALU.mult, op1=ALU.add),
         reads=["kf", "ang"], writes=["ang"])

    def wrap(t):
        S.op("dve", lambda e: e.tensor_single_scalar(out=msk, in_=t, scalar=math.pi, op=ALU.is_gt),
             reads=["ang"], writes=["msk"])
        S.op("dve", lambda e: e.scalar_tensor_tensor(out=t, in0=msk, scalar=-TWO_PI, in1=t, op0=ALU.mult, op1=ALU.add),
             reads=["msk", "ang"], writes=["ang"])
        S.op("dve", lambda e: e.tensor_single_scalar(out=msk, in_=t, scalar=-math.pi, op=ALU.is_lt),
             reads=["ang"], writes=["msk"])
        S.op("dve", lambda e: e.scalar_tensor_tensor(out=t, in0=msk, scalar=TWO_PI, in1=t, op0=ALU.mult, op1=ALU.add),
             reads=["msk", "ang"], writes=["ang"])
        S.op("dve", lambda e: e.tensor_scalar(out=t, in0=t, scalar1=math.pi, scalar2=-math.pi, op0=ALU.min, op1=ALU.max),
             reads=["ang"], writes=["ang"])

    wrap(ang)
    S.op("act", lambda e: e.activation(out=sintab[:], in_=ang, func=AF.Sin), reads=["ang"], writes=["sintab"])
    S.op("dve", lambda e: e.tensor_scalar(out=ang, in0=ang, scalar1=math.pi / 2, scalar2=None, op0=ALU.add),
         reads=["ang", "sintab"], writes=["ang"])
    wrap(ang)
    S.op("act", lambda e: e.activation(out=costab[:], in_=ang, func=AF.Sin), reads=["ang"], writes=["costab"])

    silc = small[:, 0:8]
    S.op("act", lambda e: e.activation(out=silc, in_=cfm, func=AF.Silu), reads=CK_, writes=["silc"])
    S.barrier()
    lbc = aT[:].rearrange("p a n -> p (a n)").bitcast(F32)[:, 0:1024].rearrange("p (k m) -> p k m", k=8)
    S.op("dve", lambda e: e.tensor_copy(out=lbc, in_=silc.unsqueeze(2).to_broadcast([128, 8, 128])),
         reads=["silc"], writes=["lbc"])
    wsts = [stage32.rearrange("p (k n) -> p k n", k=8),
            xs2[:].rearrange("p a c d -> p (a c d)").rearrange("p (k n) -> p k n", k=8)]
    for nb in (range(4) if mode == "A" else range(12)):
        wst = wsts[nb % 2]
        wkey = ("wst", nb % 2)
        S.dma("sp", "wst%d" % (nb % 2), lambda e, nb=nb, wst=wst: e.dma_start(
            out=wst, in_=w_ada[:, nb * 512:(nb + 1) * 512].rearrange("(k p) n -> p k n", p=128)),
            writes=[wkey])
        sec = nb // 2
        ps, pk = bankf()
        for k in range(8):
            S.op("pe", lambda e, k=k, ps=ps, wst=wst: e.matmul(ps[:], lbc[:, k, :], wst[:, k, :], start=(k == 0), stop=(k == 7)),
                 reads=["lbc", wkey], writes=[pk])
        if sec in (2, 5):
            gi = 0 if sec == 2 else 1
            dst = gate_bc[:, gi, (nb % 2) * 512:(nb % 2 + 1) * 512]
            S.op("dve", lambda e, ps=ps, dst=dst: e.tensor_tensor(out=dst, in0=ps[:], in1=dst, op=ALU.add),
                 reads=[pk, "const"], writes=["gate_bc"])
        else:
            gt = nb * 4
            col = {0: 0, 1: 8, 3: 16, 4: 24}[sec] + (nb % 2) * 4
            plus1 = 1.0 if sec in (1, 4) else 0.0
            for et in range(4):
                tmp = sq[:, et % 2, 0:128]
                S.op("dve", lambda e, ps=ps, et=et, tmp=tmp: e.tensor_tensor(
                    out=tmp, in0=ps[:, et * 128:(et + 1) * 128], in1=identb[:], op=ALU.mult),
                    reads=[pk, "identb"], writes=[("sq", et % 2)])
                S.op("dve", lambda e, et=et, tmp=tmp, col=col: e.reduce_sum(
                    out=modfm[:, col + et:col + et + 1], in_=tmp, axis=AX.X),
                    reads=[("sq", et % 2)], writes=["modfm"])
            S.op("dve", lambda e, col=col, gt=gt, plus1=plus1: e.scalar_tensor_tensor(
                out=modfm[:, col:col + 4], in0=modfm[:, col:col + 4], scalar=plus1, in1=badf[:, gt:gt + 4],
                op0=ALU.add, op1=ALU.add), reads=["modfm", "const"], writes=["modfm"])
    S.barrier()
    for wb_ in deferred_wb:
        wb_()
    S.op("pool", lambda e: e.memset(uhist[:], 0.0), writes=[("uhist", ft) for ft in range(NFT)])
    S.op("pool", lambda e: e.memset(aT[:], 0.0), writes=[("aT", ct) for ct in range(8)])

    def _kl(k):
        return list(k) if isinstance(k, list) else [k]

    PT_ALL = [("ptA", i) for i in range(4)] + ["ro_a", "ro_b", "ptC"]

    def run(g):
        for _ in g:
            pass

    def interleave(*gens):
        gens = list(gens)
        while gens:
            for g in list(gens):
                try:
                    next(g)
                except StopIteration:
                    gens.remove(g)

    def g_ln_stats(src, srck, par, res):
        o = {0: 16, 1: 112, 2: 128, 3: 144}[par]
        p = str(par)
        stt = small[:, o:o + 12]
        S.op("dve", lambda e: e.bn_stats(out=stt[:, 0:6], in_=src[:, 0:512]), reads=_kl(srck), writes=["lnstat" + p])
        S.op("dve", lambda e: e.bn_stats(out=stt[:, 6:12], in_=src[:, 512:1024]), reads=_kl(srck), writes=["lnstat2" + p])
        yield
        mv = small[:, o + 12:o + 14]
        S.op("dve", lambda e: e.bn_aggr(out=mv, in_=stt), reads=["lnstat" + p, "lnstat2" + p], writes=["mv" + p])
        rs = small[:, o + 14:o + 15]
        nmr = small[:, o + 15:o + 16]
        S.op("dve", lambda e: e.tensor_scalar(out=rs, in0=mv[:, 1:2], scalar1=EPS, scalar2=None, op0=ALU.add),
             reads=["mv" + p], writes=["rs" + p])
        yield
        S.op("act", lambda e: e.activation(out=rs, in_=rs, func=AF.Sqrt), reads=["rs" + p], writes=["rs" + p])
        yield
        S.op("dve", lambda e: e.reciprocal(out=rs, in_=rs), reads=["rs" + p], writes=["rs" + p])
        S.op("dve", lambda e: e.scalar_tensor_tensor(out=nmr, in0=mv[:, 0:1], scalar=-1.0, in1=rs,
                                                     op0=ALU.mult, op1=ALU.mult), reads=["mv" + p, "rs" + p], writes=["nmr" + p])
        res["rs"] = rs
        res["nmr"] = nmr
        res["keys"] = ["rs" + p, "nmr" + p]
        yield

    def transposes_to(src_bf, srck, n_tiles, evac):
        t0 = 0
        while t0 < n_tiles:
            nt = min(8, n_tiles - t0)
            pb, pk = bankb()
            for t in range(nt):
                S.op("pe", lambda e, t=t, t0=t0, pb=pb: e.transpose(
                    pb[:, t * 128:(t + 1) * 128], src_bf[:, (t0 + t) * 128:(t0 + t + 1) * 128], identb[:]),
                    reads=_kl(srck) + ["identb"], writes=[pk])
            evac(pb, pk, t0, nt)
            t0 += nt

    def g_ln1_and_hT(ci, col0, mod_off, par=0, spar=None, xb=None):
        if xb is None:
            xb = cur["xb"]
        src = xs2[:, xb, ci, :]
        xk = ("xs", xb, ci)
        res = {}
        yield from g_ln_stats(src, xk, par if spar is None else spar, res)
        hbp = rnb[:, par * 1024:(par + 1) * 1024]
        hk = ["rnbA"] if par == 0 else ["rnbB0", "rnbB1"]
        S.op("act", lambda e: e.activation(out=hbp, in_=src, func=AF.Identity, bias=res["nmr"], scale=res["rs"]),
             reads=[xk] + res["keys"], writes=hk)
        yield
        pb, pk = bankb()
        for t in range(8):
            S.op("pe", lambda e, t=t: e.transpose(pb[:, t * 128:(t + 1) * 128], hbp[:, t * 128:(t + 1) * 128], identb[:]),
                 reads=hk + ["identb"], writes=[pk])
        yield
        for k in range(8):
            S.op("act", lambda e, k=k: e.activation(
                out=hT[:, k, col0:col0 + 128], in_=pb[:, k * 128:(k + 1) * 128], func=AF.Identity,
                bias=modfm[:, mod_off + k:mod_off + k + 1], scale=modfm[:, mod_off + 8 + k:mod_off + 9 + k]),
                reads=[pk, "modfm"], writes=[("hT", ci)])
            if k % 4 == 3:
                yield

    def ln1_and_hT(ci, col0, mod_off):
        run(g_ln1_and_hT(ci, col0, mod_off, 0))

    def g_post_ln(z, zk, ci, par, grow, brow):
        xsrc = XS(ci)
        xk = XK(ci)
        S.op("dve", lambda e: e.scalar_tensor_tensor(out=z, in0=xsrc, scalar=ALPHA, in1=z,
                                                     op0=ALU.mult, op1=ALU.add),
             reads=[xk] + zk, writes=zk)
        yield
        res = {}
        yield from g_ln_stats(z, zk, par, res)
        S.op("act", lambda e: e.activation(out=z, in_=z, func=AF.Identity, bias=res["nmr"], scale=res["rs"]),
             reads=zk + res["keys"], writes=zk)
        yield
        S.op("dve", lambda e: e.tensor_tensor(out=z, in0=z, in1=ln_bc[:, grow, :], op=ALU.mult),
             reads=zk + ["const"], writes=zk)
        yield
        S.op("pool", lambda e: e.tensor_tensor(out=xsrc, in0=z, in1=ln_bc[:, brow, :], op=ALU.add),
             reads=zk + ["const"], writes=[xk])
        yield

    def proj_tm(ci, col0, wslot, wk, nb_bias):
        ps, pk = bankf()
        for k in range(8):
            S.op("pe", lambda e, k=k, ps=ps: e.matmul(ps[:], hT[:, k, col0:col0 + 128], wslot[:, k, :],
                                                      start=(k == 0), stop=False),
                 reads=[("hT", 0), ("hT", 1), wk], writes=[pk])
        S.op("pe", lambda e, ps=ps: e.matmul(ps[:], onehb[:, nb_bias * 128:(nb_bias + 1) * 128], bintm[:, :],
                                             start=False, stop=True),
             reads=["onehb", "bintm"], writes=[pk])
        return ps, pk

    def rotary(ps, pk, chunk, kind, blk, ci):
        pv = ps[:].rearrange("p (h t j) -> p h t j", h=4, t=2)
        x1 = pv[:, :, 0, :]
        x2 = pv[:, :, 1, :]
        cb = costab[:, chunk, :].unsqueeze(1).to_broadcast([128, 4, 64])
        sn = sintab[:, chunk, :].unsqueeze(1).to_broadcast([128, 4, 64])
        t = [rt32[:, i, :].rearrange("p (h j) -> p h j", h=4) for i in range(4)]
        rd = [pk, "costab", "sintab"]
        S.op("dve", lambda e: e.tensor_tensor(out=t[0], in0=x1, in1=cb, op=ALU.mult), reads=rd, writes=[("ptA", 0)])
        S.op("dve", lambda e: e.tensor_tensor(out=t[1], in0=x2, in1=sn, op=ALU.mult), reads=rd, writes=[("ptA", 1)])
        S.op("dve", lambda e: e.tensor_tensor(out=t[2], in0=x2, in1=cb, op=ALU.mult), reads=rd, writes=[("ptA", 2)])
        S.op("dve", lambda e: e.tensor_tensor(out=t[3], in0=x1, in1=sn, op=ALU.mult), reads=rd, writes=[("ptA", 3)])
        ov = ro32.rearrange("p (h t j) -> p h t j", h=4, t=2)
        S.op("pool", lambda e: e.tensor_tensor(out=ov[:, :, 0, :], in0=t[0], in1=t[1], op=ALU.subtract),
             reads=[("ptA", 0), ("ptA", 1)], writes=["ro_a"])
        S.op("pool", lambda e: e.tensor_tensor(out=ov[:, :, 1, :], in0=t[2], in1=t[3], op=ALU.add),
             reads=[("ptA", 2), ("ptA", 3)], writes=["ro_b"])
        o3 = ro32.rearrange("p (h d) -> p h d", h=4)
        hs = slice(blk * 4, blk * 4 + 4)
        if kind == "kA":
            return None
        dst = rotb[:, ci, :].rearrange("p (h d) -> p h d", h=4)
        tab = xi_t if kind == "q" else zneg_t
        S.op("dve", lambda e: e.tensor_tensor(out=dst, in0=o3, in1=tab[:, hs].unsqueeze(2).to_broadcast([128, 4, 128]),
                                              op=ALU.mult), reads=["ro_a", "ro_b", "const"], writes=["rnbB%d" % ci])
        return None

    def kzeta_from_ro(dst, dstk, blk, table, extra_hm):
        o3 = ro32.rearrange("p (h d) -> p h d", h=4)
        hs = slice(blk * 4, blk * 4 + 4)
        d3 = dst.rearrange("p (h d) -> p h d", h=4)
        S.op("pool", lambda e: e.tensor_tensor(out=d3, in0=o3, in1=table[:, hs].unsqueeze(2).to_broadcast([128, 4, 128]),
                                               op=ALU.mult), reads=["ro_a", "ro_b", "const"], writes=[dstk])
        if extra_hm:
            S.op("pool", lambda e: e.tensor_scalar(out=dst, in0=dst, scalar1=hm, scalar2=0.0, op0=ALU.mult, op1=ALU.add),
                 reads=[dstk, "const"], writes=[dstk])

    if mode in ("F", "A"):
        kw = [wacquire() for _ in range(2)]
        vw = [wacquire() for _ in range(4)]
        vz = pbig[:, 0:2048]
        st["f"] = 0
        kvb = [bankf() for _ in range(4)]
        st["f"] = 4

        def bankA():
            i = 4 + (st["f"] % 2)
            st["f"] += 1
            return psf[i], ("psf", i)

        def kv_accum(c_own, first, last):
            for h in range(H):
                ps, pk = kvb[h // 2]
                S.op("pe", lambda e, h=h, ps=ps: e.matmul(
                    ps[:, (h % 2) * 256:(h % 2 + 1) * 256], kze[:, 0, h * 128:(h + 1) * 128], vz[:, h * 256:(h + 1) * 256],
                    start=first, stop=last), reads=["kzeA", "vz"], writes=[pk])

        def g_pre(c_own):
            chunk = c_own + 1
            par = c_own % 2
            S.dma("sp", "xld%d" % par, lambda e: e.dma_start(out=xs2[:, 0, par, :], in_=xh[chunk * 128:(chunk + 1) * 128, :]),
                  writes=[("xs", 0, par)])
            yield
            yield from g_ln1_and_hT(par, par * 128, 0, par)

        def g_main(c_own):
            chunk = c_own + 1
            par = c_own % 2
            col = par * 128
            for blk in range(2):
                ps, pk = psf[4 + (st["f"] % 2)], ("psf", 4 + (st["f"] % 2))
                st["f"] += 1
                for k in range(8):
                    S.op("pe", lambda e, k=k, ps=ps, blk=blk: e.matmul(ps[:], hT[:, k, col:col + 128], kw[blk][0][:, k, :],
                                                                       start=(k == 0), stop=False),
                         reads=[("hT", par), kw[blk][1]], writes=[pk])
                S.op("pe", lambda e, ps=ps, blk=blk: e.matmul(ps[:], onehb[:, (2 + blk) * 128:(3 + blk) * 128], bintm[:, :],
                                                              start=False, stop=True),
                     reads=["onehb", "bintm"], writes=[pk])
                yield
                rotary(ps, pk, chunk, "kA", blk, 0)
                S.op("act", lambda e, blk=blk: e.activation(out=kze[:, 0, blk * 512:(blk + 1) * 512], in_=ro32, func=AF.Copy),
                     reads=["ro_a", "ro_b"], writes=["kzeA"])
                yield
            for blk in range(4):
                ps, pk = psf[4 + (st["f"] % 2)], ("psf", 4 + (st["f"] % 2))
                st["f"] += 1
                for k in range(8):
                    S.op("pe", lambda e, k=k, ps=ps, blk=blk: e.matmul(ps[:], hT[:, k, col:col + 128], vw[blk][0][:, k, :],
                                                                       start=(k == 0), stop=False),
                         reads=[("hT", par), vw[blk][1]], writes=[pk])
                S.op("pe", lambda e, ps=ps, blk=blk: e.matmul(ps[:], onehb[:, (4 + blk) * 128:(5 + blk) * 128], bintm[:, :],
                                                              start=False, stop=True),
                     reads=["onehb", "bintm"], writes=[pk])
                for hh in range(2):
                    h = blk * 2 + hh
                    S.op("act", lambda e, ps=ps, hh=hh, h=h: e.activation(
                        out=vz[:, h * 256:(h + 1) * 256], in_=ps[:, hh * 256:(hh + 1) * 256], func=AF.Copy,
                        scale=wtA[:, c_own, h:h + 1]), reads=[pk, "const"], writes=["vz"])
                yield

        run(g_pre(0))
        for c_own in range(16):
            gl = [g_main(c_own)]
            if c_own + 1 < 16:
                gl.append(g_pre(c_own + 1))
            interleave(*gl)
            if c_own < 15:
                kv_accum(c_own, c_own == 0, c_own == 14)
            else:
                Bst = stage32[:, 2048:4096].rearrange("p (h e) -> p h e", h=H)
                for b in range(4):
                    ps, pk = kvb[b]
                    S.op("dve", lambda e, ps=ps, b=b: e.tensor_copy(
                        out=Bst[:, 2 * b:2 * b + 2, :], in_=ps[:].rearrange("p (h e) -> p h e", h=2)),
                        reads=[pk], writes=["Bst"])
                kv_accum(c_own, True, True)
                Ast = pt[:].rearrange("p (h e) -> p h e", h=H)
                for h in range(H):
                    ps, pk = kvb[h // 2]
                    S.op("dve", lambda e, h=h, ps=ps: e.scalar_tensor_tensor(
                        out=Ast[:, h, :], in0=Bst[:, h, :], scalar=DEC128[h], in1=ps[:, (h % 2) * 256:(h % 2 + 1) * 256],
                        op0=ALU.mult, op1=ALU.add), reads=[pk, "Bst"], writes=["Ast"] + PT_ALL)
                S.dma("sp", "bnc", lambda e: e.dma_start(
                    out=bounce[0:1024, :].rearrange("(h d) e -> d h e", d=128), in_=Ast), reads=["Ast"], writes=["bounce"])
                S.dma("sp", "bnc", lambda e: e.dma_start(
                    out=bounce[1024:2048, :].rearrange("(h d) e -> d h e", d=128), in_=Bst), reads=["Bst"], writes=["bounce"])
        for _ in range(6):
            wload_next()
        st["f"] = 0
    S.barrier()
    if mode == "A":
        es.close()
        return nc
    if mode == "F":
        cc_sem = es.enter_context(nc.semaphore("cc_sem"))
        nc.gpsimd.collective_compute("AllGather", ALU.bypass, replica_groups=[list(range(NCORES))],
                                     ins=[bounce_t.ap().opt()], outs=[gath_t.ap().opt()]).then_inc(cc_sem)
        nc.gpsimd.wait_ge(cc_sem, 1)
    RF_ALL = [("Rf", h_) for h_ in range(H)]
    S.op("pool", lambda e: e.memset(Rf[:], 0.0), writes=RF_ALL)
    S.barrier()
    gbuf = [stage32[:, 0:2048], stage32[:, 2048:4096], yT[:].rearrange("p a n -> p (a n)"),
            rT[:].rearrange("p a n -> p (a n)").bitcast(F32)]
    gbuf = [g_.rearrange("p (h e) -> p h e", h=H) for g_ in gbuf]
    for j in range(NCORES):
        for part in range(2):
            bi = (2 * j + part) % 4
            gk = ("gst", bi)
            r0_ = j * 2048 + part * 1024
            S.dma("sp", "gld%d" % bi, lambda e, r0_=r0_, bi=bi: e.dma_start(
                out=gbuf[bi], in_=gath[r0_:r0_ + 1024, :].rearrange("(h d) e -> d h e", d=128)),
                reads=["gath"], writes=[gk])
            for h in range(H):
                sc = cf_t[:, j * 8 + h:j * 8 + h + 1] if part == 0 else cf_t[:, 64 + j:65 + j]
                S.op("dve", lambda e, bi=bi, h=h, sc=sc: e.scalar_tensor_tensor(
                    out=Rf[:, h, :], in0=gbuf[bi][:, h, :], scalar=sc, in1=Rf[:, h, :],
                    op0=ALU.mult, op1=ALU.add), reads=[gk, ("Rf", h), "const"], writes=[("Rf", h)])
    S.op("act", lambda e: e.activation(out=Rb[:], in_=Rf[:], func=AF.Copy), reads=RF_ALL, writes=["Rb"])
    S.barrier()

    for gi, g in enumerate(groups):
        C = len(g)
        nt = C * 128
        halo = (gi == 0)
        c0 = g[0]
        cur["xb"] = gi % 2

        def xload(gj):
            gg = groups[gj]
            xb = gj % 2
            S.dma("sp", "xld%d" % xb, lambda e: e.dma_start(
                out=xs2[:, xb, 0:len(gg), :],
                in_=xh[gg[0] * 128:(gg[0] + len(gg)) * 128, :].rearrange("(c p) d -> p c d", p=128)),
                writes=[("xs", xb, i) for i in range(CG)])
        if gi == 0:
            xload(0)
            interleave(*[g_ln1_and_hT(ci, ci * 128, 0, ci) for ci in range(C)])
        if gi + 1 < len(groups):
            xload(gi + 1)

        def s0_next():
            if gi + 1 >= len(groups):
                return []
            return [g_ln1_and_hT(ci, ci * 128, 0, ci, spar=2 + ci, xb=(gi + 1) % 2) for ci in range(len(groups[gi + 1]))]
        pend_tr = []
        for nb in TM_ORDER:
            wslot, wk = wacquire()
            for ci in range(C):
                chunk = g[ci]
                ps, pk = proj_tm(ci, ci * 128, wslot, wk, nb)
                if nb < 2:
                    rotary(ps, pk, chunk, "q", nb, ci)

                    def evq(pb, pk2, t0, ntl, nb=nb, ci=ci):
                        S.op("act", lambda e: e.activation(
                            out=qxT[:, nb * 4:nb * 4 + 4, ci * 128:(ci + 1) * 128],
                            in_=pb[:, 0:512].rearrange("p (k t) -> p k t", k=4), func=AF.Copy),
                            reads=[pk2], writes=["qxT"])
                    pend_tr.append(lambda ci=ci, evq=evq: transposes_to(rotb[:, ci, :], "rnbB%d" % ci, 4, evq))
                elif nb < 4:
                    blk = nb - 2
                    rotary(ps, pk, chunk, "k", blk, ci)
                    kzeta_from_ro(kze[:, ci, blk * 512:(blk + 1) * 512], ("kze", ci), blk, zeta_t, halo)

                    def evk(pb, pk2, t0, ntl, blk=blk, ci=ci):
                        S.op("act", lambda e: e.activation(
                            out=knT[:, blk * 4:blk * 4 + 4, ci * 128:(ci + 1) * 128],
                            in_=pb[:, 0:512].rearrange("p (k t) -> p k t", k=4), func=AF.Copy),
                            reads=[pk2], writes=["knT"])
                    pend_tr.append(lambda ci=ci, evk=evk: transposes_to(rotb[:, ci, :], "rnbB%d" % ci, 4, evk))
                else:
                    blk = nb - 4
                    S.op("act", lambda e, ps=ps, blk=blk, ci=ci: e.activation(
                        out=v_ap[:, ci, blk * 512:(blk + 1) * 512], in_=ps[:], func=AF.Copy),
                        reads=[pk], writes=[("v", ci)])
            if nb >= 4:
                while pend_tr:
                    pend_tr.pop(0)()
            wload_next()
        while pend_tr:
            pend_tr.pop(0)()
        def fm_blocks(nbs):
            for nb in nbs:
                wslot, wk = wacquire()
                for et in range(4):
                    tile = (nb - 8) * 4 + et
                    ps, pk = bankf()
                    for k in range(8):
                        S.op("pe", lambda e, k=k, ps=ps, et=et, wslot=wslot: e.matmul(
                            ps[:, 0:nt], wslot[:, k, et * 128:(et + 1) * 128], hT[:, k, 0:nt],
                            start=(k == 0), stop=(k == 7)), reads=[("hT", 0), ("hT", 1), wk], writes=[pk])
                    bcol = binfm[:, tile:tile + 1]
                    if tile < 16:
                        S.op("act", lambda e, ps=ps, tile=tile, bcol=bcol: e.activation(
                            out=sgT[:, tile, 0:nt], in_=ps[:, 0:nt], func=AF.Silu, bias=bcol),
                            reads=[pk, "const"], writes=["sgT"])
                    elif tile < 24:
                        ct = tile - 16
                        S.op("act", lambda e, ps=ps, ct=ct, bcol=bcol: e.activation(
                            out=yT[:, ct, 0:nt], in_=ps[:, 0:nt], func=AF.Identity, bias=bcol),
                            reads=[pk, "const"], writes=[("yT", ct)])
                    elif tile < 32:
                        ct = tile - 24
                        S.op("act", lambda e, ps=ps, ct=ct, bcol=bcol: e.activation(
                            out=sq[:, ct % 2, 0:nt], in_=ps[:, 0:nt], func=AF.Sigmoid, bias=bcol),
                            reads=[pk, "const"], writes=[("sq", ct % 2)])
                        S.op("dve", lambda e, ct=ct: e.tensor_tensor(
                            out=aT[:, ct, 30:30 + nt], in0=yT[:, ct, 0:nt], in1=sq[:, ct % 2, 0:nt], op=ALU.mult),
                            reads=[("yT", ct), ("sq", ct % 2)], writes=[("aT", ct)])
                        if halo:
                            S.op("dve", lambda e, ct=ct: e.tensor_scalar(
                                out=aT[:, ct, 30:30 + nt], in0=aT[:, ct, 30:30 + nt], scalar1=hm, scalar2=None,
                                op0=ALU.mult), reads=[("aT", ct), "const"], writes=[("aT", ct)])
                    else:
                        dt_ = (tile - 32) % 8
                        S.op("act", lambda e, ps=ps, dt_=dt_, bcol=bcol: e.activation(
                            out=sig[:, dt_, 0:nt], in_=ps[:, 0:nt], func=AF.Sigmoid, bias=bcol),
                            reads=[pk, "const"], writes=[("sig", dt_)])
                wload_next()

        fm_blocks(range(8, 16))
        def g_conv():
            ps_st, pkst = psf[5], ("psf", 5)
            ps_s = ps_st[:, 0:nt]
            ps_q = ps_st[:, 256:256 + nt]
            pks = pkst
            pkq = pkst
            for ct in range(8):
                ps, pk = psf[3 + ct % 2], ("psf", 3 + ct % 2)
                wslot, wk = wacquire()
                dgv = wslot.rearrange("p k n -> p (k n)")[:, 0:CK * 128].rearrange("p (t m) -> p t m", t=CK)
                for k in range(CK):
                    S.op("pe", lambda e, k=k, ct=ct, ps=ps, dgv=dgv: e.matmul(
                        ps[:, 0:nt], dgv[:, k, :], aT[:, ct, k:k + nt], start=(k == 0), stop=(k == CK - 1)),
                        reads=[wk, ("aT", ct)], writes=[pk])
                    if k % 8 == 7:
                        yield
                wload_next()
                acc = yT[:, ct, 0:nt]
                S.op("act", lambda e, ct=ct, acc=acc, ps=ps: e.activation(
                    out=acc, in_=ps[:, 0:nt], func=AF.Identity, bias=cvec[:, ct:ct + 1]),
                    reads=[pk, "const"], writes=[("yT", ct)])
                S.op("act", lambda e, ct=ct, acc=acc: e.activation(out=sq[:, ct % 2, 0:nt], in_=acc, func=AF.Square),
                     reads=[("yT", ct)], writes=[("sq", ct % 2)])
                S.op("pe", lambda e, ct=ct, acc=acc: e.matmul(ps_s, ones32[:], acc, start=(ct == 0), stop=(ct == 7)),
                     reads=[("yT", ct), "ones32"], writes=[pks])
                S.op("pe", lambda e, ct=ct: e.matmul(ps_q, ones32[:], sq[:, ct % 2, 0:nt], start=(ct == 0), stop=(ct == 7)),
                     reads=[("sq", ct % 2), "ones32"], writes=[pkq])
                yield
            S.op("pool", lambda e: e.tensor_copy(out=aT[:, :, 0:30], in_=aT[:, :, nt:nt + 30]),
                 reads=[("aT", ct) for ct in range(8)], writes=[("aT", ct) for ct in range(8)])
            mean = lnst[:, 0, 0:nt]
            var = lnst[:, 1, 0:nt]
            rstd = lnst[:, 2, 0:nt]
            S.op("act", lambda e: e.activation(out=mean, in_=ps_s, func=AF.Copy, scale=1.0 / D),
                 reads=[pks], writes=["cmean"])
            S.op("dve", lambda e: e.tensor_tensor(out=var, in0=mean, in1=mean, op=ALU.mult), reads=["cmean"], writes=["cvar"])
            S.op("dve", lambda e: e.scalar_tensor_tensor(out=var, in0=ps_q, scalar=1.0 / D, in1=var,
                                                         op0=ALU.mult, op1=ALU.subtract), reads=[pkq, "cvar"], writes=["cvar"])
            S.op("dve", lambda e: e.tensor_scalar(out=var, in0=var, scalar1=EPS, scalar2=None, op0=ALU.add),
                 reads=["cvar"], writes=["cvar"])
            yield
            S.op("act", lambda e: e.activation(out=rstd, in_=var, func=AF.Sqrt), reads=["cvar"], writes=["crstd"])
            S.op("dve", lambda e: e.reciprocal(out=rstd, in_=rstd), reads=["crstd"], writes=["crstd"])
            yield
            for ct in range(8):
                acc = yT[:, ct, 0:nt]
                S.op("dve", lambda e, acc=acc: e.tensor_tensor(out=acc, in0=acc, in1=mean, op=ALU.subtract),
                     reads=[("yT", ct), "cmean"], writes=[("yT", ct)])
                S.op("pool", lambda e, acc=acc: e.tensor_tensor(out=acc, in0=acc, in1=rstd, op=ALU.mult),
                     reads=[("yT", ct), "crstd"], writes=[("yT", ct)])
                S.op("act", lambda e, acc=acc, ct=ct: e.activation(
                    out=a2T[:, ct, 0:nt], in_=acc, func=AF.Silu, bias=cvec[:, 16 + ct:17 + ct], scale=cvec[:, 8 + ct:9 + ct]),
                    reads=[("yT", ct), "const"], writes=[("a2T", ct)])
                if ct % 2 == 1:
                    yield

        PT_ALL = [("ptA", i) for i in range(4)] + ["ro_a", "ro_b", "ptC"]
        PTZ = [("ptA", i) for i in range(4)]
        RNB_ALL = ["rnbA", "rnbB0", "rnbB1"]
        UV_ALL = [("uval", i) for i in range(4)]
        def g_ret():
            for ci in range(C):
                cs = slice(ci * 128, (ci + 1) * 128)
                for half in range(2):
                    ps, pk = bankf()
                    for hh in range(4):
                        h = half * 4 + hh
                        S.op("pe", lambda e, ps=ps, hh=hh, h=h, cs=cs: e.matmul(
                            ps[:, hh * 128:(hh + 1) * 128], knT[:, h, cs], qxT[:, h, cs], start=True, stop=True),
                            reads=["knT", "qxT"], writes=[pk])
                    S.op("dve", lambda e, ps=ps, half=half: e.tensor_tensor(
                        out=sbf[:, half * 4:half * 4 + 4, :], in0=ps[:].rearrange("p (h i) -> p h i", h=4),
                        in1=caus[:].unsqueeze(1).to_broadcast([128, 4, 128]), op=ALU.mult),
                        reads=[pk, "const"], writes=[("sbf", half)])
                    yield
                r32 = pt[:].rearrange("p (h e) -> p h e", h=H)
                for pr in range(4):
                    ps, pk = bankf()
                    for hh in range(2):
                        h = pr * 2 + hh
                        S.op("pe", lambda e, ps=ps, hh=hh, h=h, ci=ci: e.matmul(
                            ps[:, hh * 256:(hh + 1) * 256], sbf[:, h, :], v_ap[:, ci, h * 256:(h + 1) * 256],
                            start=True, stop=False), reads=[("sbf", h // 4), ("v", ci)], writes=[pk])
                        S.op("pe", lambda e, ps=ps, hh=hh, h=h, cs=cs: e.matmul(
                            ps[:, hh * 256:(hh + 1) * 256], qxT[:, h, cs], Rb[:, h, :],
                            start=False, stop=True), reads=["qxT", "Rb"], writes=[pk])
                    S.op("act", lambda e, ps=ps, pr=pr: e.activation(
                        out=pt[:, pr * 512:(pr + 1) * 512], in_=ps[:], func=AF.Copy), reads=[pk], writes=PT_ALL)
                    yield
                for pr in range(4):
                    ps, pk = bankf()
                    for hh in range(2):
                        h = pr * 2 + hh
                        S.op("pe", lambda e, ps=ps, hh=hh, h=h, ci=ci: e.matmul(
                            ps[:, hh * 256:(hh + 1) * 256], kze[:, ci, h * 128:(h + 1) * 128], v_ap[:, ci, h * 256:(h + 1) * 256],
                            start=True, stop=True), reads=[("kze", ci), ("v", ci)], writes=[pk])
                    for hh in range(2):
                        h = pr * 2 + hh
                        S.op("dve", lambda e, ps=ps, hh=hh, h=h: e.scalar_tensor_tensor(
                            out=Rf[:, h, :], in0=Rf[:, h, :], scalar=DEC128[h], in1=ps[:, hh * 256:(hh + 1) * 256],
                            op0=ALU.mult, op1=ALU.add), reads=[pk, ("Rf", h)], writes=[("Rf", h)])
                    yield
                S.op("act", lambda e: e.activation(out=Rb[:], in_=Rf[:], func=AF.Copy), reads=RF_ALL, writes=["Rb"])
                gst6 = small[:, 32:80].rearrange("p (h s) -> p h s", h=H)
                gmv = small[:, 80:96].rearrange("p (h s) -> p h s", h=H)
                gr = small[:, 96:104]
                gnm = small[:, 104:112]
                GMV_ALL = [("gmv", h_) for h_ in range(H)]
                for h in range(H):
                    S.op("dve", lambda e, h=h: e.bn_stats(out=gst6[:, h, :], in_=r32[:, h, :]), reads=PT_ALL, writes=[("gst6", h)])
                    if h % 4 == 3:
                        yield
                for h in range(H):
                    S.op("dve", lambda e, h=h: e.bn_aggr(out=gmv[:, h, :], in_=gst6[:, h, :]), reads=[("gst6", h)],
                         writes=[("gmv", h)])
                    if h % 4 == 3:
                        yield
                S.op("dve", lambda e: e.tensor_scalar(out=gr, in0=gmv[:, :, 1], scalar1=EPS, scalar2=None, op0=ALU.add),
                     reads=GMV_ALL, writes=["gr"])
                S.op("act", lambda e: e.activation(out=gr, in_=gr, func=AF.Sqrt), reads=["gr"], writes=["gr"])
                S.op("dve", lambda e: e.reciprocal(out=gr, in_=gr), reads=["gr"], writes=["gr"])
                S.op("dve", lambda e: e.tensor_tensor(out=r32, in0=r32, in1=gmv[:, :, 0:1].to_broadcast([128, H, DV]),
                                                      op=ALU.subtract), reads=PT_ALL + GMV_ALL, writes=PT_ALL)
                S.op("pool", lambda e: e.tensor_tensor(out=rnb[:].rearrange("p (h e) -> p h e", h=H), in0=r32,
                                                       in1=gr.unsqueeze(2).to_broadcast([128, H, DV]), op=ALU.mult),
                     reads=PT_ALL + ["gr"], writes=RNB_ALL)
                yield

                def evr(pb, pk2, t0, ntl, ci=ci):
                    tmp = uvf[:, 0:1024].rearrange("p (k t) -> p k t", k=8)
                    S.op("dve", lambda e: e.tensor_tensor(
                        out=tmp, in0=pb[:].rearrange("p (k t) -> p k t", k=8),
                        in1=gnfm[:, t0:t0 + 8].unsqueeze(2).to_broadcast([128, 8, 128]), op=ALU.mult),
                        reads=[pk2, "const"], writes=UV_ALL)
                    S.op("pool", lambda e: e.tensor_tensor(
                        out=tmp, in0=tmp, in1=gnfm[:, 16 + t0:24 + t0].unsqueeze(2).to_broadcast([128, 8, 128]), op=ALU.add),
                        reads=UV_ALL + ["const"], writes=UV_ALL)
                    S.op("pool", lambda e: e.tensor_tensor(
                        out=rT[:, t0:t0 + 8, ci * 128:(ci + 1) * 128], in0=tmp, in1=sgT[:, t0:t0 + 8, ci * 128:(ci + 1) * 128],
                        op=ALU.mult), reads=UV_ALL + ["sgT"], writes=["rT"])
                transposes_to(rnb, RNB_ALL, 16, evr)
                yield

        st["fset"] = [0, 1, 2]
        interleave(g_conv(), g_ret())
        st["fset"] = [0, 1, 2, 3, 4, 5]
        fm_blocks(range(16, 18))
        for nb in range(2):
            wslot, wk = wacquire()
            for et in range(4):
                dt_ = nb * 4 + et
                ps, pk = bankf()
                for k in range(8):
                    S.op("pe", lambda e, k=k, ps=ps, et=et, wslot=wslot: e.matmul(
                        ps[:, 0:nt], wslot[:, k, et * 128:(et + 1) * 128], a2T[:, k, 0:nt],
                        start=(k == 0), stop=(k == 7)), reads=[("a2T", k), wk], writes=[pk])
                S.op("dve", lambda e, ps=ps, dt_=dt_: e.tensor_tensor(
                    out=yT[:, dt_, 0:nt], in0=ps[:, 0:nt], in1=sig[:, dt_, 0:nt], op=ALU.mult),
                    reads=[pk, ("sig", dt_)], writes=[("yT", dt_)])
            wload_next()
        fm_blocks(range(18, 20))
        for nb in range(2):
            wa, wka = wacquire()
            wb_, wkb = wacquire()
            for et in range(4):
                dt_ = nb * 4 + et
                ps, pk = bankf()
                for k in range(16):
                    ws = wa if k < 8 else wb_
                    S.op("pe", lambda e, k=k, ps=ps, et=et, ws=ws: e.matmul(
                        ps[:, 0:nt], ws[:, k % 8, et * 128:(et + 1) * 128], rT[:, k, 0:nt],
                        start=(k == 0), stop=(k == 15)), reads=["rT", wka, wkb], writes=[pk])
                tmp = uacc[:, et % 2, 0:nt]
                S.op("dve", lambda e, ps=ps, dt_=dt_, tmp=tmp: e.tensor_tensor(
                    out=tmp, in0=ps[:, 0:nt], in1=sig[:, dt_, 0:nt], op=ALU.mult),
                    reads=[pk, ("sig", dt_)], writes=[("uacc", et % 2)])
                S.op("pool", lambda e, dt_=dt_, tmp=tmp: e.tensor_tensor(
                    out=a2T[:, dt_, 0:nt], in0=tmp, in1=yT[:, dt_, 0:nt], op=ALU.add),
                    reads=[("uacc", et % 2), ("yT", dt_)], writes=[("a2T", dt_)])
            wload_next()
            wload_next()
        wo = [wacquire() for _ in range(2)]
        ZK = [PTZ, ["ro_a", "ro_b", "ptC"]]
        for ci in range(C):
            z = pt[:, ci * 1024:(ci + 1) * 1024]
            for nb in range(2):
                ps, pk = bankf()
                for k in range(8):
                    S.op("pe", lambda e, k=k, ps=ps, nb=nb, ci=ci: e.matmul(
                        ps[:], a2T[:, k, ci * 128:(ci + 1) * 128], wo[nb][0][:, k, :], start=(k == 0), stop=(k == 7)),
                        reads=[("a2T", k), wo[nb][1]], writes=[pk])
                S.op("dve", lambda e, ps=ps, nb=nb, z=z: e.tensor_tensor(
                    out=z[:, nb * 512:(nb + 1) * 512], in0=ps[:], in1=gate_bc[:, 0, nb * 512:(nb + 1) * 512], op=ALU.mult),
                    reads=[pk, "gate_bc"], writes=ZK[ci])
        interleave(*[g_post_ln(pt[:, ci * 1024:(ci + 1) * 1024], ZK[ci], ci, ci, 0, 1) for ci in range(C)])
        wload_next()
        wload_next()
        interleave(*[g_ln1_and_hT(ci, ci * 128, 16, ci) for ci in range(C)])
        if halo:
            S.op("act", lambda e: e.activation(out=hTh[:], in_=hT[:, :, 126:128], func=AF.Copy),
                 reads=[("hT", 0)], writes=["hTh"])
            interleave(*s0_next())
            continue
        first_own = (gi == 1)
        GT_ALL = ["gT", ("v", 0), ("v", 1), "sgT"]
        gbufs = [(uacc[:, 0, 0:nt], ("uacc", 0)), (uacc[:, 1, 0:nt], ("uacc", 1)),
                 (sq[:, 0, 0:nt], ("sq", 0)), (sq[:, 1, 0:nt], ("sq", 1))]
        pending = []

        def flush_tail():
            while pending:
                pending.pop(0)()

        for p in range(6):
            ntile = 4 if p < 5 else 2
            for part in range(2):
                wslot, wk = wacquire()
                for et in range(ntile):
                    ft = p * 4 + et + part * 22
                    ps, pk = bankf()
                    for k in range(8):
                        S.op("pe", lambda e, k=k, ps=ps, et=et, wslot=wslot: e.matmul(
                            ps[:, 0:nt], wslot[:, k, et * 128:(et + 1) * 128], hT[:, k, 0:nt],
                            start=(k == 0), stop=(k == 7)), reads=[("hT", 0), ("hT", 1), wk], writes=[pk])
                    ub4 = st["ur"] % 4
                    st["ur"] += 1
                    ur = uraw[:, ub4, :]
                    urb = ("urawb", ub4)
                    urh = ("urawh", ub4)
                    if halo:
                        S.op("act", lambda e, ps=ps, ur=ur: e.activation(out=ur[:, 2:2 + nt], in_=ps[:, 0:nt], func=AF.Copy),
                             reads=[pk], writes=[urb])
                        S.op("pool", lambda e, ur=ur, ft=ft: e.tensor_scalar(
                            out=uhist[:, ft, :], in0=ur[:, nt:nt + 2], scalar1=hm, scalar2=0.0, op0=ALU.mult, op1=ALU.add),
                            reads=[urb, "const"], writes=[("uhist", ft)])
                        continue
                    if first_own:
                        for k in range(8):
                            S.op("pe", lambda e, k=k, ps=ps, et=et, wslot=wslot: e.matmul(
                                ps[:, nt:nt + 2], wslot[:, k, et * 128:(et + 1) * 128], hTh[:, k, :],
                                start=(k == 0), stop=(k == 7)), reads=["hTh", wk], writes=[pk])
                        S.op("act", lambda e, ps=ps, ur=ur: e.activation(out=ur[:, 0:2], in_=ps[:, nt:nt + 2], func=AF.Copy, scale=hm),
                             reads=[pk, "const"], writes=[urh])
                    else:
                        S.op("pool", lambda e, ur=ur, ft=ft: e.tensor_copy(out=ur[:, 0:2], in_=uhist[:, ft, :]),
                             reads=[("uhist", ft)], writes=[urh])
                    S.op("act", lambda e, ps=ps, ur=ur: e.activation(out=ur[:, 2:2 + nt], in_=ps[:, 0:nt], func=AF.Copy),
                         reads=[pk], writes=[urb])
                    if part == 0:
                        acc, acck = uval[:, et, 0:nt], ("uval", et)
                    else:
                        acc, acck = gbufs[st["gb"] % 4]
                        st["gb"] += 1
                    fw = fwfm[:, ft * 3:ft * 3 + 3]
                    S.op("act", lambda e, ps=ps, acc=acc, fw=fw, ft=ft: e.activation(
                        out=acc, in_=ps[:, 0:nt], func=AF.Identity, bias=fbfm[:, ft:ft + 1], scale=fw[:, 2:3]),
                        reads=[pk, "const"], writes=[acck])
                    flush_tail()
                    S.op("dve", lambda e, ur=ur, acc=acc, fw=fw: e.scalar_tensor_tensor(
                        out=acc, in0=ur[:, 1:1 + nt], scalar=fw[:, 1:2], in1=acc, op0=ALU.mult, op1=ALU.add),
                        reads=[urb, urh, acck], writes=[acck])
                    S.op("dve", lambda e, ur=ur, acc=acc, fw=fw: e.scalar_tensor_tensor(
                        out=acc, in0=ur[:, 0:nt], scalar=fw[:, 0:1], in1=acc, op0=ALU.mult, op1=ALU.add),
                        reads=[urb, urh, acck], writes=[acck])
                    S.op("pool", lambda e, ur=ur, ft=ft: e.tensor_copy(out=uhist[:, ft, :], in_=ur[:, nt:nt + 2]),
                         reads=[urb], writes=[("uhist", ft)])
                    if part == 1:
                        def tail(acc=acc, acck=acck, et=et, ft=ft):
                            S.op("act", lambda e: e.activation(out=acc, in_=acc, func=AF.Silu),
                                 reads=[acck], writes=[acck])
                            S.op("pool", lambda e: e.tensor_tensor(
                                out=gT[:, ft - 22, 0:nt], in0=uval[:, et, 0:nt], in1=acc, op=ALU.mult),
                                reads=[acck, ("uval", et)], writes=GT_ALL)
                        pending.append(tail)
                wload_next()
        flush_tail()
        if halo:
            interleave(*s0_next())
            continue
        for nb in range(2):
            wd = [wacquire() for _ in range(3)]
            for ci in range(C):
                z = pt[:, ci * 1024:(ci + 1) * 1024]
                ps, pk = bankf()
                for k in range(22):
                    ws, wkk = wd[k // 8]
                    S.op("pe", lambda e, k=k, ps=ps, ws=ws, ci=ci: e.matmul(
                        ps[:], gT[:, k, ci * 128:(ci + 1) * 128], ws[:, k % 8, :], start=(k == 0), stop=(k == 21)),
                        reads=GT_ALL + [wkk], writes=[pk])
                S.op("dve", lambda e, ps=ps, nb=nb, z=z: e.tensor_tensor(
                    out=z[:, nb * 512:(nb + 1) * 512], in0=ps[:], in1=gate_bc[:, 1, nb * 512:(nb + 1) * 512], op=ALU.mult),
                    reads=[pk, "gate_bc"], writes=ZK[ci])
            for _ in range(3):
                wload_next()
        interleave(*([g_post_ln(pt[:, ci * 1024:(ci + 1) * 1024], ZK[ci], ci, ci, 2, 3) for ci in range(C)] + s0_next()))
        r0 = (c0 - 1) * 128
        S.dma("sp", "yst%d" % (gi % 2), lambda e, r0=r0, C=C, gi=gi: e.dma_start(
            out=y[r0:r0 + C * 128, :].rearrange("(c p) d -> p c d", p=128), in_=xs2[:, gi % 2, 0:C, :]),
            reads=[("xs", gi % 2, i) for i in range(CG)], writes=["yout"])
    S.barrier()

    es.close()
    return nc


def _bf16_safe(a):
    return np.ascontiguousarray(a, dtype=np.float32)


def kernel(x, c, positions, w_ada, b_ada, w_in, b_in, conv_dw_w, conv_dw_b, conv_ln_g, conv_ln_b,
           w_conv_out, ret_gn_g, ret_gn_b, w_ret_out, w_out, ln1_g, ln1_b, w_up, ffn_dw_w, ffn_dw_b,
           w_down, ln2_g, ln2_b):
    f = _bf16_safe
    x2 = f(x)[0]
    pos = np.asarray(positions)[0].astype(np.int32)

    def fm(v, ntile):
        return np.ascontiguousarray(np.asarray(v, np.float32).reshape(ntile, 128).T)

    lg = np.array(LG, np.float64)
    p = np.arange(128, dtype=np.float64)[:, None]
    s = 128.0 ** -0.5
    xi = np.exp(lg[None, :] * (p + 1.0))
    zeta = s * np.exp(lg[None, :] * (127.0 - p))
    zneg = s * np.exp(-lg[None, :] * (p + 1.0))
    wtA = np.zeros((128, 16, 8))
    for cc in range(16):
        if cc < 15:
            wtA[:, cc, :] = s * np.exp(lg[None, :] * (1919.0 - (128.0 * cc + p)))
        else:
            wtA[:, cc, :] = s * np.exp(lg[None, :] * (127.0 - p))
    tabs = np.concatenate([xi, zeta, zneg, wtA.reshape(128, 128)], axis=1).astype(np.float32)
    caus = (np.arange(128)[None, :] >= np.arange(128)[:, None]).astype(np.float32)
    idn = np.eye(128, dtype=np.float32)
    half = 64
    invf1 = (np.float32(10000.0) ** (-np.arange(half, dtype=np.float32) / np.float32(half))).astype(np.float32)
    invf = np.ascontiguousarray(np.broadcast_to(invf1[None, :], (128, 64))).astype(np.float32)
    oneh = np.zeros((12, 8 * 128), np.float32)
    for r in range(8):
        oneh[r, r * 128:(r + 1) * 128] = 1.0

    b_in1 = np.asarray(b_in, np.float32)[0]
    shared = {
        "tabs": tabs, "caus": caus, "idn": idn, "invf": invf, "oneh": oneh,
        "c_fm": fm(np.asarray(c)[0], 8),
        "w_ada": f(w_ada)[0], "bada_fm": fm(np.asarray(b_ada)[0], 48), "bada_row": f(b_ada),
        "w_in": f(w_in)[0],
        "bin_tm": np.ascontiguousarray(b_in1[0:6144].reshape(12, 512)),
        "bin_fm": fm(np.concatenate([b_in1[4096:6144], b_in1[6144:10240]]), 48),
        "cw_fm": np.ascontiguousarray(np.asarray(conv_dw_w, np.float32)[0].T.reshape(8, 128, CK).transpose(1, 0, 2).reshape(128, 8 * CK)),
        "cvec_fm": np.concatenate([fm(np.asarray(conv_dw_b)[0], 8), fm(np.asarray(conv_ln_g)[0], 8),
                                   fm(np.asarray(conv_ln_b)[0], 8)], axis=1),
        "w_conv_out": f(w_conv_out)[0],
        "gn_fm": np.concatenate([fm(np.asarray(ret_gn_g)[0], 16), fm(np.asarray(ret_gn_b)[0], 16)], axis=1),
        "w_ret_out": f(w_ret_out)[0], "w_out": f(w_out)[0],
        "ln_rows": np.stack([np.asarray(a, np.float32)[0] for a in (ln1_g, ln1_b, ln2_g, ln2_b)]),
        "w_up": f(w_up)[0],
        "fw_fm": np.ascontiguousarray(np.asarray(ffn_dw_w, np.float32)[0].T.reshape(NFT, 128, 3).transpose(1, 0, 2).reshape(128, NFT * 3)),
        "fb_fm": fm(np.asarray(ffn_dw_b)[0], NFT),
        "w_down": f(w_down)[0],
    }
    shared = {k: np.ascontiguousarray(v, dtype=np.float32) for k, v in shared.items()}
    in_maps = []
    for i in range(NCORES):
        start = i * TOK
        xh = np.zeros((NCH * 128, D), np.float32)
        ph = np.zeros((NCH * 128,), np.int32)
        xh[128:] = x2[start:start + TOK]
        ph[128:] = pos[start:start + TOK]
        if i > 0:
            xh[:128] = x2[start - 128:start]
            ph[:128] = pos[start - 128:start]
        meta = np.zeros((128, 2), np.float32)
        meta[:, 0] = i
        meta[:, 1] = 1.0 if i > 0 else 0.0
        cf = np.zeros((8, 8))
        cb = np.zeros((8,))
        for j in range(NCORES):
            if j <= i - 2:
                cf[j, :] = np.exp(lg * (1920.0 + 2048.0 * (i - 2 - j)))
            if j == i - 1:
                cb[j] = 1.0
        cft = np.concatenate([cf.reshape(-1), cb]).astype(np.float32)
        m = dict(shared)
        m["xh"] = xh
        m["pos_t"] = np.ascontiguousarray(ph.reshape(NCH, 128).T)
        m["meta"] = meta
        m["cf"] = np.ascontiguousarray(np.broadcast_to(cft[None, :], (128, 72))).astype(np.float32)
        in_maps.append(m)
    if FUSED:
        nc = build("F")
        res = run_bass_kernel_spmd(nc, in_maps, core_ids=list(range(NCORES)))
    else:
        nca = build("A")
        resa = run_bass_kernel_spmd(nca, in_maps, core_ids=list(range(NCORES)))
        gath = np.ascontiguousarray(np.concatenate([r["ab"] for r in resa.results], axis=0), dtype=np.float32)
        for m in in_maps:
            m["gath"] = gath
        nc = build("B")
        res = run_bass_kernel_spmd(nc, in_maps, core_ids=list(range(NCORES)))
    out = np.concatenate([r["y"] for r in res.results], axis=0)
    return out.reshape(1, SEQ, D).astype(np.float32)
```

```python
import math
from contextlib import ExitStack
import numpy as np
import concourse.bass as bass
import concourse.mybir as mybir
from concourse.bass_utils import run_bass_kernel_spmd

F32 = mybir.dt.float32
BF16 = mybir.dt.bfloat16
I32 = mybir.dt.int32
AF = mybir.ActivationFunctionType
ALU = mybir.AluOpType
AX = mybir.AxisListType

NCORES = 8
D = 1024
SEQ = 16384
TOK = SEQ // NCORES
NCH = TOK // 128 + 1
H = 8
DK = 128
DV = 256
DFF = 2816
NFT = 2 * DFF // 128
CK = 31
EPS = 1e-5
ALPHA = 2.0 ** 0.25
CG = 2
NSLOT = 6
TM_ORDER = [0, 4, 1, 5, 2, 6, 3, 7]
FUSED = False
LG = [math.log(1.0 - 2.0 ** (-5.0 - h)) for h in range(H)]
DEC128 = [math.exp(LG[h] * 128.0) for h in range(H)]
TWO_PI = 2.0 * math.pi
C1 = 6.28125
C2 = TWO_PI - C1


class Sched:
    ENG = ("pe", "act", "dve", "pool", "sp")

    def __init__(self, nc, es):
        self.nc = nc
        self.streams = {e: [] for e in self.ENG}
        self.count = {e: 0 for e in self.ENG}
        self.sem = {e: es.enter_context(nc.semaphore("s_" + e)) for e in self.ENG}
        self.dsem = {}
        self.dcount = {}
        self.es = es
        self.waited = {}
        self.last_w = {}
        self.readers = {}
        self.eobj = {"pe": nc.tensor, "act": nc.scalar, "dve": nc.vector, "pool": nc.gpsimd, "sp": nc.sync}

    def _need(self, eng, tok):
        key, val, src = tok
        if self.waited.get((eng, key), 0) >= val:
            return
        self.waited[(eng, key)] = val
        self.eobj[eng].wait_ge(self.semof(key), val)

    def _deps(self, eng, reads, writes):
        for r in reads:
            t = self.last_w.get(r)
            if t is not None and not (t[2] == "pe" and eng == "pe"):
                self._need(eng, t)
        for w in writes:
            t = self.last_w.get(w)
            if t is not None and not (t[2] == "pe" and eng == "pe"):
                self._need(eng, t)
            for src, t in self.readers.get(w, {}).items():
                if src != eng:
                    self._need(eng, t)

    def _record(self, tok, reads, writes):
        for w in writes:
            self.last_w[w] = tok
            self.readers[w] = {}
        for r in reads:
            self.readers.setdefault(r, {})[tok[2]] = tok

    def op(self, eng, fn, reads=(), writes=()):
        self._deps(eng, reads, writes)
        self.count[eng] += 1
        tok = (eng, self.count[eng], eng)
        fn(self.eobj[eng]).then_inc(self.sem[eng], 1)
        self._record(tok, reads, writes)

    def dma(self, queue, semname, fn, reads=(), writes=()):
        if semname not in self.dsem:
            self.dsem[semname] = self.es.enter_context(self.nc.semaphore("d_" + semname))
            self.dcount[semname] = 0
        self._deps(queue, reads, writes)
        self.dcount[semname] += 16
        tok = (semname, self.dcount[semname], "dma:" + semname)
        fn(self.eobj[queue]).then_inc(self.dsem[semname], 16)
        self._record(tok, reads, writes)

    def barrier(self):
        for e in self.ENG:
            for o in self.ENG:
                if o != e and self.count[o] > 0:
                    self._need(e, (o, self.count[o], o))
            for s, v in self.dcount.items():
                self._need(e, (s, v, "dma:" + s))

    def semof(self, key):
        return self.sem[key] if key in self.sem else self.dsem[key]

    def emit(self, block):
        nc = self.nc
        engs = {"pe": block.tensor, "act": block.scalar, "dve": block.vector,
                "pool": block.gpsimd, "sp": block.sync}
        for name, deco in engs.items():
            stream = self.streams[name]

            def body(e, stream=stream, name=name):
                for it in stream:
                    if it[0] == "wait":
                        e.wait_ge(self.semof(it[1]), it[2])
                    elif it[0] == "op":
                        it[1](e).then_inc(self.sem[name], 1)
                    else:
                        it[1](e).then_inc(self.dsem[it[2]], 16)
            deco(body)


def build(mode="F"):
    nc = bass.Bass("TRN2", target_bir_lowering=False)
    es = ExitStack()
    S = Sched(nc, es)

    def din(name, shape, dt=F32):
        return nc.dram_tensor(name, list(shape), dt, kind="ExternalInput").ap()

    xh = din("xh", [NCH * 128, D])
    pos_i = din("pos_t", [128, NCH], I32)
    meta = din("meta", [128, 2])
    cf_d = din("cf", [128, 8 * 8 + 8])
    tabs_d = din("tabs", [128, 8 * 3 + 16 * 8])
    caus_d = din("caus", [128, 128])
    idn_d = din("idn", [128, 128])
    invf_d = din("invf", [128, 64])
    oneh_d = din("oneh", [12, 8 * 128])
    c_fm = din("c_fm", [128, 8])
    w_ada = din("w_ada", [D, 6 * D])
    bada_fm = din("bada_fm", [128, 48])
    bada_row = din("bada_row", [1, 6 * D])
    w_in = din("w_in", [D, 10240])
    bin_tm = din("bin_tm", [12, 512])
    bin_fm = din("bin_fm", [128, 48])
    cw_fm = din("cw_fm", [128, 8 * CK])
    cvec_fm = din("cvec_fm", [128, 8 * 3])
    w_conv_out = din("w_conv_out", [D, D])
    gn_fm = din("gn_fm", [128, 32])
    w_ret_out = din("w_ret_out", [2 * D, D])
    w_out = din("w_out", [D, D])
    ln_rows = din("ln_rows", [4, D])
    w_up = din("w_up", [D, 2 * DFF])
    fw_fm = din("fw_fm", [128, NFT * 3])
    fb_fm = din("fb_fm", [128, NFT])
    w_down = din("w_down", [DFF, D])
    if mode in ("F", "B"):
        y = nc.dram_tensor("y", [TOK, D], F32, kind="ExternalOutput").ap()
    if mode == "F":
        bounce_t = nc.dram_tensor("bounce", [2 * H * 128, DV], F32)
        gath_t = nc.dram_tensor("gath", [NCORES * 2 * H * 128, DV], F32)
        bounce = bounce_t.ap()
        gath = gath_t.ap()
    elif mode == "A":
        bounce = nc.dram_tensor("ab", [2 * H * 128, DV], F32, kind="ExternalOutput").ap()
    else:
        gath = din("gath", [NCORES * 2 * H * 128, DV])
    dbg_out = {}

    def sb(name, shape, dt=F32):
        return es.enter_context(nc.sbuf_tensor("sb_" + name, list(shape), dt))

    NT = CG * 128
    meta_t = sb("meta_t", [128, 2])
    cf_t = sb("cf_t", [128, 72])
    tabs = sb("tabs", [128, 24 + 128])
    caus = sb("caus", [128, 128])
    identb = sb("identb", [128, 128], BF16)
    onehb = sb("onehb", [12, 8 * 128], BF16)
    bintm = sb("bintm", [12, 512], BF16)
    binfm = sb("binfm", [128, 48])
    cwfm = sb("cwfm", [128, 8 * CK])
    cvec = sb("cvec", [128, 24])
    gnfm = sb("gnfm", [128, 32])
    fwfm = sb("fwfm", [128, NFT * 3])
    fbfm = sb("fbfm", [128, NFT])
    modfm = sb("modfm", [128, 32])
    gate_bc = sb("gate_bc", [128, 2, D])
    ln_bc = sb("ln_bc", [128, 4, D])
    costab = sb("costab", [128, NCH, 64])
    sintab = sb("sintab", [128, NCH, 64])
    ones32 = sb("ones32", [128, 128])
    wbuf = sb("wbuf", [128, NSLOT, 8, 512], BF16)
    xs2 = sb("xs", [128, 2, CG, D])
    cur = {"xb": 0}

    def XS(ci):
        return xs2[:, cur["xb"], ci, :]

    def XK(ci):
        return ("xs", cur["xb"], ci)
    hT = sb("hT", [128, 8, NT], BF16)
    qxT = sb("qxT", [128, 8, NT], BF16)
    knT = sb("knT", [128, 8, NT], BF16)
    kze = sb("kze", [128, CG, D], BF16)
    pbig = sb("pbig", [128, 8192], BF16)
    aT = sb("aT", [128, 8, 30 + NT], BF16)
    yT = sb("yT", [128, 8, NT])
    pt = sb("pt", [128, 2048])
    a2T = sb("a2T", [128, 8, NT], BF16)
    sig = sb("sig", [128, 8, NT], BF16)
    rT = sb("rT", [128, 16, NT], BF16)
    rnb = sb("rnb", [128, 2048], BF16)
    hb = rnb[:, 0:1024]
    rotb = rnb[:, 1024:2048].rearrange("p (a n) -> p a n", a=2)
    rt32 = pt[:, 0:1024].rearrange("p (a n) -> p a n", a=4)
    ro32 = pt[:, 1024:1536]
    sbf = sb("sbf", [128, 8, 128], BF16)
    Rf = sb("Rf", [128, H, DV])
    Rb = sb("Rb", [128, H, DV], BF16)
    uraw = sb("uraw", [128, 4, 2 + NT])
    uval = sb("uval", [128, 4, NT])
    uacc = sb("uacc", [128, 2, NT])
    uhist = sb("uhist", [128, NFT, 2])
    hTh = sb("hTh", [128, 8, 2], BF16)
    sq = sb("sq", [128, 2, NT])
    lnst = sb("lnst", [128, 3, NT])
    small = sb("small", [128, 160])
    uvf = uval[:].rearrange("p a n -> p (a n)")

    psf = [es.enter_context(nc.psum_tensor("psf%d" % i, [128, 512], F32)) for i in range(6)]
    psb = [es.enter_context(nc.psum_tensor("psb%d" % i, [128, 1024], BF16)) for i in range(2)]
    st = {"f": 0, "b": 0, "w": 0, "ev": 0, "dg": 0, "ur": 0, "gb": 0}

    st["fset"] = [0, 1, 2, 3, 4, 5]

    def bankf():
        fs = st["fset"]
        i = fs[st["f"] % len(fs)]
        st["f"] += 1
        return psf[i], ("psf", i)

    def bankb():
        i = st["b"] % 2
        st["b"] += 1
        return psb[i], ("psb", i)

    v_ap = pbig[:, 0:4096].rearrange("p (c e) -> p c e", c=CG)
    sgT = pbig[:, 4096:8192].rearrange("p (t n) -> p t n", t=16)
    gT = pbig[:, 0:22 * NT].rearrange("p (t n) -> p t n", t=22)
    stage32 = pbig[:].bitcast(F32)

    xi_t = tabs[:, 0:8]
    zeta_t = tabs[:, 8:16]
    zneg_t = tabs[:, 16:24]
    wtA = tabs[:, 24:152].rearrange("p (c h) -> p c h", c=16)
    hm = meta_t[:, 1:2]

    cst = []

    def cload(dst, src, q="sp"):
        S.dma(q, "cst", lambda e, d=dst, s=src: e.dma_start(out=d, in_=s), writes=["const"])

    cload(meta_t[:], meta[:, :])
    cload(cf_t[:], cf_d[:, :])
    cload(tabs[:], tabs_d[:, :])
    cload(caus[:], caus_d[:, :])
    cload(binfm[:], bin_fm[:, :])
    cload(cwfm[:], cw_fm[:, :])
    cload(cvec[:], cvec_fm[:, :])
    cload(gnfm[:], gn_fm[:, :])
    cload(fwfm[:], fw_fm[:, :])
    cload(fbfm[:], fb_fm[:, :])
    for r in range(4):
        cload(ln_bc[:, r, :], ln_rows[r:r + 1, :].partition_broadcast(128))
    cload(gate_bc[:, 0, :], bada_row[0:1, 2 * D:3 * D].partition_broadcast(128))
    cload(gate_bc[:, 1, :], bada_row[0:1, 5 * D:6 * D].partition_broadcast(128))
    idf = pt[:, 0:128]
    ohf = pt[0:12, 128:128 + 1024]
    bif = uvf[0:12, 0:512]
    posf = uvf[:, 512:512 + NCH]
    posi = sb("posi", [128, NCH], I32)
    invf = uvf[:, 576:640]
    cfm = uvf[:, 640:648]
    badf = uvf[:, 648:696]
    cload(idf, idn_d[:, :])
    cload(ohf, oneh_d[:, :])
    cload(bif, bin_tm[:, :])
    cload(posi[:], pos_i[:, :])
    cload(invf, invf_d[:, :])
    cload(cfm, c_fm[:, :])
    cload(badf, bada_fm[:, :])
    S.barrier()
    blocks = []

    uids = {}

    def add_block(w, r0, kc, c0, ncols):
        key = (w.tensor.name, r0, c0)
        if key not in uids:
            uids[key] = len(uids)
        blocks.append((w[r0:r0 + kc * 128, c0:c0 + ncols], kc, ncols, uids[key]))

    def add_dg_block(ct):
        blocks.append((None, 0, 0, ("dg", ct)))

    if mode in ("F", "A"):
        for nb in range(2, 8):
            add_block(w_in, 0, 8, nb * 512, 512)
    groups = [[0]] + [[1 + 2 * g, 2 + 2 * g] for g in range(8)]
    for g in (groups if mode in ("F", "B") else []):
        for nb in TM_ORDER + list(range(8, 16)):
            add_block(w_in, 0, 8, nb * 512, 512)
        for ct in range(8):
            add_dg_block(ct)
        for nb in range(16, 18):
            add_block(w_in, 0, 8, nb * 512, 512)
        for nb in range(2):
            add_block(w_conv_out, 0, 8, nb * 512, 512)
        for nb in range(18, 20):
            add_block(w_in, 0, 8, nb * 512, 512)
        for nb in range(2):
            for kb in range(2):
                add_block(w_ret_out, kb * 1024, 8, nb * 512, 512)
        for nb in range(2):
            add_block(w_out, 0, 8, nb * 512, 512)
        for p in (range(6) if g != [0] else []):
            nv = 512 if p < 5 else 256
            add_block(w_up, 0, 8, p * 512, nv)
            add_block(w_up, 0, 8, DFF + p * 512, nv)
        if g != [0]:
            for nb in range(2):
                add_block(w_down, 0, 8, nb * 512, 512)
                add_block(w_down, 1024, 8, nb * 512, 512)
                add_block(w_down, 2048, 6, nb * 512, 512)
    wst_ = {"loaded": 0, "next": 0}

    def wload_next():
        i = wst_["loaded"]
        if i >= len(blocks):
            return
        src, kc, ncols, uid = blocks[i]
        slot = i % NSLOT
        if src is None:
            ct = uid[1]
            S.dma("sp", "w%d" % slot, lambda e, ct=ct, slot=slot: e.dma_start(
                out=wbuf[:, slot].rearrange("p k n -> p (k n)")[:, 0:CK * 128], in_=dgsc[ct * 128:(ct + 1) * 128, 0:CK * 128]),
                reads=["dgsc"], writes=[("w", slot)])
            wst_["loaded"] += 1
            return
        if wst_.get("wsc") is None:
            nuse = {}
            for b in blocks:
                if b[0] is not None:
                    nuse[b[3]] = nuse.get(b[3], 0) + 1
            wst_["nuse"] = nuse
            wst_["cast"] = set()
            wst_["wsc"] = nc.dram_tensor("wsc", [max(1, len(uids)) * 128, 4096], BF16).ap()
        wsc = wst_["wsc"]
        scr = wsc[uid * 128:(uid + 1) * 128, :].rearrange("p (k n) -> p k n", k=8)[:, 0:kc, 0:ncols]
        if uid not in wst_["cast"]:
            wst_["cast"].add(uid)
            S.dma("pool", "w%d" % slot, lambda e, src=src, kc=kc, ncols=ncols, slot=slot: e.dma_start(
                out=wbuf[:, slot, 0:kc, 0:ncols], in_=src.rearrange("(k p) n -> p k n", p=128)),
                writes=[("w", slot)])
            if wst_["nuse"][uid] > 1:
                def wb(scr=scr, kc=kc, ncols=ncols, slot=slot, uid=uid):
                    S.dma("sp", "wb%d" % slot, lambda e: e.dma_start(
                        out=scr, in_=wbuf[:, slot, 0:kc, 0:ncols]), reads=[("w", slot)], writes=[("wsc", uid)])
                if wst_.get("defer") is not None:
                    wst_["defer"].append(wb)
                else:
                    wb()
        else:
            S.dma("sp", "w%d" % slot, lambda e, scr=scr, kc=kc, ncols=ncols, slot=slot: e.dma_start(
                out=wbuf[:, slot, 0:kc, 0:ncols], in_=scr), reads=[("wsc", uid)], writes=[("w", slot)])
        wst_["loaded"] += 1

    def wacquire():
        i = wst_["next"]
        wst_["next"] += 1
        assert i < wst_["loaded"], "weight block not prefetched"
        slot = i % NSLOT
        return wbuf[:, slot], ("w", slot)

    wst_["defer"] = []
    for _ in range(NSLOT):
        wload_next()
    deferred_wb = wst_["defer"]
    wst_["defer"] = None

    CK_ = ["const"]
    S.op("dve", lambda e: e.tensor_copy(out=identb[:], in_=idf), reads=CK_, writes=["identb"])
    S.op("dve", lambda e: e.tensor_copy(out=onehb[:], in_=ohf), reads=CK_, writes=["onehb"])
    S.op("dve", lambda e: e.tensor_copy(out=bintm[:], in_=bif), reads=CK_, writes=["bintm"])
    S.op("dve", lambda e: e.tensor_copy(out=posf, in_=posi[:]), reads=CK_, writes=["posf"])
    S.op("pool", lambda e: e.memset(ones32[:], 1.0), writes=["ones32"])
    dgsc = nc.dram_tensor("dgsc", [8 * 128, 4096], BF16).ap()
    if mode in ("F", "B"):
        dstg = [yT[:].rearrange("p a n -> p (a n)").bitcast(BF16), rT[:].rearrange("p a n -> p (a n)")]
        for ct in range(8):
            stg = dstg[ct % 2][:, 0:CK * 128]
            sk = ("dgst", ct % 2)
            S.op("dve", lambda e, ct=ct, stg=stg: e.tensor_tensor(
                out=stg.rearrange("p (t m) -> p t m", t=CK),
                in0=identb[:].unsqueeze(1).to_broadcast([128, CK, 128]),
                in1=cwfm[:, ct * CK:(ct + 1) * CK].unsqueeze(2).to_broadcast([128, CK, 128]), op=ALU.mult),
                reads=["identb", "const"], writes=[sk])
            S.dma("sp", "dgw%d" % (ct % 2), lambda e, ct=ct, stg=stg: e.dma_start(
                out=dgsc[ct * 128:(ct + 1) * 128, 0:CK * 128], in_=stg), reads=[sk], writes=["dgsc"])
        S.barrier()
    NA = NCH * 64
    ang = stage32[:, 0:NA].rearrange("p (c j) -> p c j", c=NCH)
    kf = stage32[:, NA:2 * NA].rearrange("p (c j) -> p c j", c=NCH)
    ki = pbig[:].bitcast(I32)[:, 2 * NA:3 * NA].rearrange("p (c j) -> p c j", c=NCH)
    msk = yT[:].rearrange("p a n -> p (a n)")[:, 0:NA].rearrange("p (c j) -> p c j", c=NCH)
    S.op("dve", lambda e: e.tensor_tensor(out=ang, in0=posf.unsqueeze(2).to_broadcast([128, NCH, 64]),
                                          in1=invf.unsqueeze(1).to_broadcast([128, NCH, 64]), op=ALU.mult),
         reads=["posf", "const"], writes=["ang"])
    S.op("dve", lambda e: e.tensor_scalar(out=kf, in0=ang, scalar1=1.0 / TWO_PI, scalar2=None, op0=ALU.mult),
         reads=["ang"], writes=["kf"])
    S.op("dve", lambda e: e.tensor_copy(out=ki, in_=kf), reads=["kf"], writes=["ki"])
    S.op("dve", lambda e: e.tensor_copy(out=kf, in_=ki), reads=["ki"], writes=["kf"])
    S.op("dve", lambda e: e.scalar_tensor_tensor(out=ang, in0=kf, scalar=-C1, in1=ang, op0=ALU.mult, op1=ALU.add),
         reads=["kf", "ang"], writes=["ang"])
    S.op("dve", lambda e: e.scalar_tensor_tensor(out=ang, in0=kf, scalar=-C2, in1=ang, op0=ALU.mult, op1=ALU.add),
         reads=["kf", "ang"], writes=["ang"])

    def wrap(t):
        S.op("dve", lambda e: e.tensor_single_scalar(out=msk, in_=t, scalar=math.pi, op=ALU.is_gt),
             reads=["ang"], writes=["msk"])
        S.op("dve", lambda e: e.scalar_tensor_tensor(out=t, in0=msk, scalar=-TWO_PI, in1=t, op0=ALU.mult, op1=ALU.add),
             reads=["msk", "ang"], writes=["ang"])
        S.op("dve", lambda e: e.tensor_single_scalar(out=msk, in_=t, scalar=-math.pi, op=ALU.is_lt),
             reads=["ang"], writes=["msk"])
        S.op("dve", lambda e: e.scalar_tensor_tensor(out=t, in0=msk, scalar=TWO_PI, in1=t, op0=ALU.mult, op1=ALU.add),
             reads=["msk", "ang"], writes=["ang"])
        S.op("dve", lambda e: e.tensor_scalar(out=t, in0=t, scalar1=math.pi, scalar2=-math.pi, op0=ALU.min, op1=ALU.max),
             reads=["ang"], writes=["ang"])

    wrap(ang)
    S.op("act", lambda e: e.activation(out=sintab[:], in_=ang, func=AF.Sin), reads=["ang"], writes=["sintab"])
    S.op("dve", lambda e: e.tensor_scalar(out=ang, in0=ang, scalar1=math.pi / 2, scalar2=None, op0=ALU.add),
         reads=["ang", "sintab"], writes=["ang"])
    wrap(ang)
    S.op("act", lambda e: e.activation(out=costab[:], in_=ang, func=AF.Sin), reads=["ang"], writes=["costab"])

    silc = small[:, 0:8]
    S.op("act", lambda e: e.activation(out=silc, in_=cfm, func=AF.Silu), reads=CK_, writes=["silc"])
    S.barrier()
    lbc = aT[:].rearrange("p a n -> p (a n)").bitcast(F32)[:, 0:1024].rearrange("p (k m) -> p k m", k=8)
    S.op("dve", lambda e: e.tensor_copy(out=lbc, in_=silc.unsqueeze(2).to_broadcast([128, 8, 128])),
         reads=["silc"], writes=["lbc"])
    wsts = [stage32.rearrange("p (k n) -> p k n", k=8),
            xs2[:].rearrange("p a c d -> p (a c d)").rearrange("p (k n) -> p k n", k=8)]
    for nb in (range(4) if mode == "A" else range(12)):
        wst = wsts[nb % 2]
        wkey = ("wst", nb % 2)
        S.dma("sp", "wst%d" % (nb % 2), lambda e, nb=nb, wst=wst: e.dma_start(
            out=wst, in_=w_ada[:, nb * 512:(nb + 1) * 512].rearrange("(k p) n -> p k n", p=128)),
            writes=[wkey])
        sec = nb // 2
        ps, pk = bankf()
        for k in range(8):
            S.op("pe", lambda e, k=k, ps=ps, wst=wst: e.matmul(ps[:], lbc[:, k, :], wst[:, k, :], start=(k == 0), stop=(k == 7)),
                 reads=["lbc", wkey], writes=[pk])
        if sec in (2, 5):
            gi = 0 if sec == 2 else 1
            dst = gate_bc[:, gi, (nb % 2) * 512:(nb % 2 + 1) * 512]
            S.op("dve", lambda e, ps=ps, dst=dst: e.tensor_tensor(out=dst, in0=ps[:], in1=dst, op=ALU.add),
                 reads=[pk, "const"], writes=["gate_bc"])
        else:
            gt = nb * 4
            col = {0: 0, 1: 8, 3: 16, 4: 24}[sec] + (nb % 2) * 4
            plus1 = 1.0 if sec in (1, 4) else 0.0
            for et in range(4):
                tmp = sq[:, et % 2, 0:128]
                S.op("dve", lambda e, ps=ps, et=et, tmp=tmp: e.tensor_tensor(
                    out=tmp, in0=ps[:, et * 128:(et + 1) * 128], in1=identb[:], op=ALU.mult),
                    reads=[pk, "identb"], writes=[("sq", et % 2)])
                S.op("dve", lambda e, et=et, tmp=tmp, col=col: e.reduce_sum(
                    out=modfm[:, col + et:col + et + 1], in_=tmp, axis=AX.X),
                    reads=[("sq", et % 2)], writes=["modfm"])
            S.op("dve", lambda e, col=col, gt=gt, plus1=plus1: e.scalar_tensor_tensor(
                out=modfm[:, col:col + 4], in0=modfm[:, col:col + 4], scalar=plus1, in1=badf[:, gt:gt + 4],
                op0=ALU.add, op1=ALU.add), reads=["modfm", "const"], writes=["modfm"])
    S.barrier()
    for wb_ in deferred_wb:
        wb_()
    S.op("pool", lambda e: e.memset(uhist[:], 0.0), writes=[("uhist", ft) for ft in range(NFT)])
    S.op("pool", lambda e: e.memset(aT[:], 0.0), writes=[("aT", ct) for ct in range(8)])

    def _kl(k):
        return list(k) if isinstance(k, list) else [k]

    PT_ALL = [("ptA", i) for i in range(4)] + ["ro_a", "ro_b", "ptC"]

    def run(g):
        for _ in g:
            pass

    def interleave(*gens):
        gens = list(gens)
        while gens:
            for g in list(gens):
                try:
                    next(g)
                except StopIteration:
                    gens.remove(g)

    def g_ln_stats(src, srck, par, res):
        o = {0: 16, 1: 112, 2: 128, 3: 144}[par]
        p = str(par)
        stt = small[:, o:o + 12]
        S.op("dve", lambda e: e.bn_stats(out=stt[:, 0:6], in_=src[:, 0:512]), reads=_kl(srck), writes=["lnstat" + p])
        S.op("dve", lambda e: e.bn_stats(out=stt[:, 6:12], in_=src[:, 512:1024]), reads=_kl(srck), writes=["lnstat2" + p])
        yield
        mv = small[:, o + 12:o + 14]
        S.op("dve", lambda e: e.bn_aggr(out=mv, in_=stt), reads=["lnstat" + p, "lnstat2" + p], writes=["mv" + p])
        rs = small[:, o + 14:o + 15]
        nmr = small[:, o + 15:o + 16]
        S.op("dve", lambda e: e.tensor_scalar(out=rs, in0=mv[:, 1:2], scalar1=EPS, scalar2=None, op0=ALU.add),
             reads=["mv" + p], writes=["rs" + p])
        yield
        S.op("act", lambda e: e.activation(out=rs, in_=rs, func=AF.Sqrt), reads=["rs" + p], writes=["rs" + p])
        yield
        S.op("dve", lambda e: e.reciprocal(out=rs, in_=rs), reads=["rs" + p], writes=["rs" + p])
        S.op("dve", lambda e: e.scalar_tensor_tensor(out=nmr, in0=mv[:, 0:1], scalar=-1.0, in1=rs,
                                                     op0=ALU.mult, op1=ALU.mult), reads=["mv" + p, "rs" + p], writes=["nmr" + p])
        res["rs"] = rs
        res["nmr"] = nmr
        res["keys"] = ["rs" + p, "nmr" + p]
        yield

    def transposes_to(src_bf, srck, n_tiles, evac):
        t0 = 0
        while t0 < n_tiles:
            nt = min(8, n_tiles - t0)
            pb, pk = bankb()
            for t in range(nt):
                S.op("pe", lambda e, t=t, t0=t0, pb=pb: e.transpose(
                    pb[:, t * 128:(t + 1) * 128], src_bf[:, (t0 + t) * 128:(t0 + t + 1) * 128], identb[:]),
                    reads=_kl(srck) + ["identb"], writes=[pk])
            evac(pb, pk, t0, nt)
            t0 += nt

    def g_ln1_and_hT(ci, col0, mod_off, par=0, spar=None, xb=None):
        if xb is None:
            xb = cur["xb"]
        src = xs2[:, xb, ci, :]
        xk = ("xs", xb, ci)
        res = {}
        yield from g_ln_stats(src, xk, par if spar is None else spar, res)
        hbp = rnb[:, par * 1024:(par + 1) * 1024]
        hk = ["rnbA"] if par == 0 else ["rnbB0", "rnbB1"]
        S.op("act", lambda e: e.activation(out=hbp, in_=src, func=AF.Identity, bias=res["nmr"], scale=res["rs"]),
             reads=[xk] + res["keys"], writes=hk)
        yield
        pb, pk = bankb()
        for t in range(8):
            S.op("pe", lambda e, t=t: e.transpose(pb[:, t * 128:(t + 1) * 128], hbp[:, t * 128:(t + 1) * 128], identb[:]),
                 reads=hk + ["identb"], writes=[pk])
        yield
        for k in range(8):
            S.op("act", lambda e, k=k: e.activation(
                out=hT[:, k, col0:col0 + 128], in_=pb[:, k * 128:(k + 1) * 128], func=AF.Identity,
                bias=modfm[:, mod_off + k:mod_off + k + 1], scale=modfm[:, mod_off + 8 + k:mod_off + 9 + k]),
                reads=[pk, "modfm"], writes=[("hT", ci, k)])
            if k % 4 == 3:
                yield

    def ln1_and_hT(ci, col0, mod_off):
        run(g_ln1_and_hT(ci, col0, mod_off, 0))

    def g_post_ln(z, zk, ci, par, grow, brow):
        xsrc = XS(ci)
        xk = XK(ci)
        S.op("dve", lambda e: e.scalar_tensor_tensor(out=z, in0=xsrc, scalar=ALPHA, in1=z,
                                                     op0=ALU.mult, op1=ALU.add),
             reads=[xk] + zk, writes=zk)
        yield
        res = {}
        yield from g_ln_stats(z, zk, par, res)
        S.op("act", lambda e: e.activation(out=z, in_=z, func=AF.Identity, bias=res["nmr"], scale=res["rs"]),
             reads=zk + res["keys"], writes=zk)
        yield
        S.op("dve", lambda e: e.tensor_tensor(out=z, in0=z, in1=ln_bc[:, grow, :], op=ALU.mult),
             reads=zk + ["const"], writes=zk)
        yield
        S.op("pool", lambda e: e.tensor_tensor(out=xsrc, in0=z, in1=ln_bc[:, brow, :], op=ALU.add),
             reads=zk + ["const"], writes=[xk])
        yield

    def proj_tm(ci, col0, wslot, wk, nb_bias):
        ps, pk = bankf()
        for k in range(8):
            S.op("pe", lambda e, k=k, ps=ps: e.matmul(ps[:], hT[:, k, col0:col0 + 128], wslot[:, k, :],
                                                      start=(k == 0), stop=False),
                 reads=[("hT", ci, k), wk], writes=[pk])
        S.op("pe", lambda e, ps=ps: e.matmul(ps[:], onehb[:, nb_bias * 128:(nb_bias + 1) * 128], bintm[:, :],
                                             start=False, stop=True),
             reads=["onehb", "bintm"], writes=[pk])
        return ps, pk

    def rotary(ps, pk, chunk, kind, blk, ci):
        pv = ps[:].rearrange("p (h t j) -> p h t j", h=4, t=2)
        x1 = pv[:, :, 0, :]
        x2 = pv[:, :, 1, :]
        cb = costab[:, chunk, :].unsqueeze(1).to_broadcast([128, 4, 64])
        sn = sintab[:, chunk, :].unsqueeze(1).to_broadcast([128, 4, 64])
        t = [rt32[:, i, :].rearrange("p (h j) -> p h j", h=4) for i in range(4)]
        rd = [pk, "costab", "sintab"]
        S.op("dve", lambda e: e.tensor_tensor(out=t[0], in0=x1, in1=cb, op=ALU.mult), reads=rd, writes=[("ptA", 0)])
        S.op("dve", lambda e: e.tensor_tensor(out=t[1], in0=x2, in1=sn, op=ALU.mult), reads=rd, writes=[("ptA", 1)])
        S.op("dve", lambda e: e.tensor_tensor(out=t[2], in0=x2, in1=cb, op=ALU.mult), reads=rd, writes=[("ptA", 2)])
        S.op("dve", lambda e: e.tensor_tensor(out=t[3], in0=x1, in1=sn, op=ALU.mult), reads=rd, writes=[("ptA", 3)])
        ov = ro32.rearrange("p (h t j) -> p h t j", h=4, t=2)
        S.op("pool", lambda e: e.tensor_tensor(out=ov[:, :, 0, :], in0=t[0], in1=t[1], op=ALU.subtract),
             reads=[("ptA", 0), ("ptA", 1)], writes=["ro_a"])
        S.op("pool", lambda e: e.tensor_tensor(out=ov[:, :, 1, :], in0=t[2], in1=t[3], op=ALU.add),
             reads=[("ptA", 2), ("ptA", 3)], writes=["ro_b"])
        o3 = ro32.rearrange("p (h d) -> p h d", h=4)
        hs = slice(blk * 4, blk * 4 + 4)
        if kind == "kA":
            return None
        dst = rotb[:, ci, :].rearrange("p (h d) -> p h d", h=4)
        tab = xi_t if kind == "q" else zneg_t
        S.op("dve", lambda e: e.tensor_tensor(out=dst, in0=o3, in1=tab[:, hs].unsqueeze(2).to_broadcast([128, 4, 128]),
                                              op=ALU.mult), reads=["ro_a", "ro_b", "const"], writes=["rnbB%d" % ci])
        return None

    def kzeta_from_ro(dst, dstk, blk, table, extra_hm):
        o3 = ro32.rearrange("p (h d) -> p h d", h=4)
        hs = slice(blk * 4, blk * 4 + 4)
        d3 = dst.rearrange("p (h d) -> p h d", h=4)
        S.op("pool", lambda e: e.tensor_tensor(out=d3, in0=o3, in1=table[:, hs].unsqueeze(2).to_broadcast([128, 4, 128]),
                                               op=ALU.mult), reads=["ro_a", "ro_b", "const"], writes=[dstk])
        if extra_hm:
            S.op("pool", lambda e: e.tensor_scalar(out=dst, in0=dst, scalar1=hm, scalar2=0.0, op0=ALU.mult, op1=ALU.add),
                 reads=[dstk, "const"], writes=[dstk])

    if mode in ("F", "A"):
        kw = [wacquire() for _ in range(2)]
        vw = [wacquire() for _ in range(4)]
        vz = pbig[:, 0:2048]
        st["f"] = 0
        kvb = [bankf() for _ in range(4)]
        st["f"] = 4

        def bankA():
            i = 4 + (st["f"] % 2)
            st["f"] += 1
            return psf[i], ("psf", i)

        def kv_accum(c_own, first, last):
            for h in range(H):
                ps, pk = kvb[h // 2]
                S.op("pe", lambda e, h=h, ps=ps: e.matmul(
                    ps[:, (h % 2) * 256:(h % 2 + 1) * 256], kze[:, 0, h * 128:(h + 1) * 128], vz[:, h * 256:(h + 1) * 256],
                    start=first, stop=last), reads=["kzeA", "vz"], writes=[pk])

        def g_pre(c_own):
            chunk = c_own + 1
            par = c_own % 2
            S.dma("sp", "xld%d" % par, lambda e: e.dma_start(out=xs2[:, 0, par, :], in_=xh[chunk * 128:(chunk + 1) * 128, :]),
                  writes=[("xs", 0, par)])
            yield
            yield from g_ln1_and_hT(par, par * 128, 0, par)

        def g_main(c_own):
            chunk = c_own + 1
            par = c_own % 2
            col = par * 128
            for blk in range(2):
                ps, pk = psf[4 + (st["f"] % 2)], ("psf", 4 + (st["f"] % 2))
                st["f"] += 1
                for k in range(8):
                    S.op("pe", lambda e, k=k, ps=ps, blk=blk: e.matmul(ps[:], hT[:, k, col:col + 128], kw[blk][0][:, k, :],
                                                                       start=(k == 0), stop=False),
                         reads=[("hT", par, k), kw[blk][1]], writes=[pk])
                S.op("pe", lambda e, ps=ps, blk=blk: e.matmul(ps[:], onehb[:, (2 + blk) * 128:(3 + blk) * 128], bintm[:, :],
                                                              start=False, stop=True),
                     reads=["onehb", "bintm"], writes=[pk])
                yield
                rotary(ps, pk, chunk, "kA", blk, 0)
                S.op("act", lambda e, blk=blk: e.activation(out=kze[:, 0, blk * 512:(blk + 1) * 512], in_=ro32, func=AF.Copy),
                     reads=["ro_a", "ro_b"], writes=["kzeA"])
                yield
            for blk in range(4):
                ps, pk = psf[4 + (st["f"] % 2)], ("psf", 4 + (st["f"] % 2))
                st["f"] += 1
                for k in range(8):
                    S.op("pe", lambda e, k=k, ps=ps, blk=blk: e.matmul(ps[:], hT[:, k, col:col + 128], vw[blk][0][:, k, :],
                                                                       start=(k == 0), stop=False),
                         reads=[("hT", par, k), vw[blk][1]], writes=[pk])
                S.op("pe", lambda e, ps=ps, blk=blk: e.matmul(ps[:], onehb[:, (4 + blk) * 128:(5 + blk) * 128], bintm[:, :],
                                                              start=False, stop=True),
                     reads=["onehb", "bintm"], writes=[pk])
                for hh in range(2):
                    h = blk * 2 + hh
                    S.op("act", lambda e, ps=ps, hh=hh, h=h: e.activation(
                        out=vz[:, h * 256:(h + 1) * 256], in_=ps[:, hh * 256:(hh + 1) * 256], func=AF.Copy,
                        scale=wtA[:, c_own, h:h + 1]), reads=[pk, "const"], writes=["vz"])
                yield

        run(g_pre(0))
        for c_own in range(16):
            gl = [g_main(c_own)]
            if c_own + 1 < 16:
                gl.append(g_pre(c_own + 1))
            interleave(*gl)
            if c_own < 15:
                kv_accum(c_own, c_own == 0, c_own == 14)
            else:
                Bst = stage32[:, 2048:4096].rearrange("p (h e) -> p h e", h=H)
                for b in range(4):
                    ps, pk = kvb[b]
                    S.op("dve", lambda e, ps=ps, b=b: e.tensor_copy(
                        out=Bst[:, 2 * b:2 * b + 2, :], in_=ps[:].rearrange("p (h e) -> p h e", h=2)),
                        reads=[pk], writes=["Bst"])
                kv_accum(c_own, True, True)
                Ast = pt[:].rearrange("p (h e) -> p h e", h=H)
                for h in range(H):
                    ps, pk = kvb[h // 2]
                    S.op("dve", lambda e, h=h, ps=ps: e.scalar_tensor_tensor(
                        out=Ast[:, h, :], in0=Bst[:, h, :], scalar=DEC128[h], in1=ps[:, (h % 2) * 256:(h % 2 + 1) * 256],
                        op0=ALU.mult, op1=ALU.add), reads=[pk, "Bst"], writes=["Ast"] + PT_ALL)
                S.dma("sp", "bnc", lambda e: e.dma_start(
                    out=bounce[0:1024, :].rearrange("(h d) e -> d h e", d=128), in_=Ast), reads=["Ast"], writes=["bounce"])
                S.dma("sp", "bnc", lambda e: e.dma_start(
                    out=bounce[1024:2048, :].rearrange("(h d) e -> d h e", d=128), in_=Bst), reads=["Bst"], writes=["bounce"])
        for _ in range(6):
            wload_next()
        st["f"] = 0
    S.barrier()
    if mode == "A":
        es.close()
        return nc
    if mode == "F":
        cc_sem = es.enter_context(nc.semaphore("cc_sem"))
        nc.gpsimd.collective_compute("AllGather", ALU.bypass, replica_groups=[list(range(NCORES))],
                                     ins=[bounce_t.ap().opt()], outs=[gath_t.ap().opt()]).then_inc(cc_sem)
        nc.gpsimd.wait_ge(cc_sem, 1)
    RF_ALL = [("Rf", h_) for h_ in range(H)]
    S.op("pool", lambda e: e.memset(Rf[:], 0.0), writes=RF_ALL)
    S.barrier()
    gbuf = [stage32[:, 0:2048], stage32[:, 2048:4096], yT[:].rearrange("p a n -> p (a n)"),
            rT[:].rearrange("p a n -> p (a n)").bitcast(F32)]
    gbuf = [g_.rearrange("p (h e) -> p h e", h=H) for g_ in gbuf]
    for j in range(NCORES):
        for part in range(2):
            bi = (2 * j + part) % 4
            gk = ("gst", bi)
            r0_ = j * 2048 + part * 1024
            S.dma("sp", "gld%d" % bi, lambda e, r0_=r0_, bi=bi: e.dma_start(
                out=gbuf[bi], in_=gath[r0_:r0_ + 1024, :].rearrange("(h d) e -> d h e", d=128)),
                reads=["gath"], writes=[gk])
            for h in range(H):
                sc = cf_t[:, j * 8 + h:j * 8 + h + 1] if part == 0 else cf_t[:, 64 + j:65 + j]
                S.op("dve", lambda e, bi=bi, h=h, sc=sc: e.scalar_tensor_tensor(
                    out=Rf[:, h, :], in0=gbuf[bi][:, h, :], scalar=sc, in1=Rf[:, h, :],
                    op0=ALU.mult, op1=ALU.add), reads=[gk, ("Rf", h), "const"], writes=[("Rf", h)])
    S.op("act", lambda e: e.activation(out=Rb[:], in_=Rf[:], func=AF.Copy), reads=RF_ALL, writes=["Rb"])
    S.barrier()

    for gi, g in enumerate(groups):
        C = len(g)
        nt = C * 128
        halo = (gi == 0)
        c0 = g[0]
        cur["xb"] = gi % 2

        def xload(gj):
            gg = groups[gj]
            xb = gj % 2
            S.dma("sp", "xld%d" % xb, lambda e: e.dma_start(
                out=xs2[:, xb, 0:len(gg), :],
                in_=xh[gg[0] * 128:(gg[0] + len(gg)) * 128, :].rearrange("(c p) d -> p c d", p=128)),
                writes=[("xs", xb, i) for i in range(CG)])
        if gi == 0:
            xload(0)
            interleave(*[g_ln1_and_hT(ci, ci * 128, 0, ci) for ci in range(C)])
        if gi + 1 < len(groups):
            xload(gi + 1)

        def s0_next():
            if gi + 1 >= len(groups):
                return []
            return [g_ln1_and_hT(ci, ci * 128, 0, ci, spar=2 + ci, xb=(gi + 1) % 2) for ci in range(len(groups[gi + 1]))]
        pend_tr = []
        for nb in TM_ORDER:
            wslot, wk = wacquire()
            for ci in range(C):
                chunk = g[ci]
                ps, pk = proj_tm(ci, ci * 128, wslot, wk, nb)
                if nb < 2:
                    rotary(ps, pk, chunk, "q", nb, ci)

                    def evq(pb, pk2, t0, ntl, nb=nb, ci=ci):
                        S.op("act", lambda e: e.activation(
                            out=qxT[:, nb * 4:nb * 4 + 4, ci * 128:(ci + 1) * 128],
                            in_=pb[:, 0:512].rearrange("p (k t) -> p k t", k=4), func=AF.Copy),
                            reads=[pk2], writes=["qxT"])
                    pend_tr.append(lambda ci=ci, evq=evq: transposes_to(rotb[:, ci, :], "rnbB%d" % ci, 4, evq))
                elif nb < 4:
                    blk = nb - 2
                    rotary(ps, pk, chunk, "k", blk, ci)
                    kzeta_from_ro(kze[:, ci, blk * 512:(blk + 1) * 512], ("kze", ci), blk, zeta_t, halo)

                    def evk(pb, pk2, t0, ntl, blk=blk, ci=ci):
                        S.op("act", lambda e: e.activation(
                            out=knT[:, blk * 4:blk * 4 + 4, ci * 128:(ci + 1) * 128],
                            in_=pb[:, 0:512].rearrange("p (k t) -> p k t", k=4), func=AF.Copy),
                            reads=[pk2], writes=["knT"])
                    pend_tr.append(lambda ci=ci, evk=evk: transposes_to(rotb[:, ci, :], "rnbB%d" % ci, 4, evk))
                else:
                    blk = nb - 4
                    S.op("act", lambda e, ps=ps, blk=blk, ci=ci: e.activation(
                        out=v_ap[:, ci, blk * 512:(blk + 1) * 512], in_=ps[:], func=AF.Copy),
                        reads=[pk], writes=[("v", ci, blk)])
            if nb >= 4:
                while pend_tr:
                    pend_tr.pop(0)()
            wload_next()
        while pend_tr:
            pend_tr.pop(0)()
        def fm_blocks(nbs):
            for nb in nbs:
                wslot, wk = wacquire()
                for et in range(4):
                    tile = (nb - 8) * 4 + et
                    ps, pk = bankf()
                    for k in range(8):
                        S.op("pe", lambda e, k=k, ps=ps, et=et, wslot=wslot: e.matmul(
                            ps[:, 0:nt], wslot[:, k, et * 128:(et + 1) * 128], hT[:, k, 0:nt],
                            start=(k == 0), stop=(k == 7)), reads=[("hT", 0, k), ("hT", 1, k), wk], writes=[pk])
                    bcol = binfm[:, tile:tile + 1]
                    if tile < 16:
                        S.op("act", lambda e, ps=ps, tile=tile, bcol=bcol: e.activation(
                            out=sgT[:, tile, 0:nt], in_=ps[:, 0:nt], func=AF.Silu, bias=bcol),
                            reads=[pk, "const"], writes=[("sgT", tile)])
                    elif tile < 24:
                        ct = tile - 16
                        S.op("act", lambda e, ps=ps, ct=ct, bcol=bcol: e.activation(
                            out=yT[:, ct, 0:nt], in_=ps[:, 0:nt], func=AF.Identity, bias=bcol),
                            reads=[pk, "const"], writes=[("yT", ct)])
                    elif tile < 32:
                        ct = tile - 24
                        S.op("act", lambda e, ps=ps, ct=ct, bcol=bcol: e.activation(
                            out=sq[:, ct % 2, 0:nt], in_=ps[:, 0:nt], func=AF.Sigmoid, bias=bcol),
                            reads=[pk, "const"], writes=[("sq", ct % 2)])
                        S.op("dve", lambda e, ct=ct: e.tensor_tensor(
                            out=aT[:, ct, 30:30 + nt], in0=yT[:, ct, 0:nt], in1=sq[:, ct % 2, 0:nt], op=ALU.mult),
                            reads=[("yT", ct), ("sq", ct % 2)], writes=[("aT", ct)])
                        if halo:
                            S.op("dve", lambda e, ct=ct: e.tensor_scalar(
                                out=aT[:, ct, 30:30 + nt], in0=aT[:, ct, 30:30 + nt], scalar1=hm, scalar2=None,
                                op0=ALU.mult), reads=[("aT", ct), "const"], writes=[("aT", ct)])
                    else:
                        dt_ = (tile - 32) % 8
                        S.op("act", lambda e, ps=ps, dt_=dt_, bcol=bcol: e.activation(
                            out=sig[:, dt_, 0:nt], in_=ps[:, 0:nt], func=AF.Sigmoid, bias=bcol),
                            reads=[pk, "const"], writes=[("sig", dt_)])
                wload_next()

        fm_blocks(range(8, 16))
        def g_conv():
            ps_st, pkst = psf[5], ("psf", 5)
            ps_s = ps_st[:, 0:nt]
            ps_q = ps_st[:, 256:256 + nt]
            pks = pkst
            pkq = pkst
            for ct in range(8):
                ps, pk = psf[3 + ct % 2], ("psf", 3 + ct % 2)
                wslot, wk = wacquire()
                dgv = wslot.rearrange("p k n -> p (k n)")[:, 0:CK * 128].rearrange("p (t m) -> p t m", t=CK)
                for k in range(CK):
                    S.op("pe", lambda e, k=k, ct=ct, ps=ps, dgv=dgv: e.matmul(
                        ps[:, 0:nt], dgv[:, k, :], aT[:, ct, k:k + nt], start=(k == 0), stop=(k == CK - 1)),
                        reads=[wk, ("aT", ct)], writes=[pk])
                    if k % 8 == 7:
                        yield
                wload_next()
                acc = yT[:, ct, 0:nt]
                S.op("act", lambda e, ct=ct, acc=acc, ps=ps: e.activation(
                    out=acc, in_=ps[:, 0:nt], func=AF.Identity, bias=cvec[:, ct:ct + 1]),
                    reads=[pk, "const"], writes=[("yT", ct)])
                S.op("act", lambda e, ct=ct, acc=acc: e.activation(out=sq[:, ct % 2, 0:nt], in_=acc, func=AF.Square),
                     reads=[("yT", ct)], writes=[("sq", ct % 2)])
                S.op("pe", lambda e, ct=ct, acc=acc: e.matmul(ps_s, ones32[:], acc, start=(ct == 0), stop=(ct == 7)),
                     reads=[("yT", ct), "ones32"], writes=[pks])
                S.op("pe", lambda e, ct=ct: e.matmul(ps_q, ones32[:], sq[:, ct % 2, 0:nt], start=(ct == 0), stop=(ct == 7)),
                     reads=[("sq", ct % 2), "ones32"], writes=[pkq])
                yield
            S.op("pool", lambda e: e.tensor_copy(out=aT[:, :, 0:30], in_=aT[:, :, nt:nt + 30]),
                 reads=[("aT", ct) for ct in range(8)], writes=[("aT", ct) for ct in range(8)])
            mean = lnst[:, 0, 0:nt]
            var = lnst[:, 1, 0:nt]
            rstd = lnst[:, 2, 0:nt]
            S.op("act", lambda e: e.activation(out=mean, in_=ps_s, func=AF.Copy, scale=1.0 / D),
                 reads=[pks], writes=["cmean"])
            S.op("dve", lambda e: e.tensor_tensor(out=var, in0=mean, in1=mean, op=ALU.mult), reads=["cmean"], writes=["cvar"])
            S.op("dve", lambda e: e.scalar_tensor_tensor(out=var, in0=ps_q, scalar=1.0 / D, in1=var,
                                                         op0=ALU.mult, op1=ALU.subtract), reads=[pkq, "cvar"], writes=["cvar"])
            S.op("dve", lambda e: e.tensor_scalar(out=var, in0=var, scalar1=EPS, scalar2=None, op0=ALU.add),
                 reads=["cvar"], writes=["cvar"])
            yield
            S.op("act", lambda e: e.activation(out=rstd, in_=var, func=AF.Sqrt), reads=["cvar"], writes=["crstd"])
            S.op("dve", lambda e: e.reciprocal(out=rstd, in_=rstd), reads=["crstd"], writes=["crstd"])
            yield
            for ct in range(8):
                acc = yT[:, ct, 0:nt]
                S.op("dve", lambda e, acc=acc: e.tensor_tensor(out=acc, in0=acc, in1=mean, op=ALU.subtract),
                     reads=[("yT", ct), "cmean"], writes=[("yT", ct)])
                S.op("pool", lambda e, acc=acc: e.tensor_tensor(out=acc, in0=acc, in1=rstd, op=ALU.mult),
                     reads=[("yT", ct), "crstd"], writes=[("yT", ct)])
                S.op("act", lambda e, acc=acc, ct=ct: e.activation(
                    out=a2T[:, ct, 0:nt], in_=acc, func=AF.Silu, bias=cvec[:, 16 + ct:17 + ct], scale=cvec[:, 8 + ct:9 + ct]),
                    reads=[("yT", ct), "const"], writes=[("a2T", ct)])
                if ct % 2 == 1:
                    yield

        PT_ALL = [("ptA", i) for i in range(4)] + ["ro_a", "ro_b", "ptC"]
        PTZ = [("ptA", i) for i in range(4)]
        RNB_ALL = ["rnbA", "rnbB0", "rnbB1"]
        UV_ALL = [("uval", i) for i in range(4)]
        def g_ret():
            for ci in range(C):
                cs = slice(ci * 128, (ci + 1) * 128)
                for half in range(2):
                    ps, pk = bankf()
                    for hh in range(4):
                        h = half * 4 + hh
                        S.op("pe", lambda e, ps=ps, hh=hh, h=h, cs=cs: e.matmul(
                            ps[:, hh * 128:(hh + 1) * 128], knT[:, h, cs], qxT[:, h, cs], start=True, stop=True),
                            reads=["knT", "qxT"], writes=[pk])
                    S.op("dve", lambda e, ps=ps, half=half: e.tensor_tensor(
                        out=sbf[:, half * 4:half * 4 + 4, :], in0=ps[:].rearrange("p (h i) -> p h i", h=4),
                        in1=caus[:].unsqueeze(1).to_broadcast([128, 4, 128]), op=ALU.mult),
                        reads=[pk, "const"], writes=[("sbf", half)])
                    yield
                r32 = pt[:].rearrange("p (h e) -> p h e", h=H)
                for pr in range(4):
                    ps, pk = bankf()
                    for hh in range(2):
                        h = pr * 2 + hh
                        S.op("pe", lambda e, ps=ps, hh=hh, h=h, ci=ci: e.matmul(
                            ps[:, hh * 256:(hh + 1) * 256], sbf[:, h, :], v_ap[:, ci, h * 256:(h + 1) * 256],
                            start=True, stop=False), reads=[("sbf", h // 4), ("v", ci, h // 2)], writes=[pk])
                        S.op("pe", lambda e, ps=ps, hh=hh, h=h, cs=cs: e.matmul(
                            ps[:, hh * 256:(hh + 1) * 256], qxT[:, h, cs], Rb[:, h, :],
                            start=False, stop=True), reads=["qxT", "Rb"], writes=[pk])
                    S.op("act", lambda e, ps=ps, pr=pr: e.activation(
                        out=pt[:, pr * 512:(pr + 1) * 512], in_=ps[:], func=AF.Copy), reads=[pk], writes=PT_ALL)
                    yield
                for pr in range(4):
                    ps, pk = bankf()
                    for hh in range(2):
                        h = pr * 2 + hh
                        S.op("pe", lambda e, ps=ps, hh=hh, h=h, ci=ci: e.matmul(
                            ps[:, hh * 256:(hh + 1) * 256], kze[:, ci, h * 128:(h + 1) * 128], v_ap[:, ci, h * 256:(h + 1) * 256],
                            start=True, stop=True), reads=[("kze", ci), ("v", ci, h // 2)], writes=[pk])
                    for hh in range(2):
                        h = pr * 2 + hh
                        S.op("dve", lambda e, ps=ps, hh=hh, h=h: e.scalar_tensor_tensor(
                            out=Rf[:, h, :], in0=Rf[:, h, :], scalar=DEC128[h], in1=ps[:, hh * 256:(hh + 1) * 256],
                            op0=ALU.mult, op1=ALU.add), reads=[pk, ("Rf", h)], writes=[("Rf", h)])
                    yield
                S.op("act", lambda e: e.activation(out=Rb[:], in_=Rf[:], func=AF.Copy), reads=RF_ALL, writes=["Rb"])
                gst6 = small[:, 32:80].rearrange("p (h s) -> p h s", h=H)
                gmv = small[:, 80:96].rearrange("p (h s) -> p h s", h=H)
                gr = small[:, 96:104]
                gnm = small[:, 104:112]
                GMV_ALL = [("gmv", h_) for h_ in range(H)]
                for h in range(H):
                    S.op("dve", lambda e, h=h: e.bn_stats(out=gst6[:, h, :], in_=r32[:, h, :]), reads=PT_ALL, writes=[("gst6", h)])
                    if h % 4 == 3:
                        yield
                for h in range(H):
                    S.op("dve", lambda e, h=h: e.bn_aggr(out=gmv[:, h, :], in_=gst6[:, h, :]), reads=[("gst6", h)],
                         writes=[("gmv", h)])
                    if h % 4 == 3:
                        yield
                S.op("dve", lambda e: e.tensor_scalar(out=gr, in0=gmv[:, :, 1], scalar1=EPS, scalar2=None, op0=ALU.add),
                     reads=GMV_ALL, writes=["gr"])
                S.op("act", lambda e: e.activation(out=gr, in_=gr, func=AF.Sqrt), reads=["gr"], writes=["gr"])
                S.op("dve", lambda e: e.reciprocal(out=gr, in_=gr), reads=["gr"], writes=["gr"])
                S.op("dve", lambda e: e.tensor_tensor(out=r32, in0=r32, in1=gmv[:, :, 0:1].to_broadcast([128, H, DV]),
                                                      op=ALU.subtract), reads=PT_ALL + GMV_ALL, writes=PT_ALL)
                S.op("pool", lambda e: e.tensor_tensor(out=rnb[:].rearrange("p (h e) -> p h e", h=H), in0=r32,
                                                       in1=gr.unsqueeze(2).to_broadcast([128, H, DV]), op=ALU.mult),
                     reads=PT_ALL + ["gr"], writes=RNB_ALL)
                yield

                def evr(pb, pk2, t0, ntl, ci=ci):
                    tmp = uvf[:, 0:1024].rearrange("p (k t) -> p k t", k=8)
                    S.op("dve", lambda e: e.tensor_tensor(
                        out=tmp, in0=pb[:].rearrange("p (k t) -> p k t", k=8),
                        in1=gnfm[:, t0:t0 + 8].unsqueeze(2).to_broadcast([128, 8, 128]), op=ALU.mult),
                        reads=[pk2, "const"], writes=UV_ALL)
                    S.op("pool", lambda e: e.tensor_tensor(
                        out=tmp, in0=tmp, in1=gnfm[:, 16 + t0:24 + t0].unsqueeze(2).to_broadcast([128, 8, 128]), op=ALU.add),
                        reads=UV_ALL + ["const"], writes=UV_ALL)
                    S.op("pool", lambda e: e.tensor_tensor(
                        out=rT[:, t0:t0 + 8, ci * 128:(ci + 1) * 128], in0=tmp, in1=sgT[:, t0:t0 + 8, ci * 128:(ci + 1) * 128],
                        op=ALU.mult), reads=UV_ALL + [("sgT", t_) for t_ in range(t0, t0 + 8)], writes=["rT"])
                transposes_to(rnb, RNB_ALL, 16, evr)
                yield

        st["fset"] = [0, 1, 2]
        interleave(g_conv(), g_ret())
        st["fset"] = [0, 1, 2, 3, 4, 5]
        fm_blocks(range(16, 18))
        for nb in range(2):
            wslot, wk = wacquire()
            for et in range(4):
                dt_ = nb * 4 + et
                ps, pk = bankf()
                for k in range(8):
                    S.op("pe", lambda e, k=k, ps=ps, et=et, wslot=wslot: e.matmul(
                        ps[:, 0:nt], wslot[:, k, et * 128:(et + 1) * 128], a2T[:, k, 0:nt],
                        start=(k == 0), stop=(k == 7)), reads=[("a2T", k), wk], writes=[pk])
                S.op("dve", lambda e, ps=ps, dt_=dt_: e.tensor_tensor(
                    out=yT[:, dt_, 0:nt], in0=ps[:, 0:nt], in1=sig[:, dt_, 0:nt], op=ALU.mult),
                    reads=[pk, ("sig", dt_)], writes=[("yT", dt_)])
            wload_next()
        fm_blocks(range(18, 20))
        for nb in range(2):
            wa, wka = wacquire()
            wb_, wkb = wacquire()
            for et in range(4):
                dt_ = nb * 4 + et
                ps, pk = bankf()
                for k in range(16):
                    ws = wa if k < 8 else wb_
                    S.op("pe", lambda e, k=k, ps=ps, et=et, ws=ws: e.matmul(
                        ps[:, 0:nt], ws[:, k % 8, et * 128:(et + 1) * 128], rT[:, k, 0:nt],
                        start=(k == 0), stop=(k == 15)), reads=["rT", wka, wkb], writes=[pk])
                tmp = uacc[:, et % 2, 0:nt]
                S.op("dve", lambda e, ps=ps, dt_=dt_, tmp=tmp: e.tensor_tensor(
                    out=tmp, in0=ps[:, 0:nt], in1=sig[:, dt_, 0:nt], op=ALU.mult),
                    reads=[pk, ("sig", dt_)], writes=[("uacc", et % 2)])
                S.op("pool", lambda e, dt_=dt_, tmp=tmp: e.tensor_tensor(
                    out=a2T[:, dt_, 0:nt], in0=tmp, in1=yT[:, dt_, 0:nt], op=ALU.add),
                    reads=[("uacc", et % 2), ("yT", dt_)], writes=[("a2T", dt_)])
            wload_next()
            wload_next()
        wo = [wacquire() for _ in range(2)]
        ZK = [PTZ, ["ro_a", "ro_b", "ptC"]]
        for ci in range(C):
            z = pt[:, ci * 1024:(ci + 1) * 1024]
            for nb in range(2):
                ps, pk = bankf()
                for k in range(8):
                    S.op("pe", lambda e, k=k, ps=ps, nb=nb, ci=ci: e.matmul(
                        ps[:], a2T[:, k, ci * 128:(ci + 1) * 128], wo[nb][0][:, k, :], start=(k == 0), stop=(k == 7)),
                        reads=[("a2T", k), wo[nb][1]], writes=[pk])
                S.op("dve", lambda e, ps=ps, nb=nb, z=z: e.tensor_tensor(
                    out=z[:, nb * 512:(nb + 1) * 512], in0=ps[:], in1=gate_bc[:, 0, nb * 512:(nb + 1) * 512], op=ALU.mult),
                    reads=[pk, "gate_bc"], writes=ZK[ci])
        interleave(*[g_post_ln(pt[:, ci * 1024:(ci + 1) * 1024], ZK[ci], ci, ci, 0, 1) for ci in range(C)])
        wload_next()
        wload_next()
        interleave(*[g_ln1_and_hT(ci, ci * 128, 16, ci) for ci in range(C)])
        if halo:
            S.op("act", lambda e: e.activation(out=hTh[:], in_=hT[:, :, 126:128], func=AF.Copy),
                 reads=[("hT", 0, k_) for k_ in range(8)], writes=["hTh"])
            interleave(*s0_next())
            continue
        first_own = (gi == 1)
        GT_ALL = ["gT"] + [("v", c_, b_) for c_ in range(2) for b_ in range(4)] + [("sgT", t_) for t_ in range(16)]
        gbufs = [(uacc[:, 0, 0:nt], ("uacc", 0)), (uacc[:, 1, 0:nt], ("uacc", 1)),
                 (sq[:, 0, 0:nt], ("sq", 0)), (sq[:, 1, 0:nt], ("sq", 1))]
        pending = []

        def flush_tail():
            while pending:
                pending.pop(0)()

        for p in range(6):
            ntile = 4 if p < 5 else 2
            for part in range(2):
                wslot, wk = wacquire()
                for et in range(ntile):
                    ft = p * 4 + et + part * 22
                    ps, pk = bankf()
                    for k in range(8):
                        S.op("pe", lambda e, k=k, ps=ps, et=et, wslot=wslot: e.matmul(
                            ps[:, 0:nt], wslot[:, k, et * 128:(et + 1) * 128], hT[:, k, 0:nt],
                            start=(k == 0), stop=(k == 7)), reads=[("hT", 0, k), ("hT", 1, k), wk], writes=[pk])
                    ub4 = st["ur"] % 4
                    st["ur"] += 1
                    ur = uraw[:, ub4, :]
                    urb = ("urawb", ub4)
                    urh = ("urawh", ub4)
                    if halo:
                        S.op("act", lambda e, ps=ps, ur=ur: e.activation(out=ur[:, 2:2 + nt], in_=ps[:, 0:nt], func=AF.Copy),
                             reads=[pk], writes=[urb])
                        S.op("pool", lambda e, ur=ur, ft=ft: e.tensor_scalar(
                            out=uhist[:, ft, :], in0=ur[:, nt:nt + 2], scalar1=hm, scalar2=0.0, op0=ALU.mult, op1=ALU.add),
                            reads=[urb, "const"], writes=[("uhist", ft)])
                        continue
                    if first_own:
                        for k in range(8):
                            S.op("pe", lambda e, k=k, ps=ps, et=et, wslot=wslot: e.matmul(
                                ps[:, nt:nt + 2], wslot[:, k, et * 128:(et + 1) * 128], hTh[:, k, :],
                                start=(k == 0), stop=(k == 7)), reads=["hTh", wk], writes=[pk])
                        S.op("act", lambda e, ps=ps, ur=ur: e.activation(out=ur[:, 0:2], in_=ps[:, nt:nt + 2], func=AF.Copy, scale=hm),
                             reads=[pk, "const"], writes=[urh])
                    else:
                        S.op("pool", lambda e, ur=ur, ft=ft: e.tensor_copy(out=ur[:, 0:2], in_=uhist[:, ft, :]),
                             reads=[("uhist", ft)], writes=[urh])
                    S.op("act", lambda e, ps=ps, ur=ur: e.activation(out=ur[:, 2:2 + nt], in_=ps[:, 0:nt], func=AF.Copy),
                         reads=[pk], writes=[urb])
                    if part == 0:
                        acc, acck = uval[:, et, 0:nt], ("uval", et)
                    else:
                        acc, acck = gbufs[st["gb"] % 4]
                        st["gb"] += 1
                    fw = fwfm[:, ft * 3:ft * 3 + 3]
                    S.op("act", lambda e, ps=ps, acc=acc, fw=fw, ft=ft: e.activation(
                        out=acc, in_=ps[:, 0:nt], func=AF.Identity, bias=fbfm[:, ft:ft + 1], scale=fw[:, 2:3]),
                        reads=[pk, "const"], writes=[acck])
                    flush_tail()
                    S.op("dve", lambda e, ur=ur, acc=acc, fw=fw: e.scalar_tensor_tensor(
                        out=acc, in0=ur[:, 1:1 + nt], scalar=fw[:, 1:2], in1=acc, op0=ALU.mult, op1=ALU.add),
                        reads=[urb, urh, acck], writes=[acck])
                    S.op("dve", lambda e, ur=ur, acc=acc, fw=fw: e.scalar_tensor_tensor(
                        out=acc, in0=ur[:, 0:nt], scalar=fw[:, 0:1], in1=acc, op0=ALU.mult, op1=ALU.add),
                        reads=[urb, urh, acck], writes=[acck])
                    S.op("pool", lambda e, ur=ur, ft=ft: e.tensor_copy(out=uhist[:, ft, :], in_=ur[:, nt:nt + 2]),
                         reads=[urb], writes=[("uhist", ft)])
                    if part == 1:
                        def tail(acc=acc, acck=acck, et=et, ft=ft):
                            S.op("act", lambda e: e.activation(out=acc, in_=acc, func=AF.Silu),
                                 reads=[acck], writes=[acck])
                            S.op("pool", lambda e: e.tensor_tensor(
                                out=gT[:, ft - 22, 0:nt], in0=uval[:, et, 0:nt], in1=acc, op=ALU.mult),
                                reads=[acck, ("uval", et)], writes=GT_ALL)
                        pending.append(tail)
                wload_next()
        flush_tail()
        if halo:
            interleave(*s0_next())
            continue
        for nb in range(2):
            wd = [wacquire() for _ in range(3)]
            for ci in range(C):
                z = pt[:, ci * 1024:(ci + 1) * 1024]
                ps, pk = bankf()
                for k in range(22):
                    ws, wkk = wd[k // 8]
                    S.op("pe", lambda e, k=k, ps=ps, ws=ws, ci=ci: e.matmul(
                        ps[:], gT[:, k, ci * 128:(ci + 1) * 128], ws[:, k % 8, :], start=(k == 0), stop=(k == 21)),
                        reads=GT_ALL + [wkk], writes=[pk])
                S.op("dve", lambda e, ps=ps, nb=nb, z=z: e.tensor_tensor(
                    out=z[:, nb * 512:(nb + 1) * 512], in0=ps[:], in1=gate_bc[:, 1, nb * 512:(nb + 1) * 512], op=ALU.mult),
                    reads=[pk, "gate_bc"], writes=ZK[ci])
            for _ in range(3):
                wload_next()
        interleave(*([g_post_ln(pt[:, ci * 1024:(ci + 1) * 1024], ZK[ci], ci, ci, 2, 3) for ci in range(C)] + s0_next()))
        r0 = (c0 - 1) * 128
        S.dma("sp", "yst%d" % (gi % 2), lambda e, r0=r0, C=C, gi=gi: e.dma_start(
            out=y[r0:r0 + C * 128, :].rearrange("(c p) d -> p c d", p=128), in_=xs2[:, gi % 2, 0:C, :]),
            reads=[("xs", gi % 2, i) for i in range(CG)], writes=["yout"])
    S.barrier()

    es.close()
    return nc


def _bf16_safe(a):
    return np.ascontiguousarray(a, dtype=np.float32)


def kernel(x, c, positions, w_ada, b_ada, w_in, b_in, conv_dw_w, conv_dw_b, conv_ln_g, conv_ln_b,
           w_conv_out, ret_gn_g, ret_gn_b, w_ret_out, w_out, ln1_g, ln1_b, w_up, ffn_dw_w, ffn_dw_b,
           w_down, ln2_g, ln2_b):
    f = _bf16_safe
    x2 = f(x)[0]
    pos = np.asarray(positions)[0].astype(np.int32)

    def fm(v, ntile):
        return np.ascontiguousarray(np.asarray(v, np.float32).reshape(ntile, 128).T)

    lg = np.array(LG, np.float64)
    p = np.arange(128, dtype=np.float64)[:, None]
    s = 128.0 ** -0.5
    xi = np.exp(lg[None, :] * (p + 1.0))
    zeta = s * np.exp(lg[None, :] * (127.0 - p))
    zneg = s * np.exp(-lg[None, :] * (p + 1.0))
    wtA = np.zeros((128, 16, 8))
    for cc in range(16):
        if cc < 15:
            wtA[:, cc, :] = s * np.exp(lg[None, :] * (1919.0 - (128.0 * cc + p)))
        else:
            wtA[:, cc, :] = s * np.exp(lg[None, :] * (127.0 - p))
    tabs = np.concatenate([xi, zeta, zneg, wtA.reshape(128, 128)], axis=1).astype(np.float32)
    caus = (np.arange(128)[None, :] >= np.arange(128)[:, None]).astype(np.float32)
    idn = np.eye(128, dtype=np.float32)
    half = 64
    invf1 = (np.float32(10000.0) ** (-np.arange(half, dtype=np.float32) / np.float32(half))).astype(np.float32)
    invf = np.ascontiguousarray(np.broadcast_to(invf1[None, :], (128, 64))).astype(np.float32)
    oneh = np.zeros((12, 8 * 128), np.float32)
    for r in range(8):
        oneh[r, r * 128:(r + 1) * 128] = 1.0

    b_in1 = np.asarray(b_in, np.float32)[0]
    shared = {
        "tabs": tabs, "caus": caus, "idn": idn, "invf": invf, "oneh": oneh,
        "c_fm": fm(np.asarray(c)[0], 8),
        "w_ada": f(w_ada)[0], "bada_fm": fm(np.asarray(b_ada)[0], 48), "bada_row": f(b_ada),
        "w_in": f(w_in)[0],
        "bin_tm": np.ascontiguousarray(b_in1[0:6144].reshape(12, 512)),
        "bin_fm": fm(np.concatenate([b_in1[4096:6144], b_in1[6144:10240]]), 48),
        "cw_fm": np.ascontiguousarray(np.asarray(conv_dw_w, np.float32)[0].T.reshape(8, 128, CK).transpose(1, 0, 2).reshape(128, 8 * CK)),
        "cvec_fm": np.concatenate([fm(np.asarray(conv_dw_b)[0], 8), fm(np.asarray(conv_ln_g)[0], 8),
                                   fm(np.asarray(conv_ln_b)[0], 8)], axis=1),
        "w_conv_out": f(w_conv_out)[0],
        "gn_fm": np.concatenate([fm(np.asarray(ret_gn_g)[0], 16), fm(np.asarray(ret_gn_b)[0], 16)], axis=1),
        "w_ret_out": f(w_ret_out)[0], "w_out": f(w_out)[0],
        "ln_rows": np.stack([np.asarray(a, np.float32)[0] for a in (ln1_g, ln1_b, ln2_g, ln2_b)]),
        "w_up": f(w_up)[0],
        "fw_fm": np.ascontiguousarray(np.asarray(ffn_dw_w, np.float32)[0].T.reshape(NFT, 128, 3).transpose(1, 0, 2).reshape(128, NFT * 3)),
        "fb_fm": fm(np.asarray(ffn_dw_b)[0], NFT),
        "w_down": f(w_down)[0],
    }
    shared = {k: np.ascontiguousarray(v, dtype=np.float32) for k, v in shared.items()}
    in_maps = []
    for i in range(NCORES):
        start = i * TOK
        xh = np.zeros((NCH * 128, D), np.float32)
        ph = np.zeros((NCH * 128,), np.int32)
        xh[128:] = x2[start:start + TOK]
        ph[128:] = pos[start:start + TOK]
        if i > 0:
            xh[:128] = x2[start - 128:start]
            ph[:128] = pos[start - 128:start]
        meta = np.zeros((128, 2), np.float32)
        meta[:, 0] = i
        meta[:, 1] = 1.0 if i > 0 else 0.0
        cf = np.zeros((8, 8))
        cb = np.zeros((8,))
        for j in range(NCORES):
            if j <= i - 2:
                cf[j, :] = np.exp(lg * (1920.0 + 2048.0 * (i - 2 - j)))
            if j == i - 1:
                cb[j] = 1.0
        cft = np.concatenate([cf.reshape(-1), cb]).astype(np.float32)
        m = dict(shared)
        m["xh"] = xh
        m["pos_t"] = np.ascontiguousarray(ph.reshape(NCH, 128).T)
        m["meta"] = meta
        m["cf"] = np.ascontiguousarray(np.broadcast_to(cft[None, :], (128, 72))).astype(np.float32)
        in_maps.append(m)
    if FUSED:
        nc = build("F")
        res = run_bass_kernel_spmd(nc, in_maps, core_ids=list(range(NCORES)))
    else:
        nca = build("A")
        resa = run_bass_kernel_spmd(nca, in_maps, core_ids=list(range(NCORES)))
        gath = np.ascontiguousarray(np.concatenate([r["ab"] for r in resa.results], axis=0), dtype=np.float32)
        for m in in_maps:
            m["gath"] = gath
        nc = build("B")
        res = run_bass_kernel_spmd(nc, in_maps, core_ids=list(range(NCORES)))
    out = np.concatenate([r["y"] for r in res.results], axis=0)
    return out.reshape(1, SEQ, D).astype(np.float32)
```
